# Optimizing a Trainium2 kernel written in Bass

```python
import jax, jax.numpy as jnp
from jax import lax
import numpy as np

D_MODEL = 1024
BATCH = 32
SEQ = 256
DEPTH = 1
DEC_BATCH = 2
DEC_SEQ = 4096
PAST_LEN = 256

GRID_W = 64
D_RNN = 1024
RG_BLOCKS = 16
RG_BW = D_RNN // RG_BLOCKS
CONV_W = 4
RG_C = 8.0
ML_HEADS = 4
ML_DIM = 1024
ML_HD = ML_DIM // ML_HEADS
CHUNK = 64
D_FF = 4 * D_MODEL
ALPHA = (2 * DEPTH) ** 0.25
BETA = (8 * DEPTH) ** -0.25
LN_EPS = 1e-5
IN_SIZES = [D_RNN, D_RNN, ML_DIM, ML_DIM, ML_DIM, ML_DIM, 2 * D_MODEL, 2 * ML_HEADS, 2 * ML_HEADS]
D_IN = sum(IN_SIZES)

kernel_name = "bidir_rglru_mlstm_prefix_diffusion_step"


def layer_norm(x, g=None, b=None):
    xf = x.astype(jnp.float32)
    mu = jnp.mean(xf, -1, keepdims=True)
    var = jnp.mean(jnp.square(xf - mu), -1, keepdims=True)
    y = (xf - mu) * lax.rsqrt(var + LN_EPS)
    if g is not None:
        y = y * g.astype(jnp.float32) + b.astype(jnp.float32)
    return y.astype(x.dtype)


def centred_dwconv(x, w, b):
    left = CONV_W // 2
    right = CONV_W - 1 - left
    t = x.shape[-2]
    pad = [(0, 0)] * (x.ndim - 2) + [(left, right), (0, 0)]
    xp = jnp.pad(x, pad)
    return sum(w[k] * xp[..., k:k + t, :] for k in range(CONV_W)) + b


def block_diag_linear(x, w, b):
    xb = x.reshape(x.shape[:-1] + (RG_BLOCKS, RG_BW))
    y = jnp.einsum('btnk,nkj->btnj', xb, w)
    return y.reshape(x.shape) + b


def _linear_combine(left, right):
    a1, b1 = left
    a2, b2 = right
    return a1 * a2, a2 * b1 + b2


def rglru_scan(xc, w_a, b_a, w_x, b_x, lam, h0):
    f32 = jnp.float32
    r = jax.nn.sigmoid(block_diag_linear(xc, w_a, b_a).astype(f32))
    i = jax.nn.sigmoid(block_diag_linear(xc, w_x, b_x).astype(f32))
    log_a = RG_C * r * jax.nn.log_sigmoid(lam.astype(f32))
    a = jnp.exp(log_a)
    gain = jnp.sqrt(jnp.clip(-jnp.expm1(2.0 * log_a), 0.0, 1.0))
    bterm = gain * i * xc.astype(f32)
    bterm = bterm.at[:, 0].add(a[:, 0] * h0.astype(f32))
    _, h = lax.associative_scan(_linear_combine, (a, bterm), axis=1)
    return h, h[:, -1]


def mlstm_chunkwise(q, k, v, ig, lf, c0, n0, m0):
    bsz, nh, t, hd = q.shape
    nc = t // CHUNK

    def to_chunks(z):
        return jnp.moveaxis(z.reshape(z.shape[:2] + (nc, CHUNK) + z.shape[3:]), 2, 0)

    causal = jnp.tril(jnp.ones((CHUNK, CHUNK), dtype=bool))

    def step(carry, xs):
        cm, nv, m = carry
        qc, kc, vc, ic, fc = xs
        bcum = jnp.cumsum(fc, axis=-1)
        dmat = bcum[..., :, None] - bcum[..., None, :] + ic[..., None, :]
        dmat = jnp.where(causal, dmat, -jnp.inf)
        inter = bcum + m[..., None]
        m_row = jnp.maximum(jnp.max(dmat, -1), inter)
        wts = jnp.exp(dmat - m_row[..., None])
        w_inter = jnp.exp(inter - m_row)
        s = jnp.einsum('bhid,bhjd->bhij', qc, kc) * wts
        num = jnp.einsum('bhij,bhjd->bhid', s, vc) + w_inter[..., None] * jnp.einsum('bhid,bhde->bhie', qc, cm)
        den = jnp.sum(s, -1) + w_inter * jnp.einsum('bhid,bhd->bhi', qc, nv)
        h = num / jnp.maximum(jnp.abs(den), jnp.exp(-m_row))[..., None]
        btot = bcum[..., -1]
        wk_log = btot[..., None] - bcum + ic
        m_new = jnp.maximum(btot + m, jnp.max(wk_log, -1))
        wk = jnp.exp(wk_log - m_new[..., None])
        decay = jnp.exp(btot + m - m_new)
        c_new = decay[..., None, None] * cm + jnp.einsum('bhj,bhjd,bhje->bhde', wk, kc, vc)
        n_new = decay[..., None] * nv + jnp.einsum('bhj,bhjd->bhd', wk, kc)
        return (c_new, n_new, m_new), h

    (cf, nf, mf), hs = lax.scan(step, (c0, n0, m0), (to_chunks(q), to_chunks(k), to_chunks(v), to_chunks(ig), to_chunks(lf)))
    h = jnp.moveaxis(hs, 0, 2).reshape(bsz, nh, t, hd)
    return h, cf, nf, mf


def mixer(u, lw, h0, c0, n0, m0, rows):
    f32 = jnp.float32
    bsz, t, _ = u.shape
    proj = u @ lw['w_in']
    xr, zr, q, k, v, o, gm, ig, fg = jnp.split(proj, np.cumsum(IN_SIZES)[:-1].tolist(), axis=-1)

    if rows is None:
        xc = centred_dwconv(xr, lw['conv_w'], lw['conv_b'])
    else:
        xc = centred_dwconv(xr.reshape(bsz, rows, GRID_W, D_RNN), lw['conv_w'], lw['conv_b']).reshape(bsz, t, D_RNN)
    hf, hf_fin = rglru_scan(xc, lw['wa'][0], lw['ba'][0], lw['wx'][0], lw['bx'][0], lw['lam'][0], h0[:, 0])
    hb, hb_fin = rglru_scan(jnp.flip(xc, 1), lw['wa'][1], lw['ba'][1], lw['wx'][1], lw['bx'][1], lw['lam'][1], h0[:, 1])
    h_rg = (hf + jnp.flip(hb, 1)).astype(u.dtype) * jax.nn.gelu(zr)
    y_rg = h_rg @ lw['w_rg_proj']

    def heads(z):
        return z.reshape(bsz, t, ML_HEADS, ML_HD).transpose(0, 2, 1, 3).astype(f32)
    qh = heads(q) * (ML_HD ** -0.5)
    kh = heads(k)
    vh = heads(v)
    igp = (ig + lw['b_ig'].reshape(-1)).astype(f32).reshape(bsz, t, 2, ML_HEADS).transpose(2, 0, 3, 1)
    lfp = jax.nn.log_sigmoid((fg + lw['b_fg'].reshape(-1)).astype(f32)).reshape(bsz, t, 2, ML_HEADS).transpose(2, 0, 3, 1)
    f32s = lambda z: z.astype(f32)
    h_f, cf, nf, mf = mlstm_chunkwise(qh, kh, vh, igp[0], lfp[0], f32s(c0[:, 0]), f32s(n0[:, 0]), f32s(m0[:, 0]))
    h_b, cb, nb, mb = mlstm_chunkwise(jnp.flip(qh, 2), jnp.flip(kh, 2), jnp.flip(vh, 2),
                                      jnp.flip(igp[1], -1), jnp.flip(lfp[1], -1),
                                      f32s(c0[:, 1]), f32s(n0[:, 1]), f32s(m0[:, 1]))
    hsum = (h_f + jnp.flip(h_b, 2)).transpose(0, 2, 1, 3)
    h_ml = layer_norm(hsum).reshape(bsz, t, ML_DIM).astype(u.dtype) * lw['gn_g']
    y_ml = (jax.nn.sigmoid(o) * h_ml) @ lw['w_ml_proj']

    g_rg, g_ml = jnp.split(jax.nn.sigmoid(gm + lw['b_merge']), 2, axis=-1)
    out = (g_rg * y_rg + g_ml * y_ml) @ lw['w_out']
    h_fin = jnp.stack([hf_fin, hb_fin], axis=1)
    c_fin = jnp.stack([cf, cb], axis=1)
    n_fin = jnp.stack([nf, nb], axis=1)
    m_fin = jnp.stack([mf, mb], axis=1)
    return out, h_fin, c_fin, n_fin, m_fin


def trunk_layer(x, mod, lw, h0, c0, n0, m0, rows):
    shift1, scale1, gate1, shift2, scale2, gate2 = jnp.split(mod, 6, axis=-1)
    u = layer_norm(x) * (1.0 + scale1) + shift1
    mix, h_fin, c_fin, n_fin, m_fin = mixer(u, lw, h0, c0, n0, m0, rows)
    x = layer_norm(ALPHA * x + gate1 * mix, lw['ln_g'][0], lw['ln_b'][0])
    u = layer_norm(x) * (1.0 + scale2) + shift2
    hid = jnp.square(jax.nn.relu(u @ lw['w_fc'] + lw['b_fc']))
    x = layer_norm(ALPHA * x + gate2 * (hid @ lw['w_proj'] + lw['b_proj']), lw['ln_g'][1], lw['ln_b'][1])
    return x, h_fin, c_fin, n_fin, m_fin


def setup_inputs(seed: int = 0) -> dict:
    key = jax.random.key(seed)
    ks = iter(jax.random.split(key, 40))
    f32 = jnp.float32

    def nrm(shape, scale):
        return jax.random.normal(next(ks), shape, f32) * scale

    d = {}
    d['x_prompt'] = nrm((BATCH, SEQ, D_MODEL), 1.0)
    d['x_sample'] = nrm((DEC_BATCH, DEC_SEQ, D_MODEL), 1.0)
    d['state_rglru_h'] = nrm((DEC_BATCH, DEPTH, 2, D_RNN), 0.5)
    d['state_mlstm_C'] = nrm((DEC_BATCH, DEPTH, 2, ML_HEADS, ML_HD, ML_HD), 1.0)
    d['state_mlstm_n'] = nrm((DEC_BATCH, DEPTH, 2, ML_HEADS, ML_HD), 1.0)
    d['state_mlstm_m'] = nrm((DEC_BATCH, DEPTH, 2, ML_HEADS), 0.5)
    d['c'] = nrm((DEC_BATCH, D_MODEL), 1.0)
    d['c_ctx'] = nrm((D_MODEL,), 1.0)
    d['w_ada'] = nrm((DEPTH, D_MODEL, 6 * D_MODEL), 0.5 * D_MODEL ** -0.5)
    d['b_ada'] = nrm((DEPTH, 6 * D_MODEL), 0.02)
    col_scale = jnp.concatenate([jnp.ones((D_IN - 4 * ML_HEADS,), f32), jnp.full((4 * ML_HEADS,), 0.1, f32)])
    d['w_in'] = nrm((DEPTH, D_MODEL, D_IN), D_MODEL ** -0.5) * col_scale
    d['rg_conv_w'] = nrm((DEPTH, CONV_W, D_RNN), CONV_W ** -0.5)
    d['rg_conv_b'] = nrm((DEPTH, D_RNN), 0.02)
    d['rg_wa'] = nrm((DEPTH, 2, RG_BLOCKS, RG_BW, RG_BW), RG_BW ** -0.5)
    d['rg_ba'] = nrm((DEPTH, 2, D_RNN), 0.02)
    d['rg_wx'] = nrm((DEPTH, 2, RG_BLOCKS, RG_BW, RG_BW), RG_BW ** -0.5)
    d['rg_bx'] = nrm((DEPTH, 2, D_RNN), 0.02)
    a_pow = jax.random.uniform(next(ks), (DEPTH, 2, D_RNN), f32, 0.9, 0.999)
    a_base = a_pow ** (1.0 / RG_C)
    d['rg_lambda'] = jnp.log(a_base) - jnp.log1p(-a_base)
    d['w_rg_proj'] = nrm((DEPTH, D_RNN, D_MODEL), BETA * D_RNN ** -0.5)
    d['ml_b_igate'] = nrm((DEPTH, 2, ML_HEADS), 0.1)
    d['ml_b_fgate'] = jnp.broadcast_to(jnp.linspace(3.0, 6.0, ML_HEADS, dtype=f32), (DEPTH, 2, ML_HEADS)) + nrm((DEPTH, 2, ML_HEADS), 0.1)
    d['ml_gn_g'] = 1.0 + nrm((DEPTH, ML_DIM), 0.02)
    d['w_ml_proj'] = nrm((DEPTH, ML_DIM, D_MODEL), BETA * ML_DIM ** -0.5)
    d['b_merge'] = nrm((DEPTH, 2 * D_MODEL), 0.02)
    d['w_out'] = nrm((DEPTH, D_MODEL, D_MODEL), BETA * D_MODEL ** -0.5)
    d['ln_g'] = 1.0 + nrm((DEPTH, 2, D_MODEL), 0.02)
    d['ln_b'] = nrm((DEPTH, 2, D_MODEL), 0.02)
    d['w_fc'] = nrm((DEPTH, D_MODEL, D_FF), D_MODEL ** -0.5)
    d['b_fc'] = nrm((DEPTH, D_FF), 0.02)
    d['w_proj'] = nrm((DEPTH, D_FF, D_MODEL), BETA * D_FF ** -0.5)
    d['b_proj'] = nrm((DEPTH, D_MODEL), 0.02)
    return d


def reference(x_prompt, x_sample, state_rglru_h, state_mlstm_C, state_mlstm_n, state_mlstm_m, c, c_ctx,
              w_ada, b_ada, w_in, rg_conv_w, rg_conv_b, rg_wa, rg_ba, rg_wx, rg_bx, rg_lambda, w_rg_proj,
              ml_b_igate, ml_b_fgate, ml_gn_g, w_ml_proj, b_merge, w_out, ln_g, ln_b, w_fc, b_fc, w_proj, b_proj):
    f32 = jnp.float32
    bp = x_prompt.shape[0]
    rows = x_sample.shape[1] // GRID_W
    zero_h = jnp.zeros((bp, 2, D_RNN), f32)
    zero_c = jnp.zeros((bp, 2, ML_HEADS, ML_HD, ML_HD), f32)
    zero_n = jnp.zeros((bp, 2, ML_HEADS, ML_HD), f32)
    zero_m = jnp.zeros((bp, 2, ML_HEADS), f32)
    xp = x_prompt
    xs = x_sample
    new_h, new_c, new_n, new_m = [], [], [], []
    for l in range(DEPTH):
        lw = {'w_in': w_in[l], 'conv_w': rg_conv_w[l], 'conv_b': rg_conv_b[l], 'wa': rg_wa[l], 'ba': rg_ba[l],
              'wx': rg_wx[l], 'bx': rg_bx[l], 'lam': rg_lambda[l], 'w_rg_proj': w_rg_proj[l],
              'b_ig': ml_b_igate[l], 'b_fg': ml_b_fgate[l], 'gn_g': ml_gn_g[l], 'w_ml_proj': w_ml_proj[l],
              'b_merge': b_merge[l], 'w_out': w_out[l], 'ln_g': ln_g[l], 'ln_b': ln_b[l],
              'w_fc': w_fc[l], 'b_fc': b_fc[l], 'w_proj': w_proj[l], 'b_proj': b_proj[l]}
        mod_ctx = (jax.nn.silu(c_ctx) @ w_ada[l] + b_ada[l])[None, None, :]
        mod_lat = (jax.nn.silu(c) @ w_ada[l] + b_ada[l])[:, None, :]
        xp, h_fin, c_fin, n_fin, m_fin = trunk_layer(xp, mod_ctx, lw, zero_h, zero_c, zero_n, zero_m, None)
        new_h.append(h_fin)
        new_c.append(c_fin)
        new_n.append(n_fin)
        new_m.append(m_fin)
        xs, _, _, _, _ = trunk_layer(xs, mod_lat, lw, state_rglru_h[:, l], state_mlstm_C[:, l],
                                     state_mlstm_n[:, l], state_mlstm_m[:, l], rows)
    new_rglru_h = jnp.stack(new_h, axis=1).astype(x_prompt.dtype)
    new_mlstm_C = jnp.stack(new_c, axis=1).astype(x_prompt.dtype)
    new_mlstm_n = jnp.stack(new_n, axis=1).astype(x_prompt.dtype)
    new_mlstm_m = jnp.stack(new_m, axis=1).astype(x_prompt.dtype)
    return (xp, xs, new_rglru_h, new_mlstm_C, new_mlstm_n, new_mlstm_m)
```

```python
import os
import numpy as np
from contextlib import ExitStack
import concourse.bass as bass
import concourse.mybir as mybir
from concourse.bass_utils import run_bass_kernel_spmd

F32 = mybir.dt.float32
BF16 = mybir.dt.bfloat16
AF = mybir.ActivationFunctionType
ALU = mybir.AluOpType
AX = mybir.AxisListType

ATTACH_WAIT = True
D = 1024
NCORES = 8
ALPHA = 2.0 ** 0.25
LN_EPS = 1e-5
HD = 256
NH = 4
TB = 1024
NCH = TB // 128
OFF_XR, OFF_ZR, OFF_Q, OFF_K, OFF_V, OFF_O, OFF_GM, OFF_IG, OFF_FG = 0, 1024, 2048, 3072, 4096, 5120, 6144, 8192, 8200
D_IN = 8208


class Tok:
    __slots__ = ("sem", "val", "eng", "snap")

    def __init__(self, sem, val, eng, snap=None):
        self.sem, self.val, self.eng, self.snap = sem, val, eng, snap


class KState:
    __slots__ = ("w", "r")

    def __init__(self):
        self.w = None
        self.r = {}


class Trk:
    NSLOT = 8

    def __init__(self, nc, es):
        self.nc = nc
        self.eng = {"pe": nc.tensor, "act": nc.scalar, "dve": nc.vector, "pool": nc.gpsimd, "sp": nc.sync}
        self.sem = {e: es.enter_context(nc.semaphore("s_" + e)) for e in self.eng}
        self.cnt = {e: 0 for e in self.eng}
        self.seen = {e: {} for e in self.eng}
        self.keys = {}
        self.dq = ("sp", "pool", "act")
        self.dsem = {q: [es.enter_context(nc.semaphore("d_%s%d" % (q, i))) for i in range(self.NSLOT)] for q in self.dq}
        self.dcnt = {q: 0 for q in self.dq}
        self.dtok = {q: [None] * self.NSLOT for q in self.dq}
        self.nwait = 0
        self.nops = 0

    def _overlaps(self, key):
        root = self.keys.get(key[0])
        if not root:
            return []
        out = []
        n = len(key)
        for k2, st in root.items():
            m = min(n, len(k2))
            if k2[:m] == key[:m]:
                out.append((k2, st))
        return out

    def _state(self, key):
        root = self.keys.setdefault(key[0], {})
        st = root.get(key)
        if st is None:
            st = KState()
            root[key] = st
        return st

    def _deps(self, r, w):
        deps = []
        for k in r:
            for _, st in self._overlaps(k):
                if st.w is not None:
                    deps.append((st.w, True))
        for k in w:
            for _, st in self._overlaps(k):
                if st.w is not None:
                    deps.append((st.w, False))
                for t in st.r.values():
                    deps.append((t, False))
        return deps

    def _waits(self, eng, deps, skip_same=True, attach=False):
        need = {}
        for t, raw in deps:
            if skip_same and t.eng == eng and not raw and eng == "pe":
                continue
            cur = need.get(id(t.sem))
            if cur is None or cur.val < t.val:
                need[id(t.sem)] = t
        todo = []
        seen = self.seen[eng]
        for sid, t in sorted(need.items(), key=lambda kv: -kv[1].val):
            if seen.get(sid, 0) >= t.val:
                continue
            seen[sid] = t.val
            todo.append(t)
            self.nwait += 1
            if t.snap:
                for k2, v2 in t.snap.items():
                    if seen.get(k2, 0) < v2:
                        seen[k2] = v2
        keep = todo[-1:] if (attach and ATTACH_WAIT) else []
        for t in todo[:len(todo) - len(keep)]:
            self.eng[eng].wait_ge(t.sem, t.val)
        return keep

    def _update(self, tok, r, w):
        for k in w:
            root = self.keys.setdefault(k[0], {})
            n = len(k)
            for k2 in [k2 for k2 in root if len(k2) > n and k2[:n] == k]:
                del root[k2]
            st = self._state(k)
            st.w = tok
            st.r = {}
        for k in r:
            st = self._state(k)
            st.r[id(tok.sem)] = tok

    def op(self, eng, fn, r=(), w=(), x=()):
        r = [k if isinstance(k, tuple) else (k,) for k in r]
        w = [k if isinstance(k, tuple) else (k,) for k in w]
        deps = self._deps(r, w)
        for k in x:
            k = k if isinstance(k, tuple) else (k,)
            for _, st in self._overlaps(k):
                if st.w is not None:
                    deps.append((st.w, True))
                for t in st.r.values():
                    if t.eng != eng:
                        deps.append((t, True))
            r.append(k)
        keep = self._waits(eng, deps, attach=True)
        self.cnt[eng] += 1
        tok = Tok(self.sem[eng], self.cnt[eng], eng, dict(self.seen[eng]))
        inst = fn(self.eng[eng])
        for t in keep:
            inst._wait_ge(t.sem, t.val)
        inst.then_inc(self.sem[eng], 1)
        self._update(tok, r, w)
        self.nops += 1
        return tok

    def dma(self, q, out, in_, r=(), w=(), **kw):
        r = [k if isinstance(k, tuple) else (k,) for k in r]
        w = [k if isinstance(k, tuple) else (k,) for k in w]
        slot = self.dcnt[q] % self.NSLOT
        deps = self._deps(r, w)
        if self.dtok[q][slot] is not None:
            deps.append((self.dtok[q][slot], True))
        keep = self._waits(q, deps, skip_same=False, attach=True)
        val = 16 * (self.dcnt[q] // self.NSLOT + 1)
        self.dcnt[q] += 1
        tok = Tok(self.dsem[q][slot], val, None, dict(self.seen[q]))
        inst = self.eng[q].dma_start(out=out, in_=in_, **kw)
        for t in keep:
            inst._wait_ge(t.sem, t.val)
        inst.then_inc(self.dsem[q][slot], 16)
        self.dtok[q][slot] = tok
        self._update(tok, r, w)
        return tok

    def fence(self, *keys):
        for key in keys:
            key = key if isinstance(key, tuple) else (key,)
            root = self.keys.get(key[0])
            if not root:
                continue
            n = len(key)
            toks = {}
            for k2 in [k2 for k2 in root if len(k2) >= n and k2[:n] == key]:
                st = root.pop(k2)
                for t in ([st.w] if st.w is not None else []) + list(st.r.values()):
                    cur = toks.get(id(t.sem))
                    if cur is None or cur.val < t.val:
                        toks[id(t.sem)] = t
            st = KState()
            st.r = toks
            root[key] = st

    def finish(self):
        deps = []
        for q in self.dq:
            for t in self.dtok[q]:
                if t is not None:
                    deps.append((t, True))
        for e in self.eng:
            if self.cnt[e]:
                deps.append((Tok(self.sem[e], self.cnt[e], e), True))
        self._waits("sp", deps)


def _fm(v, nt):
    return np.ascontiguousarray(np.asarray(v, np.float32).reshape(nt, 128).T)


def make_consts():
    c = np.zeros((128, 128 * 4 + 2), np.float32)
    c[:, 0:128] = np.eye(128, dtype=np.float32)
    j = np.arange(128)[:, None]
    i = np.arange(128)[None, :]
    c[:, 128:256] = (j <= i)
    c[:, 256:384] = (j >= i)
    c[:, 384:512] = 1.0
    c[0, 512] = 1.0
    c[32, 513] = 1.0
    rm = np.ones((33, TB), np.float32)
    rm[:, ::128] = 0.0
    return c, rm


def prep_core_inputs(i, inp):
    b, seg = i // 4, i % 4
    f = lambda a: np.ascontiguousarray(np.asarray(a, np.float32))
    m = {}
    m["xp"] = f(inp["x_prompt"][4 * i:4 * i + 4].reshape(4 * 256, D))
    xs = np.asarray(inp["x_sample"][b], np.float32)
    passes = {0: [(3, 1), (2, 1), (1, 1)], 1: [(0, 0), (3, 1), (2, 1)], 2: [(0, 0), (1, 0), (3, 1)], 3: [(0, 0), (1, 0), (2, 0)]}[seg]
    m["xoth"] = f(np.concatenate([xs[s * TB:(s + 1) * TB][::-1] if d else xs[s * TB:(s + 1) * TB] for s, d in passes], 0))
    m["xseg"] = f(xs[seg * TB:(seg + 1) * TB])
    w_in0 = np.asarray(inp["w_in"][0], np.float32)
    pwg = np.zeros((128, 3, 8, 2, NH), np.float32)
    pgb = np.zeros((33, 3, 2, NH), np.float32)
    prgb = np.zeros((128, 3, 3, 8), np.float32)
    pconv = np.zeros((128, 3, 8, 5), np.float32)
    pdf = np.zeros((128, 3, 2), np.float32)
    cw4 = [_fm(inp["rg_conv_w"][0][k], 8) for k in range(4)]
    for p, (sg, d) in enumerate(passes):
        for g, off in enumerate((OFF_IG, OFF_FG)):
            for h in range(NH):
                pwg[:, p, :, g, h] = _fm(w_in0[:, off + d * NH + h], 8)
        pgb[0, p, 0] = inp["ml_b_igate"][0][d]
        pgb[0, p, 1] = inp["ml_b_fgate"][0][d]
        prgb[:, p, 0] = _fm(inp["rg_ba"][0][d], 8)
        prgb[:, p, 1] = _fm(inp["rg_bx"][0][d], 8)
        prgb[:, p, 2] = _fm(inp["rg_lambda"][0][d], 8)
        taps = [cw4[0], cw4[1], cw4[2], cw4[3], None] if d == 0 else [None, cw4[3], cw4[2], cw4[1], cw4[0]]
        for o, t in enumerate(taps):
            if t is not None:
                pconv[:, p, :, o] = t
        pdf[:, p, 0] = 1.0 if d == 0 else 0.0
        pdf[:, p, 1] = 0.0 if d == 0 else 1.0
    m["pwg"], m["pgb"], m["prgb"], m["pconv"], m["pdf"] = pwg, pgb, prgb, pconv, pdf
    m["prg_wa"] = f(np.stack([np.asarray(inp["rg_wa"][0][d]) for _, d in passes], 0))
    m["prg_wx"] = f(np.stack([np.asarray(inp["rg_wx"][0][d]) for _, d in passes], 0))
    cv = np.zeros((128, 8, 2), np.float32)
    cv[:, :, 0] = _fm(inp["c_ctx"], 8)
    cv[:, :, 1] = _fm(inp["c"][b], 8)
    m["cvec"] = cv
    m["w_ada"] = f(inp["w_ada"][0])
    bfm = _fm(inp["b_ada"][0], 48)
    m["b_ada_fm"] = f(np.stack([bfm, bfm], -1))
    m["b_ada"] = f(inp["b_ada"])
    m["w_in"] = f(inp["w_in"][0])
    m["w_gates"] = f(np.asarray(inp["w_in"][0])[:, OFF_IG:OFF_IG + 16].reshape(8, 128, 16).transpose(1, 0, 2))
    m["conv_w_fm"] = f(np.stack([_fm(inp["rg_conv_w"][0][k], 8) for k in range(4)], -1))
    m["conv_b_fm"] = _fm(inp["rg_conv_b"][0], 8)
    m["rg_wa"] = f(inp["rg_wa"][0])
    m["rg_wx"] = f(inp["rg_wx"][0])
    rgb = np.zeros((128, 3, 2, 8), np.float32)
    for d in range(2):
        rgb[:, 0, d] = _fm(inp["rg_ba"][0][d], 8)
        rgb[:, 1, d] = _fm(inp["rg_bx"][0][d], 8)
        rgb[:, 2, d] = _fm(inp["rg_lambda"][0][d], 8)
    m["rgb_fm"] = rgb
    m["w_rg_proj"] = f(inp["w_rg_proj"][0])
    m["w_ml_proj"] = f(inp["w_ml_proj"][0])
    m["w_out"] = f(inp["w_out"][0])
    m["w_fc"] = f(inp["w_fc"][0])
    m["w_proj"] = f(inp["w_proj"][0])
    gb = np.zeros((33, 2, NH), np.float32)
    gb[0, 0] = inp["ml_b_igate"][0][0]
    gb[32, 0] = inp["ml_b_igate"][0][1]
    gb[0, 1] = inp["ml_b_fgate"][0][0]
    gb[32, 1] = inp["ml_b_fgate"][0][1]
    m["gbias"] = gb
    rows = np.stack([inp["ln_g"][0][0], inp["ln_b"][0][0], inp["ln_g"][0][1], inp["ln_b"][0][1],
                     inp["b_proj"][0], inp["ml_gn_g"][0]], 0)
    m["rows6"] = f(rows)
    m["b_fc_fm"] = _fm(inp["b_fc"][0], 32)
    m["b_merge_fm"] = _fm(inp["b_merge"][0], 16)
    sh = np.zeros((128, 2, 8), np.float32)
    for d in range(2):
        sh[:, d] = _fm(inp["state_rglru_h"][b, 0, d], 8)
    m["st_h"] = sh
    C = np.asarray(inp["state_mlstm_C"][b, 0], np.float32)
    n = np.asarray(inp["state_mlstm_n"][b, 0], np.float32)
    m["st_Cn"] = f(np.concatenate([C, n[..., None]], -1))
    sm = np.zeros((33, NH), np.float32)
    sm[0] = inp["state_mlstm_m"][b, 0, 0]
    sm[32] = inp["state_mlstm_m"][b, 0, 1]
    m["st_m"] = sm
    sm2 = np.zeros((33, 2, NH), np.float32)
    sm2[0, 0] = inp["state_mlstm_m"][b, 0, 0]
    sm2[0, 1] = inp["state_mlstm_m"][b, 0, 1]
    m["st_m2"] = sm2
    c, rm = make_consts()
    m["cst"] = c
    return m


IN_SHAPES = {
    "xp": [TB, D], "xoth": [3 * TB, D], "xseg": [TB, D], "cvec": [128, 8, 2], "w_ada": [D, 6 * D],
    "b_ada_fm": [128, 48, 2], "b_ada": [1, 6 * D], "w_in": [D, D_IN], "conv_w_fm": [128, 8, 4],
    "conv_b_fm": [128, 8], "rg_wa": [2, 16, 64, 64], "rg_wx": [2, 16, 64, 64], "rgb_fm": [128, 3, 2, 8],
    "w_rg_proj": [D, D], "w_ml_proj": [D, D], "w_out": [D, D], "w_fc": [D, 4 * D], "w_proj": [4 * D, D],
    "gbias": [33, 2, NH], "rows6": [6, D], "b_fc_fm": [128, 32], "b_merge_fm": [128, 16],
    "st_h": [128, 2, 8], "st_Cn": [2, NH, HD, HD + 1], "st_m": [33, NH], "st_m2": [33, 2, NH], "pwg": [128, 3, 8, 2, NH], "pgb": [33, 3, 2, NH],
    "prgb": [128, 3, 3, 8], "pconv": [128, 3, 8, 5], "pdf": [128, 3, 2], "prg_wa": [3, 16, 64, 64], "prg_wx": [3, 16, 64, 64],
    "cst": [128, 514], "w_gates": [128, 8, 16],
}
OUT_SHAPES = {
    "yp": [TB, D], "ys": [TB, D], "o_h": [128, 4, 2, 8], "o_C": [4, 2, NH, HD, HD],
    "o_n": [128, 4, 2, NH, 2], "o_m": [33, NH, 4],
}
NFS = 14
NBS = 22


class StopBuild(Exception):
    pass


class Prog:
    def __init__(self, dbg=(), stop=None):
        self.dbg_names = list(dbg)
        self.stop = stop
        nc = self.nc = bass.Bass("TRN2", target_bir_lowering=False)
        es = self.es = ExitStack()
        self.trk = Trk(nc, es)
        self.I = {k: nc.dram_tensor(k, v, F32, kind="ExternalInput").ap() for k, v in IN_SHAPES.items()}
        self.O = {k: nc.dram_tensor(k, v, F32, kind="ExternalOutput").ap() for k, v in OUT_SHAPES.items()}
        self.dbg_out = {}
        self.rot = {}
        T = self.tile
        self.FS = T("FS", [128, NFS, 1024], F32)
        self.BS = T("BS", [128, NBS, 2048], BF16)
        self.ps = [es.enter_context(nc.psum_tensor("ps%d" % i, [128, 512], F32)) for i in range(8)]

    def tile(self, name, shape, dt):
        return self.es.enter_context(self.nc.sbuf_tensor("t_" + name, shape, dt))

    def mark(self, name):
        if self.stop == name:
            raise StopBuild(name)

    def nxt(self, name, n):
        v = self.rot.get(name, 0)
        self.rot[name] = v + 1
        return v % n

    def op(self, eng, fn, r=(), w=(), x=()):
        return self.trk.op(eng, fn, r, w, x)

    def dma(self, q, out, in_, r=(), w=(), **kw):
        return self.trk.dma(q, out, in_, r, w, **kw)

    def mm(self, out, lhsT, rhs, start, stop, r, w):
        return self.op("pe", lambda e: e.matmul(out, lhsT, rhs, start=start, stop=stop), r, w)

    def act(self, out, in_, func, r, w, bias=None, scale=None, x=()):
        kw = {}
        if bias is not None:
            kw["bias"] = bias
        if scale is not None:
            kw["scale"] = scale
        return self.op("act", lambda e: e.activation(out=out, in_=in_, func=func, **kw), r, w, x)

    def tt(self, eng, out, in0, in1, op, r, w, x=()):
        return self.op(eng, lambda e: e.tensor_tensor(out=out, in0=in0, in1=in1, op=op), r, w, x)

    def ts(self, eng, out, in0, s1, s2, op0, op1, r, w, x=()):
        if op1 is None:
            return self.op(eng, lambda e: e.tensor_scalar(out=out, in0=in0, scalar1=s1, scalar2=None, op0=op0), r, w, x)
        return self.op(eng, lambda e: e.tensor_scalar(out=out, in0=in0, scalar1=s1, scalar2=s2, op0=op0, op1=op1), r, w, x)

    def stt(self, out, in0, scalar, in1, op0, op1, r, w, x=()):
        return self.op("dve", lambda e: e.scalar_tensor_tensor(out=out, in0=in0, scalar=scalar, in1=in1, op0=op0, op1=op1), r, w, x)

    def cp(self, eng, out, in_, r, w, x=()):
        if eng == "act":
            return self.op("act", lambda e: e.copy(out=out, in_=in_), r, w, x)
        return self.op(eng, lambda e: e.tensor_copy(out=out, in_=in_), r, w, x)

    def dump(self, name, ap, shape, r):
        if name not in self.dbg_names:
            return
        d = self.nc.dram_tensor("dbg_" + name, list(shape), ap.dtype, kind="ExternalOutput").ap()
        self.dbg_out[name] = (list(shape), ap.dtype)
        self.dma("sp", d, ap, r=r, w=[("dbgout", name)])

    def f(self, s):
        return self.FS[:, s, :]

    def b(self, s):
        return self.BS[:, s, :]

    def bview(self, s, n, k, c):
        return self.BS[:, s:s + n, :].rearrange("p a (k2 c) -> p (a k2) c", c=c)

    def fview(self, s, n, k, c):
        return self.FS[:, s:s + n, :].rearrange("p a (k2 c) -> p (a k2) c", c=c)

    def setup(self):
        I, T = self.I, self.tile
        self.cst = T("cst", [128, 514], F32)
        self.dma("sp", self.cst[:], I["cst"][:, :], w=["cst"])
        self.ident = self.cst[:, 0:128]
        self.ones = self.cst[:, 384:512]
        self.maskb = T("maskb", [128, 2, 128], BF16)
        for d in range(2):
            self.cp("dve", self.maskb[:, d, :], self.cst[:, 128 * (d + 1):128 * (d + 2)], r=["cst"], w=[("maskb", d)])
        self.epsT = T("epsT", [128, 4], F32)
        for j, v in enumerate((LN_EPS, 1.0, 0.25, 0.5)):
            self.op("pool", lambda e, j=j, v=v: e.memset(self.epsT[:, j:j + 1], v), w=["epsT"])
        small = {}
        for name, key, shape in [("convw", "conv_w_fm", [128, 8, 4]), ("convb", "conv_b_fm", [128, 8]),
                                 ("rgb", "rgb_fm", [128, 3, 2, 8]), ("bfc", "b_fc_fm", [128, 32]),
                                 ("bmerge", "b_merge_fm", [128, 16]), ("pdf", "pdf", [128, 3, 2]),
                                 ("pwg", "pwg", [128, 3, 8, 2, NH]), ("prgb", "prgb", [128, 3, 3, 8]), ("pconv", "pconv", [128, 3, 8, 5]),
                                 ("badafm", "b_ada_fm", [128, 48, 2]), ("csl", "cvec", [128, 8, 2])]:
            t = T(name, shape, F32)
            src = I[key]
            self.dma("sp", t[:], src[tuple(slice(None) for _ in shape)], w=[name])
            small[name] = t
        self.__dict__.update(small)
        self.mark("s1")
        self.gbias = T("gbias", [33, 2, NH], F32)
        self.dma("sp", self.gbias[:], I["gbias"][:, :, :], w=["gbias"])
        self.RGH = T("RGH", [128, 2, 8], F32)
        self.dma("sp", self.RGH[:], I["st_h"][:, :, :], w=["RGH"])
        self.CST = T("CST", [128, 2, NH, 2, HD + 1], F32)
        for d in range(2):
            for h in range(NH):
                self.dma("sp", self.CST[:, d, h, :, :], I["st_Cn"][d, h].rearrange("(t p) e -> p t e", p=128),
                         w=[("CST", d, h)])
        self.MST = T("MST", [33, NH], F32)
        self.dma("sp", self.MST[:], I["st_m"][:, :], w=["MST"])
        self.MS2 = T("MS2", [33, 2, NH], F32)
        self.dma("sp", self.MS2[:], I["st_m2"][:, :, :], w=["MS2"])
        self.pgb = T("pgb", [33, 3, 2, NH], F32)
        self.dma("sp", self.pgb[:], I["pgb"][:, :, :, :], w=["pgb"])
        self.gbh = T("gbh", [33, NH], F32)
        self.ts("dve", self.gbh[:], self.gbias[:, 1, :], 0.5, None, ALU.mult, None, r=["gbias"], w=["gbias"])
        self.pgbh = T("pgbh", [33, 3, NH], F32)
        self.ts("dve", self.pgbh[:], self.pgb[:, :, 1, :], 0.5, None, ALU.mult, None, r=["pgb"], w=["pgb"])
        self.WRGp = T("WRGp", [128, 2, 8, 128], BF16)
        self.op("pool", lambda e: e.memset(self.WRGp[:], 0.0), w=["WRGp"])
        self.mark("s2")
        self.clam = T("clam", [128, 2, 8], F32)
        self.act(self.clam[:], self.rgb[:, 2, :, :], AF.Sigmoid, r=["rgb"], w=["clam"])
        self.act(self.clam[:], self.clam[:], AF.Ln, r=["clam"], w=["clam"])
        self.ts("dve", self.clam[:], self.clam[:], 8.0, None, ALU.mult, None, r=["clam"], w=["clam"])
        self.clam2 = T("clam2", [128, 2, 8], F32)
        self.ts("dve", self.clam2[:], self.clam[:], 2.0, None, ALU.mult, None, r=["clam"], w=["clam"])
        self.clamh = T("clamh", [128, 2, 8], F32)
        self.ts("dve", self.clamh[:], self.clam[:], 0.5, None, ALU.mult, None, r=["clam"], w=["clam"])
        self.rgbh = T("rgbh", [128, 2, 2, 8], F32)
        self.ts("dve", self.rgbh[:], self.rgb[:, 0:2, :, :], 0.5, None, ALU.mult, None, r=["rgb"], w=["rgb"])
        self.prgbh = T("prgbh", [128, 3, 2, 8], F32)
        self.ts("dve", self.prgbh[:], self.prgb[:, :, 0:2, :], 0.5, None, ALU.mult, None, r=["prgb"], w=["prgb"])
        self.pclamh = T("pclamh", [128, 3, 8], F32)
        self.pclam2 = T("pclam2", [128, 3, 8], F32)
        self.pclam = T("pclam", [128, 3, 8], F32)
        self.act(self.pclam[:], self.prgb[:, :, 2, :], AF.Sigmoid, r=["prgb"], w=["pclam"])
        self.act(self.pclam[:], self.pclam[:], AF.Ln, r=["pclam"], w=["pclam"])
        self.ts("dve", self.pclam[:], self.pclam[:], 8.0, None, ALU.mult, None, r=["pclam"], w=["pclam"])
        self.ts("dve", self.pclam2[:], self.pclam[:], 2.0, None, ALU.mult, None, r=["pclam"], w=["pclam"])
        self.ts("dve", self.pclamh[:], self.pclam[:], 0.5, None, ALU.mult, None, r=["pclam"], w=["pclam"])
        self.mark("s3")
        self.WRG = T("WRG", [128, 2, 2, 8, 128], BF16)
        self.op("pool", lambda e: e.memset(self.WRG[:], 0.0), w=["WRG"])
        for ax, key in enumerate(("rg_wa", "rg_wx")):
            for d in range(2):
                for e2 in range(2):
                    src = I[key][d, e2::2, :, :].rearrange("c k j -> k c j")
                    self.dma("pool", self.WRG[e2 * 64:(e2 + 1) * 64, d, ax, :, e2 * 64:(e2 + 1) * 64], src, r=[], w=["WRG"])
        self.mark("s4")
        self.Wg = T("Wg", [128, 8, 2, 64], BF16)
        self.op("pool", lambda e: e.memset(self.Wg[:], 0.0), w=["Wg"])
        self.Wgf = T("Wgf", [128, 8, 16], F32)
        self.dma("sp", self.Wgf[:], I["w_gates"][:, :, :], w=["Wgf"])
        self.vtok = T("vtok", [128, NCH, HD + 1], F32)
        self.op("pool", lambda e: e.memset(self.vtok[:, :, HD:HD + 1], 1.0), w=["vtok"])
        self.mark("s5")
        self.act(self.csl[:], self.csl[:], AF.Silu, r=["csl"], w=["csl"])
        self.modfm = T("modfm", [128, 48, 2], F32)
        self.mrow = T("mrow", [2, 2, 512], F32)
        self.G = T("G", [128, 2, D], F32)
        def issue_wad(cg):
            s0 = 4 * (cg % 3)
            wad = self.fview(s0, 4, 8, 512)
            src = I["w_ada"][:, cg * 512:(cg + 1) * 512].rearrange("(k p) c -> p k c", p=128)
            self.dma("sp", wad[:, 0:4, :], src[:, 0:4, :], w=[("F", s0 + j) for j in (0, 1)])
            self.dma("act", wad[:, 4:8, :], src[:, 4:8, :], w=[("F", s0 + j) for j in (2, 3)])

        issue_wad(0)
        issue_wad(1)
        for cg in range(12):
            if cg + 2 < 12:
                issue_wad(cg + 2)
            s0 = 4 * (cg % 3)
            wad = self.fview(s0, 4, 8, 512)
            rk = [("F", s0 + j) for j in range(4)]
            pt = self.ps[cg % 2]
            for k in range(8):
                self.mm(pt[0:2, :], self.csl[:, k, :], wad[:, k, :], k == 0, k == 7, r=rk + ["csl"], w=[("ps", cg % 2)])
            rr = self.nxt("mrow", 2)
            self.cp("act", self.mrow[0:2, rr, :], pt[0:2, :], r=[], w=[("mrow", rr)], x=[("ps", cg % 2)])
            p2 = self.ps[2 + cg % 2]
            for ft4 in range(4):
                self.op("pe", lambda e, ft4=ft4, p2=p2, rr=rr: e.transpose(out=p2[:, ft4 * 2:ft4 * 2 + 2],
                                                                       in_=self.mrow[0:2, rr, ft4 * 128:(ft4 + 1) * 128],
                                                                       identity=self.ident[0:2, 0:2]),
                        r=[("mrow", rr), "cst"], w=[("ps", 2 + cg % 2)])
            self.tt("dve", self.modfm[:, cg * 4:(cg + 1) * 4, :], p2[:, 0:8].rearrange("p (a b) -> p a b", b=2),
                    self.badafm[:, cg * 4:(cg + 1) * 4, :], ALU.add, r=["badafm"], w=[("modfm", cg)], x=[("ps", 2 + cg % 2)])
        self.mark("s6")
        for base in (8, 32):
            self.ts("dve", self.modfm[:, base:base + 8, :], self.modfm[:, base:base + 8, :], 1.0, None, ALU.add, None,
                    r=["modfm"], w=["modfm"])
        self.trk.fence(*[("F", s) for s in range(NFS)])
        self.dump("modfm", self.modfm[:], [128, 48, 2], r=["modfm"])
        self.dump("clam", self.clam[:], [128, 2, 8], r=["clam"])
        self.st6 = T("st6", [128, 8, 2, 6], F32)
        self.mv = T("mv", [128, 8, 2], F32)
        self.sd = T("sd", [128, 8, 3], F32)

    def gates(self, cond):
        self.trk.fence(*[("F", s) for s in range(NFS)])
        dg = self.fview(0, 2, 16, 128)
        for gate, base in ((0, 16), (1, 40)):
            for ft in range(8):
                i = gate * 8 + ft
                self.ts("dve", dg[:, i, :], self.ident, self.modfm[:, base + ft, cond:cond + 1], None, ALU.mult, None,
                        r=["cst", "modfm"], w=[("F", i // 8, i)])
                pb = self.nxt("psU", 8)
                self.mm(self.ps[pb][:, 0:128], self.ones, dg[:, i, :], True, True, r=["cst", ("F", i // 8, i)], w=[("ps", pb)])
                self.cp("act", self.G[:, gate, ft * 128:(ft + 1) * 128], self.ps[pb][:, 0:128], r=[], w=[("G", gate, ft)], x=[("ps", pb)])
        self.trk.fence(*[("F", s) for s in range(NFS)])
        self.dump("G%d" % cond, self.G[:], [128, 2, D], r=["G"])

    def ln_stats(self, x_ap, n, rk, want_nmr=True):
        i = self.nxt("lnst", 8)
        nchunk = (n + 511) // 512
        for j in range(nchunk):
            self.op("dve", lambda e, j=j: e.bn_stats(out=self.st6[:, i, j, :], in_=x_ap[:, j * 512:min(n, (j + 1) * 512)]),
                    r=rk, w=[("st6", i, j)])
        self.op("dve", lambda e: e.bn_aggr(out=self.mv[:, i, :], in_=self.st6[:, i, 0:nchunk, :]), r=[("st6", i)], w=[("mv", i)])
        self.act(self.sd[:, i, 0:1], self.mv[:, i, 1:2], AF.Sqrt, r=[("mv", i), "epsT"], w=[("sd", i, 0)], bias=self.epsT[:, 0:1])
        self.op("dve", lambda e: e.reciprocal(out=self.sd[:, i, 1:2], in_=self.sd[:, i, 0:1]), r=[("sd", i, 0)], w=[("sd", i, 1)])
        if not want_nmr:
            return self.sd[:, i, 1:2], self.mv[:, i, 0:1], [("sd", i), ("mv", i)]
        self.stt(self.sd[:, i, 2:3], self.mv[:, i, 0:1], -1.0, self.sd[:, i, 1:2], ALU.mult, ALU.mult,
                 r=[("mv", i), ("sd", i, 1)], w=[("sd", i, 2)])
        return self.sd[:, i, 1:2], self.sd[:, i, 2:3], [("sd", i)]

    def ln_affine_group(self, items, g_ap, g_key, b_ap, b_key):
        ids = [self.nxt("lnst", 8) for _ in items]
        for (x, kx), i in zip(items, ids):
            for j in range(2):
                self.op("dve", lambda e, j=j, i=i, x=x: e.bn_stats(out=self.st6[:, i, j, :], in_=x[:, j * 512:(j + 1) * 512]),
                        r=[kx], w=[("st6", i, j)])
            self.op("dve", lambda e, i=i: e.bn_aggr(out=self.mv[:, i, :], in_=self.st6[:, i, 0:2, :]), r=[("st6", i)], w=[("mv", i)])
        for (x, kx), i in zip(items, ids):
            self.act(self.sd[:, i, 0:1], self.mv[:, i, 1:2], AF.Sqrt, r=[("mv", i), "epsT"], w=[("sd", i, 0)], bias=self.epsT[:, 0:1])
        for (x, kx), i in zip(items, ids):
            self.op("dve", lambda e, i=i: e.reciprocal(out=self.sd[:, i, 1:2], in_=self.sd[:, i, 0:1]), r=[("sd", i, 0)], w=[("sd", i, 1)])
            self.stt(self.sd[:, i, 2:3], self.mv[:, i, 0:1], -1.0, self.sd[:, i, 1:2], ALU.mult, ALU.mult,
                     r=[("mv", i), ("sd", i, 1)], w=[("sd", i, 2)])
        for (x, kx), i in zip(items, ids):
            self.act(x, x, AF.Identity, r=[kx, ("sd", i)], w=[kx], bias=self.sd[:, i, 2:3], scale=self.sd[:, i, 1:2])
        for (x, kx), i in zip(items, ids):
            self.tt("dve", x, x, g_ap, ALU.mult, r=[kx, g_key], w=[kx])
        for (x, kx), i in zip(items, ids):
            self.tt("pool", x, x, b_ap, ALU.add, r=[kx, b_key], w=[kx])

    def build_uT(self, xsrc, row0, cond, s0, mod_base, x_keep=None, chunks=None):
        uT = self.bview(s0, 4, 8, TB)
        clist = list(range(NCH) if chunks is None else chunks)
        G = 4
        for g0 in range(0, len(clist), G):
            grp = clist[g0:g0 + G]
            st = {}
            for c in grp:
                if xsrc is not None:
                    xs = self.nxt("xin", 4)
                    xin, xk = self.f(xs), [("F", xs)]
                    self.dma("sp", xin, xsrc[row0 + c * 128:row0 + (c + 1) * 128, :], w=xk)
                    ns = 4 + self.nxt("xn", 4)
                else:
                    xin, xk = x_keep(c)
                    ns = 2 + self.nxt("xn", 4)
                st[c] = dict(xin=xin, xk=xk, i=self.nxt("lnst", 8), ns=ns)
            for c in grp:
                d_ = st[c]
                i = d_["i"]
                for j in range(2):
                    self.op("dve", lambda e, j=j, i=i, xin=d_["xin"]: e.bn_stats(out=self.st6[:, i, j, :], in_=xin[:, j * 512:(j + 1) * 512]),
                            r=d_["xk"], w=[("st6", i, j)])
                self.op("dve", lambda e, i=i: e.bn_aggr(out=self.mv[:, i, :], in_=self.st6[:, i, 0:2, :]), r=[("st6", i)], w=[("mv", i)])
            for c in grp:
                i = st[c]["i"]
                self.act(self.sd[:, i, 0:1], self.mv[:, i, 1:2], AF.Sqrt, r=[("mv", i), "epsT"], w=[("sd", i, 0)], bias=self.epsT[:, 0:1])
            for c in grp:
                i = st[c]["i"]
                self.op("dve", lambda e, i=i: e.reciprocal(out=self.sd[:, i, 1:2], in_=self.sd[:, i, 0:1]), r=[("sd", i, 0)], w=[("sd", i, 1)])
                self.stt(self.sd[:, i, 2:3], self.mv[:, i, 0:1], -1.0, self.sd[:, i, 1:2], ALU.mult, ALU.mult,
                         r=[("mv", i), ("sd", i, 1)], w=[("sd", i, 2)])
            for c in grp:
                d_ = st[c]
                i, ns = d_["i"], d_["ns"]
                self.act(self.f(ns), d_["xin"], AF.Identity, r=d_["xk"] + [("sd", i)], w=[("F", ns)], bias=self.sd[:, i, 2:3], scale=self.sd[:, i, 1:2])
            for c in grp:
                ns = st[c]["ns"]
                xn = self.f(ns)
                st[c]["pb"] = []
                for half in range(2):
                    pb = self.nxt("psT", 8)
                    st[c]["pb"].append(pb)
                    pt = self.ps[pb]
                    for j in range(4):
                        ft = half * 4 + j
                        self.op("pe", lambda e, ft=ft, j=j, pt=pt, xn=xn: e.transpose(out=pt[:, j * 128:(j + 1) * 128],
                                                                                     in_=xn[:, ft * 128:(ft + 1) * 128], identity=self.ident),
                                r=[("F", ns), "cst"], w=[("ps", pb)])
            for c in grp:
                for half in range(2):
                    pb = st[c]["pb"][half]
                    pt = self.ps[pb]
                    for j in range(4):
                        ft = half * 4 + j
                        sc = self.modfm[:, mod_base + 8 + ft, cond:cond + 1]
                        sh = self.modfm[:, mod_base + ft, cond:cond + 1]
                        dst = uT[:, ft, c * 128:(c + 1) * 128]
                        wk = [("B", s0 + ft // 2, ft, c)]
                        if half == 0:
                            self.act(dst, pt[:, j * 128:(j + 1) * 128], AF.Identity, r=["modfm"], w=wk, x=[("ps", pb)], bias=sh, scale=sc)
                        else:
                            self.ts("dve", dst, pt[:, j * 128:(j + 1) * 128], sc, sh, ALU.mult, ALU.add, r=["modfm"], w=wk, x=[("ps", pb)])
        return uT

    XST, XC0, RA, GN, IB, HFS, HBS, GI, GF = 0, 1, 3, 4, 5, 6, 7, 8, 9
    BC, TH, WNB, HFM, HSN, CW0 = 0, 1, 2, 4, 6, 10
    UT, HRG, HMT, W1, WKV, XCB, GZ, QT, KT, KTOK, OG = 0, 4, 8, 12, 14, 16, 17, 18, 19, 20, 21

    def load_w(self, dst, col0, wkeys, src="w_in", ncol=256):
        self.dma("pool", dst, self.I[src][:, col0:col0 + ncol].rearrange("(k p) c -> p k c", p=128), w=wkeys)

    def wplan(self, items):
        self.wq, self.wi_issue, self.wi_use = list(items), 0, 0

    def wload_next(self):
        if self.wi_issue >= len(self.wq):
            return
        ds = 12 + 2 * (self.wi_issue % 2)
        W = self.bview(ds, 2, 8, 512)
        for (dcol, src, col0, ncol) in self.wq[self.wi_issue]:
            self.load_w(W[:, :, dcol:dcol + ncol], col0, [("B", ds, dcol), ("B", ds + 1, dcol)], src=src, ncol=ncol)
        self.wi_issue += 1

    def wget(self):
        i = self.wi_use
        self.wi_use += 1
        while self.wi_issue <= i:
            self.wload_next()
        ds = 12 + 2 * (i % 2)
        return ds, self.bview(ds, 2, 8, 512)

    def unit_witems(self, h, outs):
        it = [[(0, "w_in", OFF_XR + 256 * h, 256)]]
        if outs:
            it += [[(0, "w_in", OFF_ZR + 256 * h, 256)], [(0, "w_in", OFF_Q + 256 * h, 256)], [(0, "w_in", OFF_K + 256 * h, 256)]]
        it += [[(0, "w_in", OFF_K + 256 * h, 256), (256, "w_in", OFF_V + 256 * h, 256)]]
        if outs:
            it += [[(0, "w_in", OFF_O + 256 * h, 256)]]
        return it

    def default_params(self):
        return dict(keys=["WRG", "rgb", "clam", "convw", "gbias", "Wgf"], ntap=4,
                    convw=lambda ct, k: self.convw[:, ct, k:k + 1],
                    WRG=lambda d, ax, ct: self.WRG[:, d, ax, ct, :],
                    rgbias=lambda ax, d, ct: self.rgbh[:, ax, d, ct:ct + 1],
                    clamh=lambda d, ct: self.clamh[:, d, ct:ct + 1],
                    clam=lambda d, ct: self.clam[:, d, ct:ct + 1],
                    clam2=lambda d, ct: self.clam2[:, d, ct:ct + 1],
                    gbias=lambda rows, g, h: (self.gbias[rows, 0, h:h + 1] if g == 0 else self.gbh[rows, h:h + 1]),
                    Wgf=lambda g, d, h: self.Wgf[:, :, (g * 2 + d) * NH + h:(g * 2 + d) * NH + h + 1])

    def pass_params(self, p):
        return dict(keys=["WRGp", "prgb", "pclam", "pconv", "pgb", "pwg"], ntap=5,
                    convw=lambda ct, k: self.pconv[:, p, ct, k:k + 1],
                    WRG=lambda d, ax, ct: self.WRGp[:, ax, ct, :],
                    rgbias=lambda ax, d, ct: self.prgbh[:, p, ax, ct:ct + 1],
                    clamh=lambda d, ct: self.pclamh[:, p, ct:ct + 1],
                    clam=lambda d, ct: self.pclam[:, p, ct:ct + 1],
                    clam2=lambda d, ct: self.pclam2[:, p, ct:ct + 1],
                    gbias=lambda rows, g, h: (self.pgb[rows, p, 0, h:h + 1] if g == 0 else self.pgbh[rows, p, h:h + 1]),
                    Wgf=lambda g, d, h: self.pwg[:, p, :, g, h:h + 1])

    def unit(self, h, cfg):
        seqs, dirs, outs = cfg["seqs"], cfg["dirs"], cfg["outputs"]
        L = cfg["conv_seg"]
        PR = cfg.get("params") or self.default_params()
        PK = PR["keys"]
        uT = self.bview(self.UT, 4, 8, TB)
        ukey = lambda kk: ("B", self.UT + kk // 2, kk)
        psA = lambda: self.nxt("psU", 8)
        psB = psA

        def proj_fm(col0, evac):
            s, W = self.wget()
            for t2 in range(2):
                for tb2 in range(2):
                    pb = psA()
                    for kk in range(8):
                        self.mm(self.ps[pb][:, :], W[:, kk, t2 * 128:(t2 + 1) * 128], uT[:, kk, tb2 * 512:(tb2 + 1) * 512],
                                kk == 0, kk == 7, r=[("B", s), ("B", s + 1), ukey(kk)], w=[("ps", pb)])
                    evac(pb, t2, tb2)
            self.wload_next()

        xcb = self.bview(self.XCB, 1, 2, TB)
        gz = self.bview(self.GZ, 1, 2, TB)
        hrgT = self.bview(self.HRG, 4, 8, TB)

        def evac_xr(pb, t2, tb2):
            self.cp("act", self.f(self.XST)[:, tb2 * 512:(tb2 + 1) * 512], self.ps[pb][:, :], r=[], w=[("F", self.XST, tb2)], x=[("ps", pb)])
            if tb2 == 1:
                ct = 2 * h + t2
                xs = self.f(self.XST).rearrange("p (s l) -> p s l", l=L)
                xc = self.f(self.XC0 + t2)
                xcv = xc.rearrange("p (s l) -> p s l", l=L)
                cw = lambda k: PR["convw"](ct, k)
                kx, kc = [("F", self.XST)], [("F", self.XC0 + t2)]
                self.ts("dve", xc, self.f(self.XST), cw(2), self.convb[:, ct:ct + 1], ALU.mult, ALU.add, r=kx + PK + ["convb"], w=kc)
                self.stt(xcv[:, :, 2:], xs[:, :, :L - 2], cw(0), xcv[:, :, 2:], ALU.mult, ALU.add, r=kx + kc + PK, w=kc)
                self.stt(xcv[:, :, 1:], xs[:, :, :L - 1], cw(1), xcv[:, :, 1:], ALU.mult, ALU.add, r=kx + kc + PK, w=kc)
                self.stt(xcv[:, :, :L - 1], xs[:, :, 1:], cw(3), xcv[:, :, :L - 1], ALU.mult, ALU.add, r=kx + kc + PK, w=kc)
                if PR["ntap"] == 5:
                    self.stt(xcv[:, :, :L - 2], xs[:, :, 2:], cw(4), xcv[:, :, :L - 2], ALU.mult, ALU.add, r=kx + kc + PK, w=kc)
                self.cp("act", xcb[:, t2, :], xc, r=kc, w=[("B", self.XCB, t2)])
                self.dump("xc%d_%d" % (h, t2), xc, [128, TB], r=kc)

        proj_fm(OFF_XR + 256 * h, evac_xr)
        self.mark("xr%d" % h)
        if outs:
            def evac_zr(pb, t2, tb2):
                self.act(gz[:, t2, tb2 * 512:(tb2 + 1) * 512], self.ps[pb][:, :], AF.Gelu_apprx_tanh, r=[], w=[("B", self.GZ, t2, tb2)], x=[("ps", pb)])
            proj_fm(OFF_ZR + 256 * h, evac_zr)
        self.mark("zr%d" % h)

        def rg_group(items):
            for t2, d, (RA, GN, IB), hs_ in items:
                ct = 2 * h + t2
                for ax, ds in ((0, RA), (1, IB)):
                    for tb2 in range(2):
                        pb = psB()
                        self.mm(self.ps[pb][:, :], PR["WRG"](d, ax, ct), xcb[:, t2, tb2 * 512:(tb2 + 1) * 512], True, True,
                                r=PK + [("B", self.XCB, t2)], w=[("ps", pb)])
                        self.act(self.f(ds)[:, tb2 * 512:(tb2 + 1) * 512], self.ps[pb][:, :], AF.Tanh, r=PK, w=[("F", ds, tb2)],
                                 x=[("ps", pb)], bias=PR["rgbias"](ax, d, ct), scale=0.5)
            for t2, d, (RA, GN, IB), hs_ in items:
                ct = 2 * h + t2
                xc, kc = self.f(self.XC0 + t2), [("F", self.XC0 + t2)]
                self.act(self.f(RA), self.f(RA), AF.Exp, r=[("F", RA)] + PK, w=[("F", RA)], scale=PR["clamh"](d, ct), bias=PR["clamh"](d, ct))
                self.tt("dve", self.f(GN), self.f(RA), self.f(RA), ALU.mult, r=[("F", RA)], w=[("F", GN)])
                self.stt(self.f(IB), self.f(IB), 1.0, xc, ALU.add, ALU.mult, r=[("F", IB)] + kc, w=[("F", IB)])
            for t2, d, (RA, GN, IB), hs_ in items:
                self.act(self.f(GN), self.f(GN), AF.Sqrt, r=[("F", GN), "epsT"], w=[("F", GN)], bias=self.epsT[:, 2:3], scale=-0.25)
                self.tt("dve", self.f(IB), self.f(IB), self.f(GN), ALU.mult, r=[("F", IB), ("F", GN)], w=[("F", IB)])
            for t2, d, (RA, GN, IB), hs_ in items:
                ct = 2 * h + t2
                ra, ib = self.f(RA), self.f(IB)
                hdst = self.f(hs_)
                for si, (t0, ln) in enumerate(seqs):
                    init = cfg["rg_init"](d, ct)
                    ini, ik = (0.0, []) if init is None else init
                    if d == 0:
                        o_, a_, b_ = hdst[:, t0:t0 + ln], ra[:, t0:t0 + ln], ib[:, t0:t0 + ln]
                    else:
                        o_, a_, b_ = hdst[:, t0:t0 + ln][:, ::-1], ra[:, t0:t0 + ln][:, ::-1], ib[:, t0:t0 + ln][:, ::-1]
                    self.op("dve", lambda e, o_=o_, a_=a_, b_=b_, ini=ini: e.tensor_tensor_scan(out=o_, data0=a_, data1=b_, initial=ini,
                                                                                          op0=ALU.mult, op1=ALU.add),
                            r=[("F", RA), ("F", IB)] + ik, w=[("F", hs_, si)])
                    fin = hdst[:, t0 + ln - 1:t0 + ln] if d == 0 else hdst[:, t0:t0 + 1]
                    cfg["rg_final"](d, ct, si, fin, [("F", hs_, si)])

        TS0, TS1 = (self.RA, self.GN, self.IB), (10, 11, 12)

        def rg_out(t2):
            ct = 2 * h + t2
            hf, hb = self.f(self.HFS), self.f(self.HBS)
            self.tt("pool", hf, hf, hb, ALU.add, r=[("F", self.HFS), ("F", self.HBS)], w=[("F", self.HFS)])
            self.tt("pool", hrgT[:, ct, :], hf, gz[:, t2, :], ALU.mult, r=[("F", self.HFS), ("B", self.GZ, t2)],
                    w=[("B", self.HRG + ct // 2, ct)])

        rg_steps = []
        if outs:
            for t2 in range(2):
                rg_steps.append(lambda t2=t2: rg_group([(t2, 0, TS0, self.HFS), (t2, 1, TS1, self.HBS)]))
                rg_steps.append(lambda t2=t2: rg_out(t2))
        else:
            d0 = dirs[0]
            rg_steps.append(lambda: rg_group([(0, d0, TS0, self.HFS), (1, d0, TS1, self.HBS)]))

        def rg_step():
            if rg_steps:
                rg_steps.pop(0)()


        qT = self.bview(self.QT, 1, 2, TB)
        kT = self.bview(self.KT, 1, 2, TB)
        ktok = self.bview(self.KTOK, 1, NCH, HD)
        og = self.bview(self.OG, 1, NCH, HD)
        if outs:
            def evac_q(pb, t2, tb2):
                self.ts("dve", qT[:, t2, tb2 * 512:(tb2 + 1) * 512], self.ps[pb][:, :], HD ** -0.5, None, ALU.mult, None, r=[],
                        w=[("B", self.QT, t2, tb2)], x=[("ps", pb)])
            proj_fm(OFF_Q + 256 * h, evac_q)
            rg_step()

            def evac_k(pb, t2, tb2):
                self.cp("dve", kT[:, t2, tb2 * 512:(tb2 + 1) * 512], self.ps[pb][:, :], r=[], w=[("B", self.KT, t2, tb2)], x=[("ps", pb)])
            proj_fm(OFF_K + 256 * h, evac_k)
            rg_step()
        skv, Wkv = self.wget()
        if outs:
            so, Wo = self.wget()
        for c in range(NCH):
            pb = psB()
            for kk in range(8):
                self.mm(self.ps[pb][:, :], uT[:, kk, c * 128:(c + 1) * 128], Wkv[:, kk, :], kk == 0, kk == 7,
                        r=[("B", skv), ("B", skv + 1), ukey(kk) + (c,)], w=[("ps", pb)])
            self.cp("act", ktok[:, c, :], self.ps[pb][:, 0:256], r=[], w=[("B", self.KTOK, c)], x=[("ps", pb)])
            self.cp("act", self.vtok[:, c, 0:HD], self.ps[pb][:, 256:512], r=[], w=[("vtok", c)], x=[("ps", pb)])
            if outs:
                pb = psB()
                for kk in range(8):
                    self.mm(self.ps[pb][:, 0:256], uT[:, kk, c * 128:(c + 1) * 128], Wo[:, kk, 0:256], kk == 0, kk == 7,
                            r=[("B", so), ("B", so + 1), ukey(kk) + (c,)], w=[("ps", pb)])
                self.act(og[:, c, :], self.ps[pb][:, 0:256], AF.Tanh, r=[], w=[("B", self.OG, c)], x=[("ps", pb)], scale=0.5)
                self.stt(og[:, c, :], og[:, c, :], 1.0, self.gngh[:, h * HD:(h + 1) * HD], ALU.add, ALU.mult, r=[("B", self.OG, c), "gngh"],
                         w=[("B", self.OG, c)])
            if c % 2 == 1:
                rg_step()
        while rg_steps:
            rg_step()
        self.dump("hrg%d" % h, hrgT[:, 2 * h:2 * h + 2, :], [128, 2, TB], r=[("B", self.HRG + h)])
        self.mark("rg%d" % h)
        self.wload_next()
        if outs:
            self.wload_next()
        for g, off in ((0, OFF_IG), (1, OFF_FG)):
            for d in range(2):
                col = off + d * NH + h
                self.cp("pool", self.Wg[:, :, g, 32 * d:32 * d + 1], PR["Wgf"](g, d, h), r=PK, w=[("Wg", g, d)])
        gt = [self.f(self.GI), self.f(self.GF)]
        gk = [self.GI, self.GF]
        for g in range(2):
            for tb2 in range(2):
                pb = psA()
                for kk in range(8):
                    self.mm(self.ps[pb][0:33, :], self.Wg[:, kk, g, 0:33], uT[:, kk, tb2 * 512:(tb2 + 1) * 512], kk == 0, kk == 7,
                            r=[("Wg", g), ukey(kk)], w=[("ps", pb)])
                fn = AF.Identity if g == 0 else AF.Tanh
                sc = 1.0 if g == 0 else 0.5
                self.act(gt[g][0:32, tb2 * 512:(tb2 + 1) * 512], self.ps[pb][0:32, :], fn, r=PK, w=[("F", gk[g], "n", tb2)],
                         x=[("ps", pb)], bias=PR["gbias"](slice(0, 32), g, h), scale=sc)
                lo = TB - 512 * (tb2 + 1)
                self.act(gt[g][32:33, lo:lo + 512][:, ::-1], self.ps[pb][32:33, :], fn, r=PK, w=[("F", gk[g], "f", tb2)],
                         x=[("ps", pb)], bias=PR["gbias"](slice(32, 33), g, h), scale=sc)
        self.mark("proj%d" % h)
        if outs:
            self.mlstm(h, cfg)
        else:
            self.mlstm_pass(h, cfg)

    def mlstm_tiles(self):
        if hasattr(self, "amax"):
            return
        T = self.tile
        self.amax = T("amax", [33, NCH], F32)
        self.mn = T("mn", [33, NCH], F32)
        self.mp = T("mp", [33, NCH], F32)
        self.ngp = T("ngp", [33, NCH], F32)
        self.dec = T("dec", [33, NCH], F32)
        self.decb = T("decb", [33, NCH], F32)
        self.cols = T("cols", [128, NCH, 4], F32)
        self.decrep = T("decrep", [128, 2, NCH], F32)
        self.den = T("den", [128, 4, 2], F32)
        for t in (self.amax, self.mn, self.mp, self.ngp, self.dec, self.decb):
            self.op("pool", lambda e, t=t: e.memset(t[:], 0.0), w=[t.name])

    def mlstm_pass(self, h, cfg):
        self.mlstm_tiles()
        F = self.f
        GI, GF, BC = self.GI, self.GF, self.BC
        self.trk.fence(*([("F", s) for s in range(8)] + [("B", self.XCB), ("B", self.GZ)]))
        gi, gf, bc = F(GI)[0:33], F(GF)[0:33], F(BC)[0:33]
        kn = lambda t: t.name
        m0, mk = cfg["m0"](h)
        self.act(gf, gf, AF.Ln, r=[("F", GF), "epsT"], w=[("F", GF)], scale=0.5, bias=self.epsT[0:33, 3:4])
        for c in range(NCH):
            sl = slice(c * 128, (c + 1) * 128)
            ini = 0.0 if c == 0 else bc[:, c * 128 - 1:c * 128]
            self.op("dve", lambda e, sl=sl, ini=ini: e.tensor_tensor_scan(out=bc[:, sl], data0=self.ones[0:33, 0:128], data1=gf[:, sl],
                                                                     initial=ini, op0=ALU.mult, op1=ALU.add),
                    r=[("F", GF), "cst"] + ([("F", BC, c - 1)] if c else []), w=[("F", BC, c)])
        self.tt("dve", gi, gi, bc, ALU.subtract, r=[("F", GI), ("F", BC)], w=[("F", GI)])
        am, gg, dl = self.amax[:, 0:1], self.mn[:, 0:1], self.dec[:, 0:1]
        self.op("dve", lambda e: e.tensor_reduce(out=am, in_=gi, axis=AX.X, op=ALU.max), r=[("F", GI)], w=[kn(self.amax)])
        self.tt("dve", gg, am, m0, ALU.max, r=[kn(self.amax)] + mk, w=[kn(self.mn)])
        self.tt("dve", dl, m0, gg, ALU.subtract, r=[kn(self.mn)] + mk, w=[kn(self.dec)])
        self.act(dl, dl, AF.Exp, r=[kn(self.dec)], w=[kn(self.dec)])
        self.ts("dve", self.ngp[:, 0:1], gg, -1.0, None, ALU.mult, None, r=[kn(self.mn)], w=[kn(self.ngp)])
        self.act(gi, gi, AF.Exp, r=[("F", GI), kn(self.ngp)], w=[("F", GI)], bias=self.ngp[:, 0:1])
        self.tt("dve", self.mp[:, 0:1], gg, bc[:, TB - 1:TB], ALU.add, r=[kn(self.mn), ("F", BC)], w=[kn(self.mp)])
        cfg["m_final"](h, self.mp, [kn(self.mp)], 0)
        pb = self.nxt("psA", 2)
        pw = self.ps[pb]
        one0 = self.ones[0:1, 0:1]
        for c in range(NCH):
            self.mm(pw[:, c:c + 1], gi[0:1, c * 128:(c + 1) * 128], one0, True, True, r=[("F", GI), "cst"], w=[("ps", pb)])
        self.mm(pw[:, 8:9], self.ones[0:1, 0:128], dl[0:1, :], True, True, r=[kn(self.dec), "cst"], w=[("ps", pb)])
        self.cp("dve", self.cols[:, :, 0], pw[:, 0:8], r=[], w=["cols"], x=[("ps", pb)])
        self.cp("dve", self.decrep[:, 0, 0:1], pw[:, 8:9], r=[], w=["decrep"], x=[("ps", pb)])
        ktok = self.bview(self.KTOK, 1, NCH, HD)
        rawx = self.b(self.XCB)
        for c in range(NCH):
            rv = self.nxt("vt", 4)
            vt = rawx[:, 512 + rv * 258:512 + rv * 258 + HD + 1]
            kvt = ("B", self.XCB, "vt", rv)
            self.act(vt, self.vtok[:, c, :], AF.Identity, r=[("vtok", c), "cols"], w=[kvt], scale=self.cols[:, c, 0:1])
            for dt in range(2):
                self.mm(self.ps[6 + dt][:, 0:HD + 1], ktok[:, c, dt * 128:(dt + 1) * 128], vt, c == 0, c == NCH - 1,
                        r=[("B", self.KTOK, c), kvt], w=[("ps", 6 + dt)])
        df, ndf = cfg["route"]
        rd = self.nxt("den", 4)
        dec = self.decrep[:, 0, 0:1]
        for dd, (msk, cmsk) in enumerate(((df, ndf), (ndf, df))):
            s1 = self.den[:, rd, dd:dd + 1]
            self.stt(s1, dec, msk, cmsk, ALU.mult, ALU.add, r=["decrep", "pdf"], w=[("den", rd, dd)])
            for dt in range(2):
                Cs = self.CST[:, dd, h, dt, :]
                self.ts("dve", Cs, Cs, s1, None, ALU.mult, None, r=[("CST", dd, h), ("den", rd, dd)], w=[("CST", dd, h)])
                self.stt(Cs, self.ps[6 + dt][:, 0:HD + 1], msk, Cs, ALU.mult, ALU.add, r=[("CST", dd, h), "pdf"], w=[("CST", dd, h)],
                         x=[("ps", 6 + dt)])
        self.trk.fence(*([("F", s) for s in range(10)] + [("B", self.XCB), ("B", self.GZ)]))
        self.mark("ml%d" % h)

    def mlstm(self, h, cfg):
        seqs, dirs, outs = cfg["seqs"], cfg["dirs"], cfg["outputs"]
        self.mlstm_tiles()
        F = self.f
        GI, GF, BC, TH, WNB = self.GI, self.GF, self.BC, self.TH, self.WNB
        self.trk.fence(*([("F", s) for s in range(8)] + [("B", self.XCB), ("B", self.GZ)]))
        gi, gf, bc, th = F(GI)[0:33], F(GF)[0:33], F(BC)[0:33], F(TH)[0:33]
        wnb = self.fview(WNB, 2, 2, TB)
        kn = lambda t: t.name
        self.act(gf, gf, AF.Ln, r=[("F", GF), "epsT"], w=[("F", GF)], scale=0.5, bias=self.epsT[0:33, 3:4])
        for c in range(NCH):
            sl = slice(c * 128, (c + 1) * 128)
            self.op("dve", lambda e, sl=sl: e.tensor_tensor_scan(out=bc[:, sl], data0=self.ones[0:33, 0:128], data1=gf[:, sl], initial=0.0,
                                                           op0=ALU.mult, op1=ALU.add), r=[("F", GF), "cst"], w=[("F", BC, c)])
        self.tt("dve", gi, gi, bc, ALU.subtract, r=[("F", GI), ("F", BC)], w=[("F", GI)])
        self.op("dve", lambda e: e.tensor_reduce(out=self.amax[:, :], in_=gi.rearrange("p (c l) -> p c l", l=128), axis=AX.X, op=ALU.max),
                r=[("F", GI)], w=[kn(self.amax)])
        btot = bc[:, 127:TB:128]
        m0 = cfg["m0"](h)
        for (t0, ln) in seqs:
            c0, c1 = t0 // 128, (t0 + ln) // 128
            ini, ik = (0.0, []) if m0 is None else m0
            self.op("dve", lambda e, c0=c0, c1=c1, ini=ini: e.tensor_tensor_scan(out=self.mn[:, c0:c1], data0=self.amax[:, c0:c1],
                                                                               data1=btot[:, c0:c1], initial=ini, op0=ALU.max, op1=ALU.add),
                    r=[kn(self.amax), ("F", BC)] + ik, w=[kn(self.mn)])
            if m0 is None:
                self.op("pool", lambda e, c0=c0: e.memset(self.mp[:, c0:c0 + 1], 0.0), w=[kn(self.mp)])
            else:
                self.cp("dve", self.mp[:, c0:c0 + 1], ini, r=ik, w=[kn(self.mp)])
            if c1 - c0 > 1:
                self.cp("dve", self.mp[:, c0 + 1:c1], self.mn[:, c0:c1 - 1], r=[kn(self.mn)], w=[kn(self.mp)])
        self.tt("dve", self.ngp[:, :], btot, self.mn[:, :], ALU.subtract, r=[("F", BC), kn(self.mn)], w=[kn(self.ngp)])
        self.tt("dve", self.dec[:, :], self.mp[:, :], self.ngp[:, :], ALU.add, r=[kn(self.mp), kn(self.ngp)], w=[kn(self.dec)])
        self.act(self.dec[:, :], self.dec[:, :], AF.Exp, r=[kn(self.dec)], w=[kn(self.dec)])
        for c in range(NCH):
            sl = slice(c * 128, (c + 1) * 128)
            self.act(th[:, sl], bc[:, sl], AF.Exp, r=[("F", BC, c), kn(self.ngp)], w=[("F", TH, c)], bias=self.ngp[:, c:c + 1], scale=-1.0)
            self.act(gi[:, sl], gi[:, sl], AF.Exp, r=[("F", GI), kn(self.ngp)], w=[("F", GI, c)], bias=self.ngp[:, c:c + 1])
        self.cp("dve", wnb[32:33, 0, :], gi[32:33, :][:, ::-1], r=[("F", GI)], w=[("F", WNB)])
        self.cp("dve", wnb[32:33, 1, :], th[32:33, :][:, ::-1], r=[("F", TH)], w=[("F", WNB + 1)])
        self.cp("dve", self.decb[32:33, :], self.dec[32:33, :][:, ::-1], r=[kn(self.dec)], w=[kn(self.decb)])
        cfg["m_final"](h, self.mn, [kn(self.mn)])
        pb = self.nxt("psA", 2)
        pw = self.ps[pb]
        one0, one32 = self.ones[0:1, 0:1], self.ones[32:33, 0:1]
        for c in range(NCH):
            sl = slice(c * 128, (c + 1) * 128)
            for j, (lhs, one, rk) in enumerate(((gi[0:1, sl], one0, ("F", GI)), (wnb[32:33, 0, sl], one32, ("F", WNB)),
                                                (th[0:1, sl], one0, ("F", TH)), (wnb[32:33, 1, sl], one32, ("F", WNB + 1)))):
                self.mm(pw[:, c * 4 + j:c * 4 + j + 1], lhs, one, True, True, r=[rk, "cst"], w=[("ps", pb)])
        self.mm(pw[:, 32:40], self.ones[0:1, 0:128], self.dec[0:1, :], True, True, r=[kn(self.dec), "cst"], w=[("ps", pb)])
        self.mm(pw[:, 40:48], self.ones[32:33, 0:128], self.decb[32:33, :], True, True, r=[kn(self.decb), "cst"], w=[("ps", pb)])
        self.cp("dve", self.cols[:, :, :], pw[:, 0:32].rearrange("p (c j) -> p c j", j=4), r=[], w=["cols"], x=[("ps", pb)])
        self.cp("dve", self.decrep[:, :, :], pw[:, 32:48].rearrange("p (d c) -> p d c", c=NCH), r=[], w=["decrep"], x=[("ps", pb)])
        self.dump("cols%d" % h, self.cols[:], [128, NCH, 4], r=["cols"])
        self.dump("decrep%d" % h, self.decrep[:], [128, 2, NCH], r=["decrep"])
        self.dump("mn%d" % h, self.mn[:], [33, NCH], r=[kn(self.mn)])
        self.mark("gates%d" % h)

        qT = self.bview(self.QT, 1, 2, TB)
        kT = self.bview(self.KT, 1, 2, TB)
        ktok = self.bview(self.KTOK, 1, NCH, HD)
        og = self.bview(self.OG, 1, NCH, HD)
        hmT = self.bview(self.HMT, 4, 8, TB)
        HFM = self.fview(self.HFM, 2, NCH, HD)
        rawx, rawg = self.b(self.XCB), self.b(self.GZ)
        hsn, hsn2 = F(self.HSN), F(self.HSN + 1)
        gng = cfg["gng"]
        PS_K0, PS_K1 = 6, 7
        arrived = {}
        pending = []

        def make_stages(si, d, c, cwset, first=False):
            sl = slice(c * 128, (c + 1) * 128)
            PS_N, PS_K0 = (2, 3) if d == 0 else (5, 6)
            PS_S = PS_N
            CWs = self.CW0 + 2 * cwset + d
            CW = self.f(CWs)[:, 0:514].rearrange("p (t e) -> p t e", e=HD + 1)
            wcol, thr, dcol = self.cols[:, c, d:d + 1], self.cols[:, c, 2 + d:3 + d], self.decrep[:, d, c:c + 1]
            rv = self.nxt("vt", 4)
            vt = rawx[:, 512 + rv * 258:512 + rv * 258 + HD + 1]
            kvt = ("B", self.XCB, "vt", rv)
            rc = self.nxt("cdec", 3)
            Cdec = rawg[:, rc * 516:(rc + 1) * 516].rearrange("p (t e) -> p t e", e=258)[:, :, 0:HD + 1]
            kcd = ("B", self.GZ, "cd", rc)
            rs = self.nxt("sm", 4)
            Sm = rawx[:, rs * 128:(rs + 1) * 128]
            ksm = ("B", self.XCB, "sm", rs)
            pn = self.ps[PS_N]
            rd = self.nxt("den", 4)
            dn = self.den[:, rd, :]

            def A1():
                self.act(vt, self.vtok[:, c, :], AF.Identity, r=[("vtok", c), "cols"], w=[kvt], scale=wcol)
                if not first:
                    self.act(Cdec, CW, AF.Identity, r=[("F", CWs), "decrep"], w=[kcd], scale=dcol)

            def P1():
                for dt in range(2):
                    self.mm(self.ps[PS_S][:, 384:512], kT[:, dt, sl], qT[:, dt, sl], dt == 0, dt == 1,
                            r=[("B", self.KT, dt), ("B", self.QT, dt)], w=[("ps", PS_S)])

            def D1():
                self.tt("dve", Sm, self.ps[PS_S][:, 384:512], self.maskb[:, d, :], ALU.mult, r=[("maskb", d)], w=[ksm], x=[("ps", PS_S)])

            def P2():
                for dt in range(2):
                    pk = PS_K0 + dt
                    self.mm(self.ps[pk][:, 0:HD + 1], ktok[:, c, dt * 128:(dt + 1) * 128], vt, True, True,
                            r=[("B", self.KTOK, c), kvt], w=[("ps", pk)])
                self.mm(pn[:, 0:HD + 1], Sm, vt, True, first, r=[ksm, kvt], w=[("ps", PS_N)])
                if not first:
                    for dt in range(2):
                        self.mm(pn[:, 0:HD + 1], qT[:, dt, sl], Cdec[:, dt, :], False, dt == 1, r=[("B", self.QT, dt), kcd], w=[("ps", PS_N)])

            def D2():
                for dt in range(2):
                    pk = PS_K0 + dt
                    self.stt(CW[:, dt, :], CW[:, dt, :], dcol, self.ps[pk][:, 0:HD + 1], ALU.mult, ALU.add,
                             r=[("F", CWs), "decrep"], w=[("F", CWs)], x=[("ps", pk)])

            def A2():
                self.act(dn[:, 0:1], pn[:, HD:HD + 1], AF.Abs, r=[], w=[("den", rd)], x=[("ps", PS_N)])

            def D3():
                self.ts("dve", dn[:, 0:1], dn[:, 0:1], thr, None, ALU.max, None, r=["cols", ("den", rd)], w=[("den", rd)])
                self.op("dve", lambda e: e.reciprocal(out=dn[:, 1:2], in_=dn[:, 0:1]), r=[("den", rd)], w=[("den", rd)])

            def A3():
                if c not in arrived:
                    arrived[c] = True
                    self.act(HFM[:, c, :], pn[:, 0:HD], AF.Identity, r=[("den", rd)], w=[("F", self.HFM + c // 4, c)], x=[("ps", PS_N)],
                             scale=dn[:, 1:2])
                    return
                r2 = self.nxt("hs", 4)
                hs = hsn[:, r2 * HD:(r2 + 1) * HD]
                hn = hsn2[:, (r2 % 2) * HD:(r2 % 2 + 1) * HD]
                hm = hsn2[:, 512 + (r2 % 2) * HD:512 + (r2 % 2 + 1) * HD]
                khs, khn, khm = ("F", self.HSN, "hs", r2), ("F", self.HSN + 1, "hn", r2 % 2), ("F", self.HSN + 1, "hm", r2 % 2)
                self.stt(hs, pn[:, 0:HD], dn[:, 1:2], HFM[:, c, :], ALU.mult, ALU.add, r=[("den", rd), ("F", self.HFM + c // 4, c)], w=[khs],
                         x=[("ps", PS_N)])

                def fin():
                    rstd, mean, sk = self.ln_stats(hs, HD, [khs], want_nmr=False)
                    self.ts("dve", hn, hs, mean, rstd, ALU.subtract, ALU.mult, r=[khs] + sk, w=[khn])
                    self.tt("dve", hm, hn, og[:, c, :], ALU.mult, r=[khn, ("B", self.OG, c)], w=[khm])
                    pbT = self.nxt("psA", 2)
                    for et in range(2):
                        self.op("pe", lambda e, et=et: e.transpose(out=self.ps[pbT][:, et * 128:(et + 1) * 128],
                                                                    in_=hm[:, et * 128:(et + 1) * 128], identity=self.ident),
                                r=[khm, "cst"], w=[("ps", pbT)])
                    self.cp("act", hmT[:, 2 * h:2 * h + 2, sl], self.ps[pbT][:, 0:256].rearrange("p (e t) -> p e t", t=128), r=[],
                            w=[("B", self.HMT + h, 2 * h, c), ("B", self.HMT + h, 2 * h + 1, c)], x=[("ps", pbT)])
                pending.append(fin)

            return [A1, P1, D1, P2, D2, A2, D3, A3]

        for si, (t0, ln) in enumerate(seqs):
            c0, c1 = t0 // 128, (t0 + ln) // 128
            cwset = self.nxt("cwset", 2)
            for d in dirs:
                cfg["cw_init"](h, d, self.CW0 + 2 * cwset + d)
            for s in range(c1 - c0):
                prev = pending[:]
                del pending[:]
                stg = [make_stages(si, d, c0 + s if d == 0 else c1 - 1 - s, cwset, first=(s == 0 and cfg.get("zero_state", False)))
                       for d in dirs]
                for k in range(8):
                    for st in stg:
                        st[k]()
                    if k == 4:
                        for fn in prev:
                            fn()
            for d in dirs:
                cfg["cw_final"](h, d, si, self.CW0 + 2 * cwset + d)
        for fn in pending:
            fn()
        self.dump("hm%d" % h, hmT[:, 2 * h:2 * h + 2, :], [128, 2, TB], r=[("B", self.HMT + h)])
        self.trk.fence(*([("F", s) for s in range(10)] + [("B", self.XCB), ("B", self.GZ)]))
        self.mark("ml%d" % h)

    def block_tiles(self):
        if hasattr(self, "gng"):
            return
        T = self.tile
        self.gng = T("gng", [128, D], F32)
        self.dma("sp", self.gng[:], self.I["rows6"][5:6, :].broadcast_to([128, D]), w=["gng"])
        self.gngh = self.gng
        self.ts("dve", self.gng[:], self.gng[:], 0.5, None, ALU.mult, None, r=["gng"], w=["gng", "gngh"])
        self.RGO = T("RGO", [128, 4, 2, 8], F32)
        self.NFIN = T("NFIN", [128, 4, 2, NH, 2], F32)
        self.MFIN = T("MFIN", [33, NH, 4], F32)
        self.op("pool", lambda e: e.memset(self.MFIN[:], 0.0), w=["MFIN"])
        self.RGW = T("RGW", [128, 2, 8], F32)
        self.RGI = T("RGI", [128, 8], F32)
        self.RGT = T("RGT", [128, 8], F32)
        self.MW = T("MW", [33, NH], F32)
        self.MI = T("MI", [33, NH], F32)
        self.MT = T("MT", [33, NH], F32)
        for t in (self.MW, self.MI, self.MT):
            self.op("pool", lambda e, t=t: e.memset(t[:], 0.0), w=[t.name[2:]])

    def cfg_prompt(self):
        self.block_tiles()

        def rg_final(d, ct, si, ap, keys):
            self.cp("pool", self.RGO[:, si, d, ct:ct + 1], ap, r=keys, w=[("RGO", si, d, ct)])

        def cw_init(h, d, slot):
            self.op("pool", lambda e: e.memset(self.f(slot)[:, 0:514], 0.0), w=[("F", slot)])

        def cw_final(h, d, si, slot):
            CW = self.f(slot)[:, 0:514].rearrange("p (t e) -> p t e", e=HD + 1)
            self.dma("sp", self.O["o_C"][si, d, h].rearrange("(t p) e -> p t e", p=128), CW[:, :, 0:HD], r=[("F", slot)], w=[("o_C", si, d, h)])
            self.cp("pool", self.NFIN[:, si, d, h, :], CW[:, :, HD:HD + 1].rearrange("p t o -> p (t o)"), r=[("F", slot)], w=[("NFIN", si, d, h)])

        def m_final(h, mn, keys):
            for si in range(4):
                self.cp("pool", self.MFIN[0:1, h, si:si + 1], mn[0:1, 2 * si + 1:2 * si + 2], r=keys, w=[("MFIN", h, si, 0)])
                self.cp("pool", self.MFIN[32:33, h, si:si + 1], mn[32:33, 2 * (3 - si) + 1:2 * (3 - si) + 2], r=keys, w=[("MFIN", h, si, 1)])

        return dict(seqs=[(256 * s, 256) for s in range(4)], conv_seg=256, dirs=(0, 1), outputs=True, cond=0, zero_state=True,
                    rg_init=lambda d, ct: None, rg_final=rg_final, m0=lambda h: None, m_final=m_final,
                    cw_init=cw_init, cw_final=cw_final, gng=self.gng, gng_key="gng")

    def cfg_sample(self):
        self.block_tiles()

        def cw_init(h, d, slot):
            CW = self.f(slot)[:, 0:514].rearrange("p (t e) -> p t e", e=HD + 1)
            self.cp("pool", CW, self.CST[:, d, h, :, :], r=[("CST", d, h)], w=[("F", slot)])

        return dict(seqs=[(0, TB)], conv_seg=64, dirs=(0, 1), outputs=True, cond=1,
                    rg_init=lambda d, ct: (self.RGH[:, d, ct:ct + 1], [("RGH", d)]), rg_final=lambda *a: None,
                    m0=lambda h: (self.MST[:, h:h + 1], [("MST", h)]), m_final=lambda *a: None,
                    cw_init=cw_init, cw_final=lambda *a: None, gng=self.gng, gng_key="gng")

    def sweep_pass(self, p):
        I = self.I
        self.block_tiles()
        df, ndf = self.pdf[:, p, 0:1], self.pdf[:, p, 1:2]
        for ax, key in enumerate(("prg_wa", "prg_wx")):
            for e2 in range(2):
                src = I[key][p, e2::2, :, :].rearrange("c k j -> k c j")
                self.dma("pool", self.WRGp[e2 * 64:(e2 + 1) * 64, ax, :, e2 * 64:(e2 + 1) * 64], src, w=["WRGp"])
        self.tt("pool", self.RGI[:, :], self.RGH[:, 0, :], self.RGH[:, 1, :], ALU.subtract, r=["RGH"], w=["RGI"])
        self.stt(self.RGI[:, :], self.RGI[:, :], df, self.RGH[:, 1, :], ALU.mult, ALU.add, r=["RGI", "RGH", "pdf"], w=["RGI"])
        self.tt("pool", self.MI[0:1, :], self.MS2[0:1, 0, :], self.MS2[0:1, 1, :], ALU.subtract, r=["MS2"], w=["MI"])
        self.stt(self.MI[0:1, :], self.MI[0:1, :], self.pdf[0:1, p, 0:1], self.MS2[0:1, 1, :], ALU.mult, ALU.add, r=["MI", "MS2", "pdf"], w=["MI"])

        def rg_final(d, ct, si, ap, keys):
            self.cp("pool", self.RGW[:, 0, ct:ct + 1], ap, r=keys, w=[("RGW", 0, ct)])

        def cw_init(h, d, slot):
            CW = self.f(slot)[:, 0:514].rearrange("p (t e) -> p t e", e=HD + 1)
            self.tt("pool", CW, self.CST[:, 0, h, :, :], self.CST[:, 1, h, :, :], ALU.subtract, r=[("CST", 0, h), ("CST", 1, h)], w=[("F", slot)])
            self.stt(CW, CW, df, self.CST[:, 1, h, :, :], ALU.mult, ALU.add, r=[("F", slot), ("CST", 1, h), "pdf"], w=[("F", slot)])

        def cw_final(h, d, si, slot):
            CW = self.f(slot)[:, 0:514].rearrange("p (t e) -> p t e", e=HD + 1)
            Tt = self.f(slot + 1)[:, 0:514].rearrange("p (t e) -> p t e", e=HD + 1)
            for dd, msk in ((0, df), (1, ndf)):
                self.tt("pool", Tt, CW, self.CST[:, dd, h, :, :], ALU.subtract, r=[("F", slot), ("CST", dd, h)], w=[("F", slot + 1)])
                self.stt(self.CST[:, dd, h, :, :], Tt, msk, self.CST[:, dd, h, :, :], ALU.mult, ALU.add,
                         r=[("F", slot + 1), ("CST", dd, h), "pdf"], w=[("CST", dd, h)])

        def m_final(h, mn, keys, col=NCH - 1):
            self.cp("pool", self.MW[0:1, h:h + 1], mn[0:1, col:col + 1], r=keys, w=[("MW", h)])

        cfg = dict(seqs=[(0, TB)], conv_seg=64, dirs=(0,), outputs=False, cond=1, params=self.pass_params(p), route=(df, ndf),
                   rg_init=lambda d, ct: (self.RGI[:, ct:ct + 1], ["RGI"]), rg_final=rg_final,
                   m0=lambda h: (self.MI[:, h:h + 1], ["MI"]), m_final=m_final,
                   cw_init=cw_init, cw_final=cw_final, gng=self.gng, gng_key="gng")
        self.wplan([it for h in range(NH) for it in self.unit_witems(h, False)])
        self.wload_next()
        self.wload_next()
        self.build_uT(self.I["xoth"], p * TB, 1, self.UT, 0)
        for h in range(NH):
            self.unit(h, cfg)
        for dd, msk, mrow in ((0, df, self.pdf[0:1, p, 0:1]), (1, ndf, self.pdf[0:1, p, 1:2])):
            self.tt("pool", self.RGT[:, :], self.RGW[:, 0, :], self.RGH[:, dd, :], ALU.subtract, r=[("RGW", 0), ("RGH", dd)], w=["RGT"])
            self.stt(self.RGH[:, dd, :], self.RGT[:, :], msk, self.RGH[:, dd, :], ALU.mult, ALU.add, r=["RGT", ("RGH", dd), "pdf"],
                     w=[("RGH", dd)])
            self.tt("pool", self.MT[0:1, :], self.MW[0:1, :], self.MS2[0:1, dd, :], ALU.subtract, r=["MW", "MS2"], w=["MT"])
            self.stt(self.MS2[0:1, dd, :], self.MT[0:1, :], mrow, self.MS2[0:1, dd, :], ALU.mult, ALU.add, r=["MT", "MS2", "pdf"], w=["MS2"])

    def sweeps_done(self):
        self.cp("pool", self.MST[0:1, :], self.MS2[0:1, 0, :], r=["MS2"], w=["MST"])
        self.dma("sp", self.MST[32:33, :], self.MS2[0:1, 1, :], r=["MS2"], w=["MST"])

    def post_mixer(self, cfg, xsrc):
        I = self.I
        cond = cfg["cond"]
        self.trk.fence(*([("F", s) for s in range(NFS)] + [("B", s) for s in range(12, NBS)]))
        uT = self.bview(self.UT, 4, 8, TB)
        hrgT = self.bview(self.HRG, 4, 8, TB)
        hmT = self.bview(self.HMT, 4, 8, TB)
        self.dma("sp", self.f(0), I["rows6"][0:1, :].broadcast_to([128, D]), w=[("F", 0)])
        self.dma("sp", self.f(1), I["rows6"][1:2, :].broadcast_to([128, D]), w=[("F", 1)])
        mergedT = self.bview(12, 4, 8, TB)
        gsig, tmp = self.f(4), self.f(5)

        def issue_slices(ft):
            ws = 16 + 2 * (ft % 3)
            Wsl = self.bview(ws, 2, 32, 128)
            for kind, (src, col0) in enumerate((("w_rg_proj", ft * 128), ("w_ml_proj", ft * 128),
                                                ("w_in", OFF_GM + ft * 128), ("w_in", OFF_GM + D + ft * 128))):
                self.load_w(Wsl[:, kind * 8:(kind + 1) * 8, :], col0, [("B", ws + kind // 2, kind)], src=src, ncol=128)

        issue_slices(0)
        issue_slices(1)
        acts = (hrgT, hmT, uT, uT)
        akey = (self.HRG, self.HMT, self.UT, self.UT)
        WoutA, WoutB = self.bview(20, 2, 4, D), self.bview(16, 2, 4, D)
        for ft in range(8):
            if ft + 2 < 8:
                issue_slices(ft + 2)
            if ft == 6:
                self.dma("pool", WoutA, I["w_out"][0:512, :].rearrange("(k p) c -> p k c", p=128), w=[("B", 20), ("B", 21)])
            if ft == 7:
                self.dma("pool", WoutB, I["w_out"][512:1024, :].rearrange("(k p) c -> p k c", p=128), w=[("B", 16), ("B", 17)])
            ws = 16 + 2 * (ft % 3)
            Wsl = self.bview(ws, 2, 32, 128)
            for tb2 in range(2):
                tsl = slice(tb2 * 512, (tb2 + 1) * 512)
                pbase = 4 * ((ft * 2 + tb2) % 2)
                for kind in range(4):
                    for k in range(8):
                        self.mm(self.ps[pbase + kind][:, :], Wsl[:, kind * 8 + k, :], acts[kind][:, k, tsl], k == 0, k == 7,
                                r=[("B", ws + kind // 2, kind), ("B", akey[kind] + k // 2, k)], w=[("ps", pbase + kind)])
                for j in range(2):
                    self.act(gsig[:, j * 512:(j + 1) * 512], self.ps[pbase + 2 + j][:, :], AF.Sigmoid, r=["bmerge"], w=[("F", 4, j)], x=[("ps", pbase + 2 + j)],
                             bias=self.bmerge[:, 8 * j + ft:8 * j + ft + 1])
                for j in range(2):
                    self.tt("dve", tmp[:, j * 512:(j + 1) * 512], self.ps[pbase + j][:, :], gsig[:, j * 512:(j + 1) * 512], ALU.mult,
                            r=[("F", 4, j)], w=[("F", 5, j)], x=[("ps", pbase + j)])
                self.tt("pool", mergedT[:, ft, tsl], tmp[:, 0:512], tmp[:, 512:1024], ALU.add, r=[("F", 5)], w=[("B", 12 + ft // 2, ft, tb2)])
        for g0 in range(0, NCH, 2):
            grp = (g0, g0 + 1)
            st = {}
            for c in grp:
                xs = 2 + self.nxt("xin2", 2)
                self.dma("sp", self.f(xs), xsrc[c * 128:(c + 1) * 128, :], w=[("F", xs)])
                st[c] = dict(xs=xs, x1c=self.f(6 + c), kx1=("F", 6 + c), pbs=[4 + 2 * (c % 2), 5 + 2 * (c % 2)])
            for c in grp:
                for nh in range(2):
                    pb = st[c]["pbs"][nh]
                    nsl = slice(nh * 512, (nh + 1) * 512)
                    for k in range(8):
                        Wk = WoutA[:, k, nsl] if k < 4 else WoutB[:, k - 4, nsl]
                        kw = ("B", 20 + k // 2) if k < 4 else ("B", 16 + (k - 4) // 2)
                        self.mm(self.ps[pb][:, :], mergedT[:, k, c * 128:(c + 1) * 128], Wk, k == 0, k == 7,
                                r=[("B", 12 + k // 2, k), kw], w=[("ps", pb)])
            for c in grp:
                for nh in range(2):
                    pb = st[c]["pbs"][nh]
                    nsl = slice(nh * 512, (nh + 1) * 512)
                    self.tt("dve", st[c]["x1c"][:, nsl], self.ps[pb][:, :], self.G[:, 0, nsl], ALU.mult, r=["G"], w=[st[c]["kx1"] + (nh,)],
                            x=[("ps", pb)])
            for c in grp:
                self.stt(st[c]["x1c"], self.f(st[c]["xs"]), ALPHA, st[c]["x1c"], ALU.mult, ALU.add, r=[("F", st[c]["xs"]), st[c]["kx1"]],
                         w=[st[c]["kx1"]])
            self.ln_affine_group([(st[c]["x1c"], st[c]["kx1"]) for c in grp], self.f(0), ("F", 0), self.f(1), ("F", 1))
        self.dump("x1", self.fview(6, 8, 8, D), [128, 8, D], r=[("F", 6 + c) for c in range(8)])
        self.trk.fence(*[("B", s) for s in range(0, 4)])
        self.build_uT(None, 0, cond, self.UT, 24, x_keep=lambda c: (self.f(6 + c), [("F", 6 + c)]))

    def mlp(self, cfg, ydst):
        I = self.I
        self.trk.fence(*([("F", s) for s in range(6)] + [("B", s) for s in range(4, NBS)]))
        u2T = self.bview(self.UT, 4, 8, TB)
        for j, row in enumerate((2, 3, 4)):
            self.dma("sp", self.f(j), I["rows6"][row:row + 1, :].broadcast_to([128, D]), w=[("F", j)])
        hidT = self.bview(4, 8, 16, TB)

        def issue_fc(i):
            ws = 12 + 2 * (i % 2)
            self.load_w(self.bview(ws, 2, 8, 512), i * 512, [("B", ws), ("B", ws + 1)], src="w_fc", ncol=512)

        def issue_pj(i):
            wp = 16 + i % 6
            ffh, nh, kg4 = i // 8, (i % 8) // 4, i % 4
            r0 = ffh * 2048 + kg4 * 512
            self.dma("pool", self.bview(wp, 1, 4, 512),
                     I["w_proj"][r0:r0 + 512, nh * 512:(nh + 1) * 512].rearrange("(k p) c -> p k c", p=128), w=[("B", wp)])

        issue_fc(0)
        for ffh in range(2):
            for g in range(4):
                i = ffh * 4 + g
                if i + 1 < 8:
                    issue_fc(i + 1)
                issue_pj(ffh * 8 + g)
                ws = 12 + 2 * (i % 2)
                Wfc = self.bview(ws, 2, 8, 512)
                for f4 in range(4):
                    fl = g * 4 + f4
                    fft = ffh * 16 + fl
                    for tb2 in range(2):
                        tsl = slice(tb2 * 512, (tb2 + 1) * 512)
                        pb = self.nxt("psH", 4)
                        for k in range(8):
                            self.mm(self.ps[pb][:, :], Wfc[:, k, f4 * 128:(f4 + 1) * 128], u2T[:, k, tsl], k == 0, k == 7,
                                    r=[("B", ws), ("B", ws + 1), ("B", self.UT + k // 2, k)], w=[("ps", pb)])
                        rt = self.nxt("relu", 4)
                        rl = self.f(3 + rt // 2)[:, (rt % 2) * 512:(rt % 2 + 1) * 512]
                        krl = ("F", 3 + rt // 2, rt % 2)
                        kh = ("B", 4 + fl // 2, fl, tb2)
                        if (fl + tb2) % 2 == 0:
                            self.act(rl, self.ps[pb][:, :], AF.Relu, r=["bfc"], w=[krl], x=[("ps", pb)], bias=self.bfc[:, fft:fft + 1])
                            self.tt("dve", hidT[:, fl, tsl], rl, rl, ALU.mult, r=[krl], w=[kh])
                        else:
                            self.ts("dve", rl, self.ps[pb][:, :], self.bfc[:, fft:fft + 1], 0.0, ALU.add, ALU.max, r=["bfc"], w=[krl],
                                    x=[("ps", pb)])
                            self.act(hidT[:, fl, tsl], rl, AF.Square, r=[krl], w=[kh])
            for nh in range(2):
                nsl = slice(nh * 512, (nh + 1) * 512)
                for kg4 in range(4):
                    i = ffh * 8 + nh * 4 + kg4
                    if i + 4 < (ffh + 1) * 8:
                        issue_pj(i + 4)
                    wp = 16 + i % 6
                    Wp = self.bview(wp, 1, 4, 512)
                    for c in range(NCH):
                        for k4 in range(4):
                            fl = kg4 * 4 + k4
                            self.mm(self.ps[c][:, :], hidT[:, fl, c * 128:(c + 1) * 128], Wp[:, k4, :], fl == 0, fl == 15,
                                    r=[("B", 4 + fl // 2, fl), ("B", wp)], w=[("ps", c)])
                for c in range(NCH):
                    x1c, kx1 = self.f(6 + c), ("F", 6 + c)
                    rt = self.nxt("relu", 4)
                    t = self.f(3 + rt // 2)[:, (rt % 2) * 512:(rt % 2 + 1) * 512]
                    kt = ("F", 3 + rt // 2, rt % 2)
                    if ffh == 0:
                        self.tt("dve", t, self.ps[c][:, :], self.f(2)[:, nsl], ALU.add, r=[("F", 2)], w=[kt], x=[("ps", c)])
                        self.tt("dve", t, t, self.G[:, 1, nsl], ALU.mult, r=[kt, "G"], w=[kt])
                        self.stt(x1c[:, nsl], x1c[:, nsl], ALPHA, t, ALU.mult, ALU.add, r=[kx1, kt], w=[kx1 + (nh,)])
                    else:
                        self.tt("dve", t, self.ps[c][:, :], self.G[:, 1, nsl], ALU.mult, r=["G"], w=[kt], x=[("ps", c)])
                        self.tt("dve", x1c[:, nsl], x1c[:, nsl], t, ALU.add, r=[kx1, kt], w=[kx1 + (nh,)])
        for c0 in range(0, NCH, 2):
            cs = [c0, c0 + 1]
            self.ln_affine_group([(self.f(6 + c), ("F", 6 + c)) for c in cs], self.f(0), ("F", 0), self.f(1), ("F", 1))
            for c in cs:
                self.dma("sp", ydst[c * 128:(c + 1) * 128, :], self.f(6 + c), r=[("F", 6 + c)], w=[("ydst", id(ydst), c)])
        self.trk.fence(*([("F", s) for s in range(NFS)] + [("B", s) for s in range(NBS)]))

    def full_block(self, cfg, xsrc, ydst):
        self.wplan([it for h in range(NH) for it in self.unit_witems(h, True)])
        self.wload_next()
        self.wload_next()
        self.build_uT(xsrc, 0, cfg["cond"], self.UT, 0)
        self.mark("uT")
        for h in range(NH):
            self.unit(h, cfg)
        self.mark("mixer")
        self.post_mixer(cfg, xsrc)
        self.mark("post")
        self.mlp(cfg, ydst)

    def build(self):
        try:
            self.setup()
            self.gates(0)
            self.mark("setup")
            cfgp = self.cfg_prompt()
            self.full_block(cfgp, self.I["xp"], self.O["yp"])
            self.dma("sp", self.O["o_h"][:, :, :, :], self.RGO[:], r=["RGO"], w=["o_h"])
            self.dma("sp", self.O["o_n"][:, :, :, :, :], self.NFIN[:], r=["NFIN"], w=["o_n"])
            self.dma("sp", self.O["o_m"][:, :, :], self.MFIN[:], r=["MFIN"], w=["o_m"])
            self.mark("prompt")
            self.gates(1)
            for p in range(3):
                self.sweep_pass(p)
            self.sweeps_done()
            self.mark("sweeps")
            cfgs = self.cfg_sample()
            self.full_block(cfgs, self.I["xseg"], self.O["ys"])
        except StopBuild as e:
            print("build stopped at", e)
        self.trk.finish()
        return self.nc


def assemble(results):
    yp = np.zeros((32, 256, D), np.float32)
    ys = np.zeros((2, 4 * TB, D), np.float32)
    nh_ = np.zeros((32, 1, 2, D), np.float32)
    nC = np.zeros((32, 1, 2, NH, HD, HD), np.float32)
    nn = np.zeros((32, 1, 2, NH, HD), np.float32)
    nm = np.zeros((32, 1, 2, NH), np.float32)
    for i, r in enumerate(results):
        b, seg = i // 4, i % 4
        yp[4 * i:4 * i + 4] = np.asarray(r["yp"]).reshape(4, 256, D)
        ys[b, seg * TB:(seg + 1) * TB] = np.asarray(r["ys"])
        nh_[4 * i:4 * i + 4, 0] = np.asarray(r["o_h"]).transpose(1, 2, 3, 0).reshape(4, 2, D)
        nC[4 * i:4 * i + 4, 0] = np.asarray(r["o_C"])
        nn[4 * i:4 * i + 4, 0] = np.asarray(r["o_n"]).transpose(1, 2, 3, 4, 0).reshape(4, 2, NH, HD)
        om = np.asarray(r["o_m"])
        nm[4 * i:4 * i + 4, 0, 0] = om[0].T
        nm[4 * i:4 * i + 4, 0, 1] = om[32].T
    return yp, ys, nh_, nC, nn, nm


def kernel(**inputs):
    inp = {k: np.asarray(v) for k, v in inputs.items()}
    prog = Prog()
    nc = prog.build()
    in_maps = [prep_core_inputs(i, inp) for i in range(NCORES)]
    res = run_bass_kernel_spmd(nc, in_maps, core_ids=list(range(NCORES)))
    return assemble(res.results)
```

```python
import os
import numpy as np
from contextlib import ExitStack
import concourse.bass as bass
import concourse.mybir as mybir
from concourse.bass_utils import run_bass_kernel_spmd

F32 = mybir.dt.float32
BF16 = mybir.dt.bfloat16
AF = mybir.ActivationFunctionType
ALU = mybir.AluOpType
AX = mybir.AxisListType

ATTACH_WAIT = True
D = 1024
NCORES = 8
ALPHA = 2.0 ** 0.25
LN_EPS = 1e-5
HD = 256
NH = 4
TB = 1024
NCH = TB // 128
OFF_XR, OFF_ZR, OFF_Q, OFF_K, OFF_V, OFF_O, OFF_GM, OFF_IG, OFF_FG = 0, 1024, 2048, 3072, 4096, 5120, 6144, 8192, 8200
D_IN = 8208


class Tok:
    __slots__ = ("sem", "val", "eng", "snap")

    def __init__(self, sem, val, eng, snap=None):
        self.sem, self.val, self.eng, self.snap = sem, val, eng, snap


class KState:
    __slots__ = ("w", "r")

    def __init__(self):
        self.w = None
        self.r = {}


class Trk:
    NSLOT = 8

    def __init__(self, nc, es):
        self.nc = nc
        self.eng = {"pe": nc.tensor, "act": nc.scalar, "dve": nc.vector, "pool": nc.gpsimd, "sp": nc.sync}
        self.sem = {e: es.enter_context(nc.semaphore("s_" + e)) for e in self.eng}
        self.cnt = {e: 0 for e in self.eng}
        self.seen = {e: {} for e in self.eng}
        self.keys = {}
        self.dq = ("sp", "pool", "act")
        self.dsem = {q: [es.enter_context(nc.semaphore("d_%s%d" % (q, i))) for i in range(self.NSLOT)] for q in self.dq}
        self.dcnt = {q: 0 for q in self.dq}
        self.dtok = {q: [None] * self.NSLOT for q in self.dq}
        self.nwait = 0
        self.nops = 0

    def _overlaps(self, key):
        root = self.keys.get(key[0])
        if not root:
            return []
        out = []
        n = len(key)
        for k2, st in root.items():
            m = min(n, len(k2))
            if k2[:m] == key[:m]:
                out.append((k2, st))
        return out

    def _state(self, key):
        root = self.keys.setdefault(key[0], {})
        st = root.get(key)
        if st is None:
            st = KState()
            root[key] = st
        return st

    def _deps(self, r, w):
        deps = []
        for k in r:
            for _, st in self._overlaps(k):
                if st.w is not None:
                    deps.append((st.w, True))
        for k in w:
            for _, st in self._overlaps(k):
                if st.w is not None:
                    deps.append((st.w, False))
                for t in st.r.values():
                    deps.append((t, False))
        return deps

    def _waits(self, eng, deps, skip_same=True, attach=False):
        need = {}
        for t, raw in deps:
            if skip_same and t.eng == eng and not raw and eng == "pe":
                continue
            cur = need.get(id(t.sem))
            if cur is None or cur.val < t.val:
                need[id(t.sem)] = t
        todo = []
        seen = self.seen[eng]
        for sid, t in sorted(need.items(), key=lambda kv: -kv[1].val):
            if seen.get(sid, 0) >= t.val:
                continue
            seen[sid] = t.val
            todo.append(t)
            self.nwait += 1
            if t.snap:
                for k2, v2 in t.snap.items():
                    if seen.get(k2, 0) < v2:
                        seen[k2] = v2
        keep = todo[-1:] if (attach and ATTACH_WAIT) else []
        for t in todo[:len(todo) - len(keep)]:
            self.eng[eng].wait_ge(t.sem, t.val)
        return keep

    def _update(self, tok, r, w):
        for k in w:
            root = self.keys.setdefault(k[0], {})
            n = len(k)
            for k2 in [k2 for k2 in root if len(k2) > n and k2[:n] == k]:
                del root[k2]
            st = self._state(k)
            st.w = tok
            st.r = {}
        for k in r:
            st = self._state(k)
            st.r[id(tok.sem)] = tok

    def op(self, eng, fn, r=(), w=(), x=()):
        r = [k if isinstance(k, tuple) else (k,) for k in r]
        w = [k if isinstance(k, tuple) else (k,) for k in w]
        deps = self._deps(r, w)
        for k in x:
            k = k if isinstance(k, tuple) else (k,)
            for _, st in self._overlaps(k):
                if st.w is not None:
                    deps.append((st.w, True))
                for t in st.r.values():
                    if t.eng != eng:
                        deps.append((t, True))
            r.append(k)
        keep = self._waits(eng, deps, attach=True)
        self.cnt[eng] += 1
        tok = Tok(self.sem[eng], self.cnt[eng], eng, dict(self.seen[eng]))
        inst = fn(self.eng[eng])
        for t in keep:
            inst._wait_ge(t.sem, t.val)
        inst.then_inc(self.sem[eng], 1)
        self._update(tok, r, w)
        self.nops += 1
        return tok

    def dma(self, q, out, in_, r=(), w=(), **kw):
        r = [k if isinstance(k, tuple) else (k,) for k in r]
        w = [k if isinstance(k, tuple) else (k,) for k in w]
        slot = self.dcnt[q] % self.NSLOT
        deps = self._deps(r, w)
        if self.dtok[q][slot] is not None:
            deps.append((self.dtok[q][slot], True))
        keep = self._waits(q, deps, skip_same=False, attach=True)
        val = 16 * (self.dcnt[q] // self.NSLOT + 1)
        self.dcnt[q] += 1
        tok = Tok(self.dsem[q][slot], val, None, dict(self.seen[q]))
        inst = self.eng[q].dma_start(out=out, in_=in_, **kw)
        for t in keep:
            inst._wait_ge(t.sem, t.val)
        inst.then_inc(self.dsem[q][slot], 16)
        self.dtok[q][slot] = tok
        self._update(tok, r, w)
        return tok

    def fence(self, *keys):
        for key in keys:
            key = key if isinstance(key, tuple) else (key,)
            root = self.keys.get(key[0])
            if not root:
                continue
            n = len(key)
            toks = {}
            for k2 in [k2 for k2 in root if len(k2) >= n and k2[:n] == key]:
                st = root.pop(k2)
                for t in ([st.w] if st.w is not None else []) + list(st.r.values()):
                    cur = toks.get(id(t.sem))
                    if cur is None or cur.val < t.val:
                        toks[id(t.sem)] = t
            st = KState()
            st.r = toks
            root[key] = st

    def finish(self):
        deps = []
        for q in self.dq:
            for t in self.dtok[q]:
                if t is not None:
                    deps.append((t, True))
        for e in self.eng:
            if self.cnt[e]:
                deps.append((Tok(self.sem[e], self.cnt[e], e), True))
        self._waits("sp", deps)


def _fm(v, nt):
    return np.ascontiguousarray(np.asarray(v, np.float32).reshape(nt, 128).T)


def make_consts():
    c = np.zeros((128, 128 * 4 + 2), np.float32)
    c[:, 0:128] = np.eye(128, dtype=np.float32)
    j = np.arange(128)[:, None]
    i = np.arange(128)[None, :]
    c[:, 128:256] = (j <= i)
    c[:, 256:384] = (j >= i)
    c[:, 384:512] = 1.0
    c[0, 512] = 1.0
    c[32, 513] = 1.0
    rm = np.ones((33, TB), np.float32)
    rm[:, ::128] = 0.0
    return c, rm


def prep_core_inputs(i, inp):
    b, seg = i // 4, i % 4
    f = lambda a: np.ascontiguousarray(np.asarray(a, np.float32))
    m = {}
    m["xp"] = f(inp["x_prompt"][4 * i:4 * i + 4].reshape(4 * 256, D))
    xs = np.asarray(inp["x_sample"][b], np.float32)
    passes = {0: [(3, 1), (2, 1), (1, 1)], 1: [(0, 0), (3, 1), (2, 1)], 2: [(0, 0), (1, 0), (3, 1)], 3: [(0, 0), (1, 0), (2, 0)]}[seg]
    m["xoth"] = f(np.concatenate([xs[s * TB:(s + 1) * TB][::-1] if d else xs[s * TB:(s + 1) * TB] for s, d in passes], 0))
    m["xseg"] = f(xs[seg * TB:(seg + 1) * TB])
    w_in0 = np.asarray(inp["w_in"][0], np.float32)
    pwg = np.zeros((128, 3, 8, 2, NH), np.float32)
    pgb = np.zeros((33, 3, 2, NH), np.float32)
    prgb = np.zeros((128, 3, 3, 8), np.float32)
    pconv = np.zeros((128, 3, 8, 5), np.float32)
    pdf = np.zeros((128, 3, 2), np.float32)
    cw4 = [_fm(inp["rg_conv_w"][0][k], 8) for k in range(4)]
    for p, (sg, d) in enumerate(passes):
        for g, off in enumerate((OFF_IG, OFF_FG)):
            for h in range(NH):
                pwg[:, p, :, g, h] = _fm(w_in0[:, off + d * NH + h], 8)
        pgb[0, p, 0] = inp["ml_b_igate"][0][d]
        pgb[0, p, 1] = inp["ml_b_fgate"][0][d]
        prgb[:, p, 0] = _fm(inp["rg_ba"][0][d], 8)
        prgb[:, p, 1] = _fm(inp["rg_bx"][0][d], 8)
        prgb[:, p, 2] = _fm(inp["rg_lambda"][0][d], 8)
        taps = [cw4[0], cw4[1], cw4[2], cw4[3], None] if d == 0 else [None, cw4[3], cw4[2], cw4[1], cw4[0]]
        for o, t in enumerate(taps):
            if t is not None:
                pconv[:, p, :, o] = t
        pdf[:, p, 0] = 1.0 if d == 0 else 0.0
        pdf[:, p, 1] = 0.0 if d == 0 else 1.0
    m["pwg"], m["pgb"], m["prgb"], m["pconv"], m["pdf"] = pwg, pgb, prgb, pconv, pdf
    m["prg_wa"] = f(np.stack([np.asarray(inp["rg_wa"][0][d]) for _, d in passes], 0))
    m["prg_wx"] = f(np.stack([np.asarray(inp["rg_wx"][0][d]) for _, d in passes], 0))
    cv = np.zeros((128, 8, 2), np.float32)
    cv[:, :, 0] = _fm(inp["c_ctx"], 8)
    cv[:, :, 1] = _fm(inp["c"][b], 8)
    m["cvec"] = cv
    m["w_ada"] = f(inp["w_ada"][0])
    bfm = _fm(inp["b_ada"][0], 48)
    m["b_ada_fm"] = f(np.stack([bfm, bfm], -1))
    m["b_ada"] = f(inp["b_ada"])
    m["w_in"] = f(inp["w_in"][0])
    m["w_gates"] = f(np.asarray(inp["w_in"][0])[:, OFF_IG:OFF_IG + 16].reshape(8, 128, 16).transpose(1, 0, 2))
    m["conv_w_fm"] = f(np.stack([_fm(inp["rg_conv_w"][0][k], 8) for k in range(4)], -1))
    m["conv_b_fm"] = _fm(inp["rg_conv_b"][0], 8)
    m["rg_wa"] = f(inp["rg_wa"][0])
    m["rg_wx"] = f(inp["rg_wx"][0])
    rgb = np.zeros((128, 3, 2, 8), np.float32)
    for d in range(2):
        rgb[:, 0, d] = _fm(inp["rg_ba"][0][d], 8)
        rgb[:, 1, d] = _fm(inp["rg_bx"][0][d], 8)
        rgb[:, 2, d] = _fm(inp["rg_lambda"][0][d], 8)
    m["rgb_fm"] = rgb
    m["w_rg_proj"] = f(inp["w_rg_proj"][0])
    m["w_ml_proj"] = f(inp["w_ml_proj"][0])
    m["w_out"] = f(inp["w_out"][0])
    m["w_fc"] = f(inp["w_fc"][0])
    m["w_proj"] = f(inp["w_proj"][0])
    gb = np.zeros((33, 2, NH), np.float32)
    gb[0, 0] = inp["ml_b_igate"][0][0]
    gb[32, 0] = inp["ml_b_igate"][0][1]
    gb[0, 1] = inp["ml_b_fgate"][0][0]
    gb[32, 1] = inp["ml_b_fgate"][0][1]
    m["gbias"] = gb
    rows = np.stack([inp["ln_g"][0][0], inp["ln_b"][0][0], inp["ln_g"][0][1], inp["ln_b"][0][1],
                     inp["b_proj"][0], inp["ml_gn_g"][0]], 0)
    m["rows6"] = f(rows)
    m["b_fc_fm"] = _fm(inp["b_fc"][0], 32)
    m["b_merge_fm"] = _fm(inp["b_merge"][0], 16)
    sh = np.zeros((128, 2, 8), np.float32)
    for d in range(2):
        sh[:, d] = _fm(inp["state_rglru_h"][b, 0, d], 8)
    m["st_h"] = sh
    C = np.asarray(inp["state_mlstm_C"][b, 0], np.float32)
    n = np.asarray(inp["state_mlstm_n"][b, 0], np.float32)
    m["st_Cn"] = f(np.concatenate([C, n[..., None]], -1))
    sm = np.zeros((33, NH), np.float32)
    sm[0] = inp["state_mlstm_m"][b, 0, 0]
    sm[32] = inp["state_mlstm_m"][b, 0, 1]
    m["st_m"] = sm
    sm2 = np.zeros((33, 2, NH), np.float32)
    sm2[0, 0] = inp["state_mlstm_m"][b, 0, 0]
    sm2[0, 1] = inp["state_mlstm_m"][b, 0, 1]
    m["st_m2"] = sm2
    c, rm = make_consts()
    m["cst"] = c
    return m


IN_SHAPES = {
    "xp": [TB, D], "xoth": [3 * TB, D], "xseg": [TB, D], "cvec": [128, 8, 2], "w_ada": [D, 6 * D],
    "b_ada_fm": [128, 48, 2], "b_ada": [1, 6 * D], "w_in": [D, D_IN], "conv_w_fm": [128, 8, 4],
    "conv_b_fm": [128, 8], "rg_wa": [2, 16, 64, 64], "rg_wx": [2, 16, 64, 64], "rgb_fm": [128, 3, 2, 8],
    "w_rg_proj": [D, D], "w_ml_proj": [D, D], "w_out": [D, D], "w_fc": [D, 4 * D], "w_proj": [4 * D, D],
    "gbias": [33, 2, NH], "rows6": [6, D], "b_fc_fm": [128, 32], "b_merge_fm": [128, 16],
    "st_h": [128, 2, 8], "st_Cn": [2, NH, HD, HD + 1], "st_m": [33, NH], "st_m2": [33, 2, NH], "pwg": [128, 3, 8, 2, NH], "pgb": [33, 3, 2, NH],
    "prgb": [128, 3, 3, 8], "pconv": [128, 3, 8, 5], "pdf": [128, 3, 2], "prg_wa": [3, 16, 64, 64], "prg_wx": [3, 16, 64, 64],
    "cst": [128, 514], "w_gates": [128, 8, 16],
}
OUT_SHAPES = {
    "yp": [TB, D], "ys": [TB, D], "o_h": [128, 4, 2, 8], "o_C": [4, 2, NH, HD, HD],
    "o_n": [128, 4, 2, NH, 2], "o_m": [33, NH, 4],
}
NFS = 14
NBS = 22


class StopBuild(Exception):
    pass


class Prog:
    def __init__(self, dbg=(), stop=None):
        self.dbg_names = list(dbg)
        self.stop = stop
        nc = self.nc = bass.Bass("TRN2", target_bir_lowering=False)
        es = self.es = ExitStack()
        self.trk = Trk(nc, es)
        self.I = {k: nc.dram_tensor(k, v, F32, kind="ExternalInput").ap() for k, v in IN_SHAPES.items()}
        self.O = {k: nc.dram_tensor(k, v, F32, kind="ExternalOutput").ap() for k, v in OUT_SHAPES.items()}
        self.dbg_out = {}
        self.rot = {}
        T = self.tile
        self.FS = T("FS", [128, NFS, 1024], F32)
        self.BS = T("BS", [128, NBS, 2048], BF16)
        self.ps = [es.enter_context(nc.psum_tensor("ps%d" % i, [128, 512], F32)) for i in range(8)]

    def tile(self, name, shape, dt):
        return self.es.enter_context(self.nc.sbuf_tensor("t_" + name, shape, dt))

    def mark(self, name):
        if self.stop == name:
            raise StopBuild(name)

    def nxt(self, name, n):
        v = self.rot.get(name, 0)
        self.rot[name] = v + 1
        return v % n

    def op(self, eng, fn, r=(), w=(), x=()):
        return self.trk.op(eng, fn, r, w, x)

    def dma(self, q, out, in_, r=(), w=(), **kw):
        return self.trk.dma(q, out, in_, r, w, **kw)

    def mm(self, out, lhsT, rhs, start, stop, r, w):
        return self.op("pe", lambda e: e.matmul(out, lhsT, rhs, start=start, stop=stop), r, w)

    def act(self, out, in_, func, r, w, bias=None, scale=None, x=()):
        kw = {}
        if bias is not None:
            kw["bias"] = bias
        if scale is not None:
            kw["scale"] = scale
        return self.op("act", lambda e: e.activation(out=out, in_=in_, func=func, **kw), r, w, x)

    def tt(self, eng, out, in0, in1, op, r, w, x=()):
        return self.op(eng, lambda e: e.tensor_tensor(out=out, in0=in0, in1=in1, op=op), r, w, x)

    def ts(self, eng, out, in0, s1, s2, op0, op1, r, w, x=()):
        if op1 is None:
            return self.op(eng, lambda e: e.tensor_scalar(out=out, in0=in0, scalar1=s1, scalar2=None, op0=op0), r, w, x)
        return self.op(eng, lambda e: e.tensor_scalar(out=out, in0=in0, scalar1=s1, scalar2=s2, op0=op0, op1=op1), r, w, x)

    def stt(self, out, in0, scalar, in1, op0, op1, r, w, x=()):
        return self.op("dve", lambda e: e.scalar_tensor_tensor(out=out, in0=in0, scalar=scalar, in1=in1, op0=op0, op1=op1), r, w, x)

    def cp(self, eng, out, in_, r, w, x=()):
        if eng == "act":
            return self.op("act", lambda e: e.copy(out=out, in_=in_), r, w, x)
        return self.op(eng, lambda e: e.tensor_copy(out=out, in_=in_), r, w, x)

    def dump(self, name, ap, shape, r):
        if name not in self.dbg_names:
            return
        d = self.nc.dram_tensor("dbg_" + name, list(shape), ap.dtype, kind="ExternalOutput").ap()
        self.dbg_out[name] = (list(shape), ap.dtype)
        self.dma("sp", d, ap, r=r, w=[("dbgout", name)])

    def f(self, s):
        return self.FS[:, s, :]

    def b(self, s):
        return self.BS[:, s, :]

    def bview(self, s, n, k, c):
        return self.BS[:, s:s + n, :].rearrange("p a (k2 c) -> p (a k2) c", c=c)

    def fview(self, s, n, k, c):
        return self.FS[:, s:s + n, :].rearrange("p a (k2 c) -> p (a k2) c", c=c)

    def setup(self):
        I, T = self.I, self.tile
        self.cst = T("cst", [128, 514], F32)
        self.dma("sp", self.cst[:], I["cst"][:, :], w=["cst"])
        self.ident = self.cst[:, 0:128]
        self.ones = self.cst[:, 384:512]
        self.maskb = T("maskb", [128, 2, 128], BF16)
        for d in range(2):
            self.cp("dve", self.maskb[:, d, :], self.cst[:, 128 * (d + 1):128 * (d + 2)], r=["cst"], w=[("maskb", d)])
        self.epsT = T("epsT", [128, 4], F32)
        for j, v in enumerate((LN_EPS, 1.0, 0.25, 0.5)):
            self.op("pool", lambda e, j=j, v=v: e.memset(self.epsT[:, j:j + 1], v), w=["epsT"])
        small = {}
        for name, key, shape in [("convw", "conv_w_fm", [128, 8, 4]), ("convb", "conv_b_fm", [128, 8]),
                                 ("rgb", "rgb_fm", [128, 3, 2, 8]), ("bfc", "b_fc_fm", [128, 32]),
                                 ("bmerge", "b_merge_fm", [128, 16]), ("pdf", "pdf", [128, 3, 2]),
                                 ("pwg", "pwg", [128, 3, 8, 2, NH]), ("prgb", "prgb", [128, 3, 3, 8]), ("pconv", "pconv", [128, 3, 8, 5]),
                                 ("badafm", "b_ada_fm", [128, 48, 2]), ("csl", "cvec", [128, 8, 2])]:
            t = T(name, shape, F32)
            src = I[key]
            self.dma("sp", t[:], src[tuple(slice(None) for _ in shape)], w=[name])
            small[name] = t
        self.__dict__.update(small)
        self.mark("s1")
        self.gbias = T("gbias", [33, 2, NH], F32)
        self.dma("sp", self.gbias[:], I["gbias"][:, :, :], w=["gbias"])
        self.RGH = T("RGH", [128, 2, 8], F32)
        self.dma("sp", self.RGH[:], I["st_h"][:, :, :], w=["RGH"])
        self.CST = T("CST", [128, 2, NH, 2, HD + 1], F32)
        for d in range(2):
            for h in range(NH):
                self.dma("sp", self.CST[:, d, h, :, :], I["st_Cn"][d, h].rearrange("(t p) e -> p t e", p=128),
                         w=[("CST", d, h)])
        self.MST = T("MST", [33, NH], F32)
        self.dma("sp", self.MST[:], I["st_m"][:, :], w=["MST"])
        self.MS2 = T("MS2", [33, 2, NH], F32)
        self.dma("sp", self.MS2[:], I["st_m2"][:, :, :], w=["MS2"])
        self.pgb = T("pgb", [33, 3, 2, NH], F32)
        self.dma("sp", self.pgb[:], I["pgb"][:, :, :, :], w=["pgb"])
        self.gbh = T("gbh", [33, NH], F32)
        self.ts("dve", self.gbh[:], self.gbias[:, 1, :], 0.5, None, ALU.mult, None, r=["gbias"], w=["gbias"])
        self.pgbh = T("pgbh", [33, 3, NH], F32)
        self.ts("dve", self.pgbh[:], self.pgb[:, :, 1, :], 0.5, None, ALU.mult, None, r=["pgb"], w=["pgb"])
        self.WRGp = T("WRGp", [128, 2, 8, 128], BF16)
        self.op("pool", lambda e: e.memset(self.WRGp[:], 0.0), w=["WRGp"])
        self.mark("s2")
        self.clam = T("clam", [128, 2, 8], F32)
        self.act(self.clam[:], self.rgb[:, 2, :, :], AF.Sigmoid, r=["rgb"], w=["clam"])
        self.act(self.clam[:], self.clam[:], AF.Ln, r=["clam"], w=["clam"])
        self.ts("dve", self.clam[:], self.clam[:], 8.0, None, ALU.mult, None, r=["clam"], w=["clam"])
        self.clam2 = T("clam2", [128, 2, 8], F32)
        self.ts("dve", self.clam2[:], self.clam[:], 2.0, None, ALU.mult, None, r=["clam"], w=["clam"])
        self.clamh = T("clamh", [128, 2, 8], F32)
        self.ts("dve", self.clamh[:], self.clam[:], 0.5, None, ALU.mult, None, r=["clam"], w=["clam"])
        self.rgbh = T("rgbh", [128, 2, 2, 8], F32)
        self.ts("dve", self.rgbh[:], self.rgb[:, 0:2, :, :], 0.5, None, ALU.mult, None, r=["rgb"], w=["rgb"])
        self.prgbh = T("prgbh", [128, 3, 2, 8], F32)
        self.ts("dve", self.prgbh[:], self.prgb[:, :, 0:2, :], 0.5, None, ALU.mult, None, r=["prgb"], w=["prgb"])
        self.pclamh = T("pclamh", [128, 3, 8], F32)
        self.pclam2 = T("pclam2", [128, 3, 8], F32)
        self.pclam = T("pclam", [128, 3, 8], F32)
        self.act(self.pclam[:], self.prgb[:, :, 2, :], AF.Sigmoid, r=["prgb"], w=["pclam"])
        self.act(self.pclam[:], self.pclam[:], AF.Ln, r=["pclam"], w=["pclam"])
        self.ts("dve", self.pclam[:], self.pclam[:], 8.0, None, ALU.mult, None, r=["pclam"], w=["pclam"])
        self.ts("dve", self.pclam2[:], self.pclam[:], 2.0, None, ALU.mult, None, r=["pclam"], w=["pclam"])
        self.ts("dve", self.pclamh[:], self.pclam[:], 0.5, None, ALU.mult, None, r=["pclam"], w=["pclam"])
        self.mark("s3")
        self.WRG = T("WRG", [128, 2, 2, 8, 128], BF16)
        self.op("pool", lambda e: e.memset(self.WRG[:], 0.0), w=["WRG"])
        for ax, key in enumerate(("rg_wa", "rg_wx")):
            for d in range(2):
                for e2 in range(2):
                    src = I[key][d, e2::2, :, :].rearrange("c k j -> k c j")
                    self.dma("pool", self.WRG[e2 * 64:(e2 + 1) * 64, d, ax, :, e2 * 64:(e2 + 1) * 64], src, r=[], w=["WRG"])
        self.mark("s4")
        self.Wg = T("Wg", [128, 8, 2, 64], BF16)
        self.op("pool", lambda e: e.memset(self.Wg[:], 0.0), w=["Wg"])
        self.Wgf = T("Wgf", [128, 8, 16], F32)
        self.dma("sp", self.Wgf[:], I["w_gates"][:, :, :], w=["Wgf"])
        self.vtok = T("vtok", [128, NCH, HD + 1], F32)
        self.op("pool", lambda e: e.memset(self.vtok[:, :, HD:HD + 1], 1.0), w=["vtok"])
        self.mark("s5")
        self.act(self.csl[:], self.csl[:], AF.Silu, r=["csl"], w=["csl"])
        self.modfm = T("modfm", [128, 48, 2], F32)
        self.mrow = T("mrow", [2, 2, 512], F32)
        self.G = T("G", [128, 2, D], F32)
        def issue_wad(cg):
            s0 = 4 * (cg % 3)
            wad = self.fview(s0, 4, 8, 512)
            src = I["w_ada"][:, cg * 512:(cg + 1) * 512].rearrange("(k p) c -> p k c", p=128)
            self.dma("sp", wad[:, 0:4, :], src[:, 0:4, :], w=[("F", s0 + j) for j in (0, 1)])
            self.dma("act", wad[:, 4:8, :], src[:, 4:8, :], w=[("F", s0 + j) for j in (2, 3)])

        issue_wad(0)
        issue_wad(1)
        for cg in range(12):
            if cg + 2 < 12:
                issue_wad(cg + 2)
            s0 = 4 * (cg % 3)
            wad = self.fview(s0, 4, 8, 512)
            rk = [("F", s0 + j) for j in range(4)]
            pt = self.ps[cg % 2]
            for k in range(8):
                self.mm(pt[0:2, :], self.csl[:, k, :], wad[:, k, :], k == 0, k == 7, r=rk + ["csl"], w=[("ps", cg % 2)])
            rr = self.nxt("mrow", 2)
            self.cp("act", self.mrow[0:2, rr, :], pt[0:2, :], r=[], w=[("mrow", rr)], x=[("ps", cg % 2)])
            p2 = self.ps[2 + cg % 2]
            for ft4 in range(4):
                self.op("pe", lambda e, ft4=ft4, p2=p2, rr=rr: e.transpose(out=p2[:, ft4 * 2:ft4 * 2 + 2],
                                                                       in_=self.mrow[0:2, rr, ft4 * 128:(ft4 + 1) * 128],
                                                                       identity=self.ident[0:2, 0:2]),
                        r=[("mrow", rr), "cst"], w=[("ps", 2 + cg % 2)])
            self.tt("dve", self.modfm[:, cg * 4:(cg + 1) * 4, :], p2[:, 0:8].rearrange("p (a b) -> p a b", b=2),
                    self.badafm[:, cg * 4:(cg + 1) * 4, :], ALU.add, r=["badafm"], w=[("modfm", cg)], x=[("ps", 2 + cg % 2)])
        self.mark("s6")
        for base in (8, 32):
            self.ts("dve", self.modfm[:, base:base + 8, :], self.modfm[:, base:base + 8, :], 1.0, None, ALU.add, None,
                    r=["modfm"], w=["modfm"])
        self.trk.fence(*[("F", s) for s in range(NFS)])
        self.dump("modfm", self.modfm[:], [128, 48, 2], r=["modfm"])
        self.dump("clam", self.clam[:], [128, 2, 8], r=["clam"])
        self.st6 = T("st6", [128, 8, 2, 6], F32)
        self.mv = T("mv", [128, 8, 2], F32)
        self.sd = T("sd", [128, 8, 3], F32)

    def gates(self, cond):
        self.trk.fence(*[("F", s) for s in range(NFS)])
        dg = self.fview(0, 2, 16, 128)
        for gate, base in ((0, 16), (1, 40)):
            for ft in range(8):
                i = gate * 8 + ft
                self.ts("dve", dg[:, i, :], self.ident, self.modfm[:, base + ft, cond:cond + 1], None, ALU.mult, None,
                        r=["cst", "modfm"], w=[("F", i // 8, i)])
                pb = self.nxt("psU", 8)
                self.mm(self.ps[pb][:, 0:128], self.ones, dg[:, i, :], True, True, r=["cst", ("F", i // 8, i)], w=[("ps", pb)])
                self.cp("act", self.G[:, gate, ft * 128:(ft + 1) * 128], self.ps[pb][:, 0:128], r=[], w=[("G", gate, ft)], x=[("ps", pb)])
        self.trk.fence(*[("F", s) for s in range(NFS)])
        self.dump("G%d" % cond, self.G[:], [128, 2, D], r=["G"])

    def ln_stats(self, x_ap, n, rk, want_nmr=True):
        i = self.nxt("lnst", 8)
        nchunk = (n + 511) // 512
        for j in range(nchunk):
            self.op("dve", lambda e, j=j: e.bn_stats(out=self.st6[:, i, j, :], in_=x_ap[:, j * 512:min(n, (j + 1) * 512)]),
                    r=rk, w=[("st6", i, j)])
        self.op("dve", lambda e: e.bn_aggr(out=self.mv[:, i, :], in_=self.st6[:, i, 0:nchunk, :]), r=[("st6", i)], w=[("mv", i)])
        self.act(self.sd[:, i, 0:1], self.mv[:, i, 1:2], AF.Sqrt, r=[("mv", i), "epsT"], w=[("sd", i, 0)], bias=self.epsT[:, 0:1])
        self.op("dve", lambda e: e.reciprocal(out=self.sd[:, i, 1:2], in_=self.sd[:, i, 0:1]), r=[("sd", i, 0)], w=[("sd", i, 1)])
        if not want_nmr:
            return self.sd[:, i, 1:2], self.mv[:, i, 0:1], [("sd", i), ("mv", i)]
        self.stt(self.sd[:, i, 2:3], self.mv[:, i, 0:1], -1.0, self.sd[:, i, 1:2], ALU.mult, ALU.mult,
                 r=[("mv", i), ("sd", i, 1)], w=[("sd", i, 2)])
        return self.sd[:, i, 1:2], self.sd[:, i, 2:3], [("sd", i)]

    def ln_affine_group(self, items, g_ap, g_key, b_ap, b_key):
        ids = [self.nxt("lnst", 8) for _ in items]
        for (x, kx), i in zip(items, ids):
            for j in range(2):
                self.op("dve", lambda e, j=j, i=i, x=x: e.bn_stats(out=self.st6[:, i, j, :], in_=x[:, j * 512:(j + 1) * 512]),
                        r=[kx], w=[("st6", i, j)])
            self.op("dve", lambda e, i=i: e.bn_aggr(out=self.mv[:, i, :], in_=self.st6[:, i, 0:2, :]), r=[("st6", i)], w=[("mv", i)])
        for (x, kx), i in zip(items, ids):
            self.act(self.sd[:, i, 0:1], self.mv[:, i, 1:2], AF.Sqrt, r=[("mv", i), "epsT"], w=[("sd", i, 0)], bias=self.epsT[:, 0:1])
        for (x, kx), i in zip(items, ids):
            self.op("dve", lambda e, i=i: e.reciprocal(out=self.sd[:, i, 1:2], in_=self.sd[:, i, 0:1]), r=[("sd", i, 0)], w=[("sd", i, 1)])
            self.stt(self.sd[:, i, 2:3], self.mv[:, i, 0:1], -1.0, self.sd[:, i, 1:2], ALU.mult, ALU.mult,
                     r=[("mv", i), ("sd", i, 1)], w=[("sd", i, 2)])
        for (x, kx), i in zip(items, ids):
            self.act(x, x, AF.Identity, r=[kx, ("sd", i)], w=[kx], bias=self.sd[:, i, 2:3], scale=self.sd[:, i, 1:2])
        for (x, kx), i in zip(items, ids):
            self.tt("dve", x, x, g_ap, ALU.mult, r=[kx, g_key], w=[kx])
        for (x, kx), i in zip(items, ids):
            self.tt("pool", x, x, b_ap, ALU.add, r=[kx, b_key], w=[kx])

    def build_uT(self, xsrc, row0, cond, s0, mod_base, x_keep=None, chunks=None):
        uT = self.bview(s0, 4, 8, TB)
        clist = list(range(NCH) if chunks is None else chunks)
        G = 4
        for g0 in range(0, len(clist), G):
            grp = clist[g0:g0 + G]
            st = {}
            for c in grp:
                if xsrc is not None:
                    xs = self.nxt("xin", 4)
                    xin, xk = self.f(xs), [("F", xs)]
                    self.dma("sp", xin, xsrc[row0 + c * 128:row0 + (c + 1) * 128, :], w=xk)
                    ns = 4 + self.nxt("xn", 4)
                else:
                    xin, xk = x_keep(c)
                    ns = 2 + self.nxt("xn", 4)
                st[c] = dict(xin=xin, xk=xk, i=self.nxt("lnst", 8), ns=ns)
            for c in grp:
                d_ = st[c]
                i = d_["i"]
                for j in range(2):
                    self.op("dve", lambda e, j=j, i=i, xin=d_["xin"]: e.bn_stats(out=self.st6[:, i, j, :], in_=xin[:, j * 512:(j + 1) * 512]),
                            r=d_["xk"], w=[("st6", i, j)])
                self.op("dve", lambda e, i=i: e.bn_aggr(out=self.mv[:, i, :], in_=self.st6[:, i, 0:2, :]), r=[("st6", i)], w=[("mv", i)])
            for c in grp:
                i = st[c]["i"]
                self.act(self.sd[:, i, 0:1], self.mv[:, i, 1:2], AF.Sqrt, r=[("mv", i), "epsT"], w=[("sd", i, 0)], bias=self.epsT[:, 0:1])
            for c in grp:
                i = st[c]["i"]
                self.op("dve", lambda e, i=i: e.reciprocal(out=self.sd[:, i, 1:2], in_=self.sd[:, i, 0:1]), r=[("sd", i, 0)], w=[("sd", i, 1)])
                self.stt(self.sd[:, i, 2:3], self.mv[:, i, 0:1], -1.0, self.sd[:, i, 1:2], ALU.mult, ALU.mult,
                         r=[("mv", i), ("sd", i, 1)], w=[("sd", i, 2)])
            for c in grp:
                d_ = st[c]
                i, ns = d_["i"], d_["ns"]
                self.act(self.f(ns), d_["xin"], AF.Identity, r=d_["xk"] + [("sd", i)], w=[("F", ns)], bias=self.sd[:, i, 2:3], scale=self.sd[:, i, 1:2])
            for c in grp:
                ns = st[c]["ns"]
                xn = self.f(ns)
                st[c]["pb"] = []
                for half in range(2):
                    pb = self.nxt("psT", 8)
                    st[c]["pb"].append(pb)
                    pt = self.ps[pb]
                    for j in range(4):
                        ft = half * 4 + j
                        self.op("pe", lambda e, ft=ft, j=j, pt=pt, xn=xn: e.transpose(out=pt[:, j * 128:(j + 1) * 128],
                                                                                     in_=xn[:, ft * 128:(ft + 1) * 128], identity=self.ident),
                                r=[("F", ns), "cst"], w=[("ps", pb)])
            for c in grp:
                for half in range(2):
                    pb = st[c]["pb"][half]
                    pt = self.ps[pb]
                    for j in range(4):
                        ft = half * 4 + j
                        sc = self.modfm[:, mod_base + 8 + ft, cond:cond + 1]
                        sh = self.modfm[:, mod_base + ft, cond:cond + 1]
                        dst = uT[:, ft, c * 128:(c + 1) * 128]
                        wk = [("B", s0 + ft // 2, ft, c)]
                        if half == 0:
                            self.act(dst, pt[:, j * 128:(j + 1) * 128], AF.Identity, r=["modfm"], w=wk, x=[("ps", pb)], bias=sh, scale=sc)
                        else:
                            self.ts("dve", dst, pt[:, j * 128:(j + 1) * 128], sc, sh, ALU.mult, ALU.add, r=["modfm"], w=wk, x=[("ps", pb)])
        return uT

    XST, XC0, RA, GN, IB, HFS, HBS, GI, GF = 0, 1, 3, 4, 5, 6, 7, 8, 9
    BC, TH, WNB, HFM, HSN, CW0 = 0, 1, 2, 4, 6, 10
    UT, HRG, HMT, W1, WKV, XCB, GZ, QT, KT, KTOK, OG = 0, 4, 8, 12, 14, 16, 17, 18, 19, 20, 21

    def load_w(self, dst, col0, wkeys, src="w_in", ncol=256):
        self.dma("pool", dst, self.I[src][:, col0:col0 + ncol].rearrange("(k p) c -> p k c", p=128), w=wkeys)

    def wplan(self, items):
        self.wq, self.wi_issue, self.wi_use = list(items), 0, 0

    def wload_next(self):
        if self.wi_issue >= len(self.wq):
            return
        ds = 12 + 2 * (self.wi_issue % 2)
        W = self.bview(ds, 2, 8, 512)
        for (dcol, src, col0, ncol) in self.wq[self.wi_issue]:
            self.load_w(W[:, :, dcol:dcol + ncol], col0, [("B", ds, dcol), ("B", ds + 1, dcol)], src=src, ncol=ncol)
        self.wi_issue += 1

    def wget(self):
        i = self.wi_use
        self.wi_use += 1
        while self.wi_issue <= i:
            self.wload_next()
        ds = 12 + 2 * (i % 2)
        return ds, self.bview(ds, 2, 8, 512)

    def unit_witems(self, h, outs):
        it = [[(0, "w_in", OFF_XR + 256 * h, 256)]]
        if outs:
            it += [[(0, "w_in", OFF_ZR + 256 * h, 256)], [(0, "w_in", OFF_Q + 256 * h, 256)], [(0, "w_in", OFF_K + 256 * h, 256)]]
        it += [[(0, "w_in", OFF_K + 256 * h, 256), (256, "w_in", OFF_V + 256 * h, 256)]]
        if outs:
            it += [[(0, "w_in", OFF_O + 256 * h, 256)]]
        return it

    def default_params(self):
        return dict(keys=["WRG", "rgb", "clam", "convw", "gbias", "Wgf"], ntap=4,
                    convw=lambda ct, k: self.convw[:, ct, k:k + 1],
                    WRG=lambda d, ax, ct: self.WRG[:, d, ax, ct, :],
                    rgbias=lambda ax, d, ct: self.rgbh[:, ax, d, ct:ct + 1],
                    clamh=lambda d, ct: self.clamh[:, d, ct:ct + 1],
                    clam=lambda d, ct: self.clam[:, d, ct:ct + 1],
                    clam2=lambda d, ct: self.clam2[:, d, ct:ct + 1],
                    gbias=lambda rows, g, h: (self.gbias[rows, 0, h:h + 1] if g == 0 else self.gbh[rows, h:h + 1]),
                    Wgf=lambda g, d, h: self.Wgf[:, :, (g * 2 + d) * NH + h:(g * 2 + d) * NH + h + 1])

    def pass_params(self, p):
        return dict(keys=["WRGp", "prgb", "pclam", "pconv", "pgb", "pwg"], ntap=5,
                    convw=lambda ct, k: self.pconv[:, p, ct, k:k + 1],
                    WRG=lambda d, ax, ct: self.WRGp[:, ax, ct, :],
                    rgbias=lambda ax, d, ct: self.prgbh[:, p, ax, ct:ct + 1],
                    clamh=lambda d, ct: self.pclamh[:, p, ct:ct + 1],
                    clam=lambda d, ct: self.pclam[:, p, ct:ct + 1],
                    clam2=lambda d, ct: self.pclam2[:, p, ct:ct + 1],
                    gbias=lambda rows, g, h: (self.pgb[rows, p, 0, h:h + 1] if g == 0 else self.pgbh[rows, p, h:h + 1]),
                    Wgf=lambda g, d, h: self.pwg[:, p, :, g, h:h + 1])

    def unit(self, h, cfg):
        seqs, dirs, outs = cfg["seqs"], cfg["dirs"], cfg["outputs"]
        L = cfg["conv_seg"]
        PR = cfg.get("params") or self.default_params()
        PK = PR["keys"]
        uT = self.bview(self.UT, 4, 8, TB)
        ukey = lambda kk: ("B", self.UT + kk // 2, kk)
        psA = lambda: self.nxt("psU", 8)
        psB = psA

        def proj_fm(col0, evac):
            s, W = self.wget()
            for t2 in range(2):
                for tb2 in range(2):
                    pb = psA()
                    for kk in range(8):
                        self.mm(self.ps[pb][:, :], W[:, kk, t2 * 128:(t2 + 1) * 128], uT[:, kk, tb2 * 512:(tb2 + 1) * 512],
                                kk == 0, kk == 7, r=[("B", s), ("B", s + 1), ukey(kk)], w=[("ps", pb)])
                    evac(pb, t2, tb2)
            self.wload_next()

        xcb = self.bview(self.XCB, 1, 2, TB)
        gz = self.bview(self.GZ, 1, 2, TB)
        hrgT = self.bview(self.HRG, 4, 8, TB)

        def evac_xr(pb, t2, tb2):
            self.cp("act", self.f(self.XST)[:, tb2 * 512:(tb2 + 1) * 512], self.ps[pb][:, :], r=[], w=[("F", self.XST, tb2)], x=[("ps", pb)])
            if tb2 == 1:
                ct = 2 * h + t2
                xs = self.f(self.XST).rearrange("p (s l) -> p s l", l=L)
                xc = self.f(self.XC0 + t2)
                xcv = xc.rearrange("p (s l) -> p s l", l=L)
                cw = lambda k: PR["convw"](ct, k)
                kx, kc = [("F", self.XST)], [("F", self.XC0 + t2)]
                self.ts("dve", xc, self.f(self.XST), cw(2), self.convb[:, ct:ct + 1], ALU.mult, ALU.add, r=kx + PK + ["convb"], w=kc)
                self.stt(xcv[:, :, 2:], xs[:, :, :L - 2], cw(0), xcv[:, :, 2:], ALU.mult, ALU.add, r=kx + kc + PK, w=kc)
                self.stt(xcv[:, :, 1:], xs[:, :, :L - 1], cw(1), xcv[:, :, 1:], ALU.mult, ALU.add, r=kx + kc + PK, w=kc)
                self.stt(xcv[:, :, :L - 1], xs[:, :, 1:], cw(3), xcv[:, :, :L - 1], ALU.mult, ALU.add, r=kx + kc + PK, w=kc)
                if PR["ntap"] == 5:
                    self.stt(xcv[:, :, :L - 2], xs[:, :, 2:], cw(4), xcv[:, :, :L - 2], ALU.mult, ALU.add, r=kx + kc + PK, w=kc)
                self.cp("dve", xcb[:, t2, :], xc, r=kc, w=[("B", self.XCB, t2)])
                self.dump("xc%d_%d" % (h, t2), xc, [128, TB], r=kc)

        proj_fm(OFF_XR + 256 * h, evac_xr)
        self.mark("xr%d" % h)
        if outs:
            def evac_zr(pb, t2, tb2):
                self.act(gz[:, t2, tb2 * 512:(tb2 + 1) * 512], self.ps[pb][:, :], AF.Gelu_apprx_tanh, r=[], w=[("B", self.GZ, t2, tb2)], x=[("ps", pb)])
            proj_fm(OFF_ZR + 256 * h, evac_zr)
        self.mark("zr%d" % h)

        def rg_group(items):
            for t2, d, (RA, GN, IB), hs_ in items:
                ct = 2 * h + t2
                for ax, ds in ((0, RA), (1, IB)):
                    for tb2 in range(2):
                        pb = psB()
                        self.mm(self.ps[pb][:, :], PR["WRG"](d, ax, ct), xcb[:, t2, tb2 * 512:(tb2 + 1) * 512], True, True,
                                r=PK + [("B", self.XCB, t2)], w=[("ps", pb)])
                        self.act(self.f(ds)[:, tb2 * 512:(tb2 + 1) * 512], self.ps[pb][:, :], AF.Tanh, r=PK, w=[("F", ds, tb2)],
                                 x=[("ps", pb)], bias=PR["rgbias"](ax, d, ct), scale=0.5)
            for t2, d, (RA, GN, IB), hs_ in items:
                ct = 2 * h + t2
                xc, kc = self.f(self.XC0 + t2), [("F", self.XC0 + t2)]
                self.act(self.f(RA), self.f(RA), AF.Exp, r=[("F", RA)] + PK, w=[("F", RA)], scale=PR["clamh"](d, ct), bias=PR["clamh"](d, ct))
                self.tt("dve", self.f(GN), self.f(RA), self.f(RA), ALU.mult, r=[("F", RA)], w=[("F", GN)])
                self.stt(self.f(IB), self.f(IB), 1.0, xc, ALU.add, ALU.mult, r=[("F", IB)] + kc, w=[("F", IB)])
            for t2, d, (RA, GN, IB), hs_ in items:
                self.act(self.f(GN), self.f(GN), AF.Sqrt, r=[("F", GN), "epsT"], w=[("F", GN)], bias=self.epsT[:, 2:3], scale=-0.25)
                self.tt("dve", self.f(IB), self.f(IB), self.f(GN), ALU.mult, r=[("F", IB), ("F", GN)], w=[("F", IB)])
            for t2, d, (RA, GN, IB), hs_ in items:
                ct = 2 * h + t2
                ra, ib = self.f(RA), self.f(IB)
                hdst = self.f(hs_)
                for si, (t0, ln) in enumerate(seqs):
                    init = cfg["rg_init"](d, ct)
                    ini, ik = (0.0, []) if init is None else init
                    if d == 0:
                        o_, a_, b_ = hdst[:, t0:t0 + ln], ra[:, t0:t0 + ln], ib[:, t0:t0 + ln]
                    else:
                        o_, a_, b_ = hdst[:, t0:t0 + ln][:, ::-1], ra[:, t0:t0 + ln][:, ::-1], ib[:, t0:t0 + ln][:, ::-1]
                    self.op("dve", lambda e, o_=o_, a_=a_, b_=b_, ini=ini: e.tensor_tensor_scan(out=o_, data0=a_, data1=b_, initial=ini,
                                                                                          op0=ALU.mult, op1=ALU.add),
                            r=[("F", RA), ("F", IB)] + ik, w=[("F", hs_, si)])
                    fin = hdst[:, t0 + ln - 1:t0 + ln] if d == 0 else hdst[:, t0:t0 + 1]
                    cfg["rg_final"](d, ct, si, fin, [("F", hs_, si)])

        TS0, TS1 = (self.RA, self.GN, self.IB), (10, 11, 12)

        def rg_out(t2):
            ct = 2 * h + t2
            hf, hb = self.f(self.HFS), self.f(self.HBS)
            self.tt("pool", hf, hf, hb, ALU.add, r=[("F", self.HFS), ("F", self.HBS)], w=[("F", self.HFS)])
            self.tt("pool", hrgT[:, ct, :], hf, gz[:, t2, :], ALU.mult, r=[("F", self.HFS), ("B", self.GZ, t2)],
                    w=[("B", self.HRG + ct // 2, ct)])

        rg_steps = []
        if outs:
            for t2 in range(2):
                rg_steps.append(lambda t2=t2: rg_group([(t2, 0, TS0, self.HFS), (t2, 1, TS1, self.HBS)]))
                rg_steps.append(lambda t2=t2: rg_out(t2))
        else:
            d0 = dirs[0]
            rg_steps.append(lambda: rg_group([(0, d0, TS0, self.HFS), (1, d0, TS1, self.HBS)]))

        def rg_step():
            if rg_steps:
                rg_steps.pop(0)()


        qT = self.bview(self.QT, 1, 2, TB)
        kT = self.bview(self.KT, 1, 2, TB)
        ktok = self.bview(self.KTOK, 1, NCH, HD)
        og = self.bview(self.OG, 1, NCH, HD)
        if outs:
            def evac_q(pb, t2, tb2):
                self.ts("dve", qT[:, t2, tb2 * 512:(tb2 + 1) * 512], self.ps[pb][:, :], HD ** -0.5, None, ALU.mult, None, r=[],
                        w=[("B", self.QT, t2, tb2)], x=[("ps", pb)])
            proj_fm(OFF_Q + 256 * h, evac_q)
            rg_step()

            def evac_k(pb, t2, tb2):
                self.cp("dve", kT[:, t2, tb2 * 512:(tb2 + 1) * 512], self.ps[pb][:, :], r=[], w=[("B", self.KT, t2, tb2)], x=[("ps", pb)])
            proj_fm(OFF_K + 256 * h, evac_k)
            rg_step()
        skv, Wkv = self.wget()
        if outs:
            so, Wo = self.wget()
        for c in range(NCH):
            pb = psB()
            for kk in range(8):
                self.mm(self.ps[pb][:, :], uT[:, kk, c * 128:(c + 1) * 128], Wkv[:, kk, :], kk == 0, kk == 7,
                        r=[("B", skv), ("B", skv + 1), ukey(kk) + (c,)], w=[("ps", pb)])
            self.cp("act", ktok[:, c, :], self.ps[pb][:, 0:256], r=[], w=[("B", self.KTOK, c)], x=[("ps", pb)])
            self.cp("act", self.vtok[:, c, 0:HD], self.ps[pb][:, 256:512], r=[], w=[("vtok", c)], x=[("ps", pb)])
            if outs:
                pb = psB()
                for kk in range(8):
                    self.mm(self.ps[pb][:, 0:256], uT[:, kk, c * 128:(c + 1) * 128], Wo[:, kk, 0:256], kk == 0, kk == 7,
                            r=[("B", so), ("B", so + 1), ukey(kk) + (c,)], w=[("ps", pb)])
                self.act(og[:, c, :], self.ps[pb][:, 0:256], AF.Tanh, r=[], w=[("B", self.OG, c)], x=[("ps", pb)], scale=0.5)
                self.stt(og[:, c, :], og[:, c, :], 1.0, self.gngh[:, h * HD:(h + 1) * HD], ALU.add, ALU.mult, r=[("B", self.OG, c), "gngh"],
                         w=[("B", self.OG, c)])
            if c % 2 == 1:
                rg_step()
        while rg_steps:
            rg_step()
        self.dump("hrg%d" % h, hrgT[:, 2 * h:2 * h + 2, :], [128, 2, TB], r=[("B", self.HRG + h)])
        self.mark("rg%d" % h)
        self.wload_next()
        if outs:
            self.wload_next()
        for g, off in ((0, OFF_IG), (1, OFF_FG)):
            for d in range(2):
                col = off + d * NH + h
                self.cp("pool", self.Wg[:, :, g, 32 * d:32 * d + 1], PR["Wgf"](g, d, h), r=PK, w=[("Wg", g, d)])
        gt = [self.f(self.GI), self.f(self.GF)]
        gk = [self.GI, self.GF]
        for g in range(2):
            for tb2 in range(2):
                pb = psA()
                for kk in range(8):
                    self.mm(self.ps[pb][0:33, :], self.Wg[:, kk, g, 0:33], uT[:, kk, tb2 * 512:(tb2 + 1) * 512], kk == 0, kk == 7,
                            r=[("Wg", g), ukey(kk)], w=[("ps", pb)])
                fn = AF.Identity if g == 0 else AF.Tanh
                sc = 1.0 if g == 0 else 0.5
                self.act(gt[g][0:32, tb2 * 512:(tb2 + 1) * 512], self.ps[pb][0:32, :], fn, r=PK, w=[("F", gk[g], "n", tb2)],
                         x=[("ps", pb)], bias=PR["gbias"](slice(0, 32), g, h), scale=sc)
                lo = TB - 512 * (tb2 + 1)
                self.act(gt[g][32:33, lo:lo + 512][:, ::-1], self.ps[pb][32:33, :], fn, r=PK, w=[("F", gk[g], "f", tb2)],
                         x=[("ps", pb)], bias=PR["gbias"](slice(32, 33), g, h), scale=sc)
        self.mark("proj%d" % h)
        if outs:
            self.mlstm(h, cfg)
        else:
            self.mlstm_pass(h, cfg)

    def mlstm_tiles(self):
        if hasattr(self, "amax"):
            return
        T = self.tile
        self.amax = T("amax", [33, NCH], F32)
        self.mn = T("mn", [33, NCH], F32)
        self.mp = T("mp", [33, NCH], F32)
        self.ngp = T("ngp", [33, NCH], F32)
        self.dec = T("dec", [33, NCH], F32)
        self.decb = T("decb", [33, NCH], F32)
        self.cols = T("cols", [128, NCH, 4], F32)
        self.decrep = T("decrep", [128, 2, NCH], F32)
        self.den = T("den", [128, 4, 2], F32)
        for t in (self.amax, self.mn, self.mp, self.ngp, self.dec, self.decb):
            self.op("pool", lambda e, t=t: e.memset(t[:], 0.0), w=[t.name])

    def mlstm_pass(self, h, cfg):
        self.mlstm_tiles()
        F = self.f
        GI, GF, BC = self.GI, self.GF, self.BC
        self.trk.fence(*([("F", s) for s in range(8)] + [("B", self.XCB), ("B", self.GZ)]))
        gi, gf, bc = F(GI)[0:33], F(GF)[0:33], F(BC)[0:33]
        kn = lambda t: t.name
        m0, mk = cfg["m0"](h)
        self.act(gf, gf, AF.Ln, r=[("F", GF), "epsT"], w=[("F", GF)], scale=0.5, bias=self.epsT[0:33, 3:4])
        for c in range(NCH):
            sl = slice(c * 128, (c + 1) * 128)
            ini = 0.0 if c == 0 else bc[:, c * 128 - 1:c * 128]
            self.op("dve", lambda e, sl=sl, ini=ini: e.tensor_tensor_scan(out=bc[:, sl], data0=self.ones[0:33, 0:128], data1=gf[:, sl],
                                                                     initial=ini, op0=ALU.mult, op1=ALU.add),
                    r=[("F", GF), "cst"] + ([("F", BC, c - 1)] if c else []), w=[("F", BC, c)])
        self.tt("dve", gi, gi, bc, ALU.subtract, r=[("F", GI), ("F", BC)], w=[("F", GI)])
        am, gg, dl = self.amax[:, 0:1], self.mn[:, 0:1], self.dec[:, 0:1]
        self.op("dve", lambda e: e.tensor_reduce(out=am, in_=gi, axis=AX.X, op=ALU.max), r=[("F", GI)], w=[kn(self.amax)])
        self.tt("dve", gg, am, m0, ALU.max, r=[kn(self.amax)] + mk, w=[kn(self.mn)])
        self.tt("dve", dl, m0, gg, ALU.subtract, r=[kn(self.mn)] + mk, w=[kn(self.dec)])
        self.act(dl, dl, AF.Exp, r=[kn(self.dec)], w=[kn(self.dec)])
        self.ts("dve", self.ngp[:, 0:1], gg, -1.0, None, ALU.mult, None, r=[kn(self.mn)], w=[kn(self.ngp)])
        self.act(gi, gi, AF.Exp, r=[("F", GI), kn(self.ngp)], w=[("F", GI)], bias=self.ngp[:, 0:1])
        self.tt("dve", self.mp[:, 0:1], gg, bc[:, TB - 1:TB], ALU.add, r=[kn(self.mn), ("F", BC)], w=[kn(self.mp)])
        cfg["m_final"](h, self.mp, [kn(self.mp)], 0)
        pb = self.nxt("psA", 2)
        pw = self.ps[pb]
        one0 = self.ones[0:1, 0:1]
        for c in range(NCH):
            self.mm(pw[:, c:c + 1], gi[0:1, c * 128:(c + 1) * 128], one0, True, True, r=[("F", GI), "cst"], w=[("ps", pb)])
        self.mm(pw[:, 8:9], self.ones[0:1, 0:128], dl[0:1, :], True, True, r=[kn(self.dec), "cst"], w=[("ps", pb)])
        self.cp("dve", self.cols[:, :, 0], pw[:, 0:8], r=[], w=["cols"], x=[("ps", pb)])
        self.cp("dve", self.decrep[:, 0, 0:1], pw[:, 8:9], r=[], w=["decrep"], x=[("ps", pb)])
        ktok = self.bview(self.KTOK, 1, NCH, HD)
        rawx = self.b(self.XCB)
        for c in range(NCH):
            rv = self.nxt("vt", 4)
            vt = rawx[:, 512 + rv * 258:512 + rv * 258 + HD + 1]
            kvt = ("B", self.XCB, "vt", rv)
            self.act(vt, self.vtok[:, c, :], AF.Identity, r=[("vtok", c), "cols"], w=[kvt], scale=self.cols[:, c, 0:1])
            for dt in range(2):
                self.mm(self.ps[6 + dt][:, 0:HD + 1], ktok[:, c, dt * 128:(dt + 1) * 128], vt, c == 0, c == NCH - 1,
                        r=[("B", self.KTOK, c), kvt], w=[("ps", 6 + dt)])
        df, ndf = cfg["route"]
        rd = self.nxt("den", 4)
        dec = self.decrep[:, 0, 0:1]
        for dd, (msk, cmsk) in enumerate(((df, ndf), (ndf, df))):
            s1 = self.den[:, rd, dd:dd + 1]
            self.stt(s1, dec, msk, cmsk, ALU.mult, ALU.add, r=["decrep", "pdf"], w=[("den", rd, dd)])
            for dt in range(2):
                Cs = self.CST[:, dd, h, dt, :]
                self.ts("dve", Cs, Cs, s1, None, ALU.mult, None, r=[("CST", dd, h), ("den", rd, dd)], w=[("CST", dd, h)])
                self.stt(Cs, self.ps[6 + dt][:, 0:HD + 1], msk, Cs, ALU.mult, ALU.add, r=[("CST", dd, h), "pdf"], w=[("CST", dd, h)],
                         x=[("ps", 6 + dt)])
        self.trk.fence(*([("F", s) for s in range(10)] + [("B", self.XCB), ("B", self.GZ)]))
        self.mark("ml%d" % h)

    def mlstm(self, h, cfg):
        seqs, dirs, outs = cfg["seqs"], cfg["dirs"], cfg["outputs"]
        self.mlstm_tiles()
        F = self.f
        GI, GF, BC, TH, WNB = self.GI, self.GF, self.BC, self.TH, self.WNB
        self.trk.fence(*([("F", s) for s in range(8)] + [("B", self.XCB), ("B", self.GZ)]))
        gi, gf, bc, th = F(GI)[0:33], F(GF)[0:33], F(BC)[0:33], F(TH)[0:33]
        wnb = self.fview(WNB, 2, 2, TB)
        kn = lambda t: t.name
        self.act(gf, gf, AF.Ln, r=[("F", GF), "epsT"], w=[("F", GF)], scale=0.5, bias=self.epsT[0:33, 3:4])
        for c in range(NCH):
            sl = slice(c * 128, (c + 1) * 128)
            self.op("dve", lambda e, sl=sl: e.tensor_tensor_scan(out=bc[:, sl], data0=self.ones[0:33, 0:128], data1=gf[:, sl], initial=0.0,
                                                           op0=ALU.mult, op1=ALU.add), r=[("F", GF), "cst"], w=[("F", BC, c)])
        self.tt("dve", gi, gi, bc, ALU.subtract, r=[("F", GI), ("F", BC)], w=[("F", GI)])
        self.op("dve", lambda e: e.tensor_reduce(out=self.amax[:, :], in_=gi.rearrange("p (c l) -> p c l", l=128), axis=AX.X, op=ALU.max),
                r=[("F", GI)], w=[kn(self.amax)])
        btot = bc[:, 127:TB:128]
        m0 = cfg["m0"](h)
        for (t0, ln) in seqs:
            c0, c1 = t0 // 128, (t0 + ln) // 128
            ini, ik = (0.0, []) if m0 is None else m0
            self.op("dve", lambda e, c0=c0, c1=c1, ini=ini: e.tensor_tensor_scan(out=self.mn[:, c0:c1], data0=self.amax[:, c0:c1],
                                                                               data1=btot[:, c0:c1], initial=ini, op0=ALU.max, op1=ALU.add),
                    r=[kn(self.amax), ("F", BC)] + ik, w=[kn(self.mn)])
            if m0 is None:
                self.op("pool", lambda e, c0=c0: e.memset(self.mp[:, c0:c0 + 1], 0.0), w=[kn(self.mp)])
            else:
                self.cp("dve", self.mp[:, c0:c0 + 1], ini, r=ik, w=[kn(self.mp)])
            if c1 - c0 > 1:
                self.cp("dve", self.mp[:, c0 + 1:c1], self.mn[:, c0:c1 - 1], r=[kn(self.mn)], w=[kn(self.mp)])
        self.tt("dve", self.ngp[:, :], btot, self.mn[:, :], ALU.subtract, r=[("F", BC), kn(self.mn)], w=[kn(self.ngp)])
        self.tt("dve", self.dec[:, :], self.mp[:, :], self.ngp[:, :], ALU.add, r=[kn(self.mp), kn(self.ngp)], w=[kn(self.dec)])
        self.act(self.dec[:, :], self.dec[:, :], AF.Exp, r=[kn(self.dec)], w=[kn(self.dec)])
        for c in range(NCH):
            sl = slice(c * 128, (c + 1) * 128)
            self.act(th[:, sl], bc[:, sl], AF.Exp, r=[("F", BC, c), kn(self.ngp)], w=[("F", TH, c)], bias=self.ngp[:, c:c + 1], scale=-1.0)
            self.act(gi[:, sl], gi[:, sl], AF.Exp, r=[("F", GI), kn(self.ngp)], w=[("F", GI, c)], bias=self.ngp[:, c:c + 1])
        self.cp("dve", wnb[32:33, 0, :], gi[32:33, :][:, ::-1], r=[("F", GI)], w=[("F", WNB)])
        self.cp("dve", wnb[32:33, 1, :], th[32:33, :][:, ::-1], r=[("F", TH)], w=[("F", WNB + 1)])
        self.cp("dve", self.decb[32:33, :], self.dec[32:33, :][:, ::-1], r=[kn(self.dec)], w=[kn(self.decb)])
        cfg["m_final"](h, self.mn, [kn(self.mn)])
        pb = self.nxt("psA", 2)
        pw = self.ps[pb]
        one0, one32 = self.ones[0:1, 0:1], self.ones[32:33, 0:1]
        for c in range(NCH):
            sl = slice(c * 128, (c + 1) * 128)
            for j, (lhs, one, rk) in enumerate(((gi[0:1, sl], one0, ("F", GI)), (wnb[32:33, 0, sl], one32, ("F", WNB)),
                                                (th[0:1, sl], one0, ("F", TH)), (wnb[32:33, 1, sl], one32, ("F", WNB + 1)))):
                self.mm(pw[:, c * 4 + j:c * 4 + j + 1], lhs, one, True, True, r=[rk, "cst"], w=[("ps", pb)])
        self.mm(pw[:, 32:40], self.ones[0:1, 0:128], self.dec[0:1, :], True, True, r=[kn(self.dec), "cst"], w=[("ps", pb)])
        self.mm(pw[:, 40:48], self.ones[32:33, 0:128], self.decb[32:33, :], True, True, r=[kn(self.decb), "cst"], w=[("ps", pb)])
        self.cp("dve", self.cols[:, :, :], pw[:, 0:32].rearrange("p (c j) -> p c j", j=4), r=[], w=["cols"], x=[("ps", pb)])
        self.cp("dve", self.decrep[:, :, :], pw[:, 32:48].rearrange("p (d c) -> p d c", c=NCH), r=[], w=["decrep"], x=[("ps", pb)])
        self.dump("cols%d" % h, self.cols[:], [128, NCH, 4], r=["cols"])
        self.dump("decrep%d" % h, self.decrep[:], [128, 2, NCH], r=["decrep"])
        self.dump("mn%d" % h, self.mn[:], [33, NCH], r=[kn(self.mn)])
        self.mark("gates%d" % h)

        qT = self.bview(self.QT, 1, 2, TB)
        kT = self.bview(self.KT, 1, 2, TB)
        ktok = self.bview(self.KTOK, 1, NCH, HD)
        og = self.bview(self.OG, 1, NCH, HD)
        hmT = self.bview(self.HMT, 4, 8, TB)
        HFM = self.fview(self.HFM, 2, NCH, HD)
        rawx, rawg = self.b(self.XCB), self.b(self.GZ)
        hsn, hsn2 = F(self.HSN), F(self.HSN + 1)
        gng = cfg["gng"]
        PS_K0, PS_K1 = 6, 7
        arrived = {}
        pending = []

        def make_stages(si, d, c, cwset, first=False):
            sl = slice(c * 128, (c + 1) * 128)
            PS_N, PS_K0 = (2, 3) if d == 0 else (5, 6)
            PS_S = PS_N
            CWs = self.CW0 + 2 * cwset + d
            CW = self.f(CWs)[:, 0:514].rearrange("p (t e) -> p t e", e=HD + 1)
            wcol, thr, dcol = self.cols[:, c, d:d + 1], self.cols[:, c, 2 + d:3 + d], self.decrep[:, d, c:c + 1]
            rv = self.nxt("vt", 4)
            vt = rawx[:, 512 + rv * 258:512 + rv * 258 + HD + 1]
            kvt = ("B", self.XCB, "vt", rv)
            rc = self.nxt("cdec", 3)
            Cdec = rawg[:, rc * 516:(rc + 1) * 516].rearrange("p (t e) -> p t e", e=258)[:, :, 0:HD + 1]
            kcd = ("B", self.GZ, "cd", rc)
            rs = self.nxt("sm", 4)
            Sm = rawx[:, rs * 128:(rs + 1) * 128]
            ksm = ("B", self.XCB, "sm", rs)
            pn = self.ps[PS_N]
            rd = self.nxt("den", 4)
            dn = self.den[:, rd, :]

            def A1():
                self.act(vt, self.vtok[:, c, :], AF.Identity, r=[("vtok", c), "cols"], w=[kvt], scale=wcol)
                if not first:
                    self.act(Cdec, CW, AF.Identity, r=[("F", CWs), "decrep"], w=[kcd], scale=dcol)

            def P1():
                for dt in range(2):
                    self.mm(self.ps[PS_S][:, 384:512], kT[:, dt, sl], qT[:, dt, sl], dt == 0, dt == 1,
                            r=[("B", self.KT, dt), ("B", self.QT, dt)], w=[("ps", PS_S)])

            def D1():
                self.tt("dve", Sm, self.ps[PS_S][:, 384:512], self.maskb[:, d, :], ALU.mult, r=[("maskb", d)], w=[ksm], x=[("ps", PS_S)])

            def P2():
                for dt in range(2):
                    pk = PS_K0 + dt
                    self.mm(self.ps[pk][:, 0:HD + 1], ktok[:, c, dt * 128:(dt + 1) * 128], vt, True, True,
                            r=[("B", self.KTOK, c), kvt], w=[("ps", pk)])
                self.mm(pn[:, 0:HD + 1], Sm, vt, True, first, r=[ksm, kvt], w=[("ps", PS_N)])
                if not first:
                    for dt in range(2):
                        self.mm(pn[:, 0:HD + 1], qT[:, dt, sl], Cdec[:, dt, :], False, dt == 1, r=[("B", self.QT, dt), kcd], w=[("ps", PS_N)])

            def D2():
                for dt in range(2):
                    pk = PS_K0 + dt
                    self.stt(CW[:, dt, :], CW[:, dt, :], dcol, self.ps[pk][:, 0:HD + 1], ALU.mult, ALU.add,
                             r=[("F", CWs), "decrep"], w=[("F", CWs)], x=[("ps", pk)])

            def A2():
                self.act(dn[:, 0:1], pn[:, HD:HD + 1], AF.Abs, r=[], w=[("den", rd)], x=[("ps", PS_N)])

            def D3():
                self.ts("dve", dn[:, 0:1], dn[:, 0:1], thr, None, ALU.max, None, r=["cols", ("den", rd)], w=[("den", rd)])
                self.op("dve", lambda e: e.reciprocal(out=dn[:, 1:2], in_=dn[:, 0:1]), r=[("den", rd)], w=[("den", rd)])

            def A3():
                if c not in arrived:
                    arrived[c] = True
                    self.act(HFM[:, c, :], pn[:, 0:HD], AF.Identity, r=[("den", rd)], w=[("F", self.HFM + c // 4, c)], x=[("ps", PS_N)],
                             scale=dn[:, 1:2])
                    return
                r2 = self.nxt("hs", 4)
                hs = hsn[:, r2 * HD:(r2 + 1) * HD]
                hn = hsn2[:, (r2 % 2) * HD:(r2 % 2 + 1) * HD]
                hm = hsn2[:, 512 + (r2 % 2) * HD:512 + (r2 % 2 + 1) * HD]
                khs, khn, khm = ("F", self.HSN, "hs", r2), ("F", self.HSN + 1, "hn", r2 % 2), ("F", self.HSN + 1, "hm", r2 % 2)
                self.stt(hs, pn[:, 0:HD], dn[:, 1:2], HFM[:, c, :], ALU.mult, ALU.add, r=[("den", rd), ("F", self.HFM + c // 4, c)], w=[khs],
                         x=[("ps", PS_N)])

                def fin():
                    rstd, mean, sk = self.ln_stats(hs, HD, [khs], want_nmr=False)
                    self.ts("dve", hn, hs, mean, rstd, ALU.subtract, ALU.mult, r=[khs] + sk, w=[khn])
                    self.tt("dve", hm, hn, og[:, c, :], ALU.mult, r=[khn, ("B", self.OG, c)], w=[khm])
                    pbT = self.nxt("psA", 2)
                    for et in range(2):
                        self.op("pe", lambda e, et=et: e.transpose(out=self.ps[pbT][:, et * 128:(et + 1) * 128],
                                                                    in_=hm[:, et * 128:(et + 1) * 128], identity=self.ident),
                                r=[khm, "cst"], w=[("ps", pbT)])
                    self.cp("act", hmT[:, 2 * h:2 * h + 2, sl], self.ps[pbT][:, 0:256].rearrange("p (e t) -> p e t", t=128), r=[],
                            w=[("B", self.HMT + h, 2 * h, c), ("B", self.HMT + h, 2 * h + 1, c)], x=[("ps", pbT)])
                pending.append(fin)

            return [A1, P1, D1, P2, D2, A2, D3, A3]

        for si, (t0, ln) in enumerate(seqs):
            c0, c1 = t0 // 128, (t0 + ln) // 128
            cwset = self.nxt("cwset", 2)
            for d in dirs:
                cfg["cw_init"](h, d, self.CW0 + 2 * cwset + d)
            for s in range(c1 - c0):
                prev = pending[:]
                del pending[:]
                stg = [make_stages(si, d, c0 + s if d == 0 else c1 - 1 - s, cwset, first=(s == 0 and cfg.get("zero_state", False)))
                       for d in dirs]
                for k in range(8):
                    for st in stg:
                        st[k]()
                    if k == 4:
                        for fn in prev:
                            fn()
            for d in dirs:
                cfg["cw_final"](h, d, si, self.CW0 + 2 * cwset + d)
        for fn in pending:
            fn()
        self.dump("hm%d" % h, hmT[:, 2 * h:2 * h + 2, :], [128, 2, TB], r=[("B", self.HMT + h)])
        self.trk.fence(*([("F", s) for s in range(10)] + [("B", self.XCB), ("B", self.GZ)]))
        self.mark("ml%d" % h)

    def block_tiles(self):
        if hasattr(self, "gng"):
            return
        T = self.tile
        self.gng = T("gng", [128, D], F32)
        self.dma("sp", self.gng[:], self.I["rows6"][5:6, :].broadcast_to([128, D]), w=["gng"])
        self.gngh = self.gng
        self.ts("dve", self.gng[:], self.gng[:], 0.5, None, ALU.mult, None, r=["gng"], w=["gng", "gngh"])
        self.RGO = T("RGO", [128, 4, 2, 8], F32)
        self.NFIN = T("NFIN", [128, 4, 2, NH, 2], F32)
        self.MFIN = T("MFIN", [33, NH, 4], F32)
        self.op("pool", lambda e: e.memset(self.MFIN[:], 0.0), w=["MFIN"])
        self.RGW = T("RGW", [128, 2, 8], F32)
        self.RGI = T("RGI", [128, 8], F32)
        self.RGT = T("RGT", [128, 8], F32)
        self.MW = T("MW", [33, NH], F32)
        self.MI = T("MI", [33, NH], F32)
        self.MT = T("MT", [33, NH], F32)
        for t in (self.MW, self.MI, self.MT):
            self.op("pool", lambda e, t=t: e.memset(t[:], 0.0), w=[t.name[2:]])

    def cfg_prompt(self):
        self.block_tiles()

        def rg_final(d, ct, si, ap, keys):
            self.cp("pool", self.RGO[:, si, d, ct:ct + 1], ap, r=keys, w=[("RGO", si, d, ct)])

        def cw_init(h, d, slot):
            self.op("pool", lambda e: e.memset(self.f(slot)[:, 0:514], 0.0), w=[("F", slot)])

        def cw_final(h, d, si, slot):
            CW = self.f(slot)[:, 0:514].rearrange("p (t e) -> p t e", e=HD + 1)
            self.dma("sp", self.O["o_C"][si, d, h].rearrange("(t p) e -> p t e", p=128), CW[:, :, 0:HD], r=[("F", slot)], w=[("o_C", si, d, h)])
            self.cp("pool", self.NFIN[:, si, d, h, :], CW[:, :, HD:HD + 1].rearrange("p t o -> p (t o)"), r=[("F", slot)], w=[("NFIN", si, d, h)])

        def m_final(h, mn, keys):
            for si in range(4):
                self.cp("pool", self.MFIN[0:1, h, si:si + 1], mn[0:1, 2 * si + 1:2 * si + 2], r=keys, w=[("MFIN", h, si, 0)])
                self.cp("pool", self.MFIN[32:33, h, si:si + 1], mn[32:33, 2 * (3 - si) + 1:2 * (3 - si) + 2], r=keys, w=[("MFIN", h, si, 1)])

        return dict(seqs=[(256 * s, 256) for s in range(4)], conv_seg=256, dirs=(0, 1), outputs=True, cond=0, zero_state=True,
                    rg_init=lambda d, ct: None, rg_final=rg_final, m0=lambda h: None, m_final=m_final,
                    cw_init=cw_init, cw_final=cw_final, gng=self.gng, gng_key="gng")

    def cfg_sample(self):
        self.block_tiles()

        def cw_init(h, d, slot):
            CW = self.f(slot)[:, 0:514].rearrange("p (t e) -> p t e", e=HD + 1)
            self.cp("pool", CW, self.CST[:, d, h, :, :], r=[("CST", d, h)], w=[("F", slot)])

        return dict(seqs=[(0, TB)], conv_seg=64, dirs=(0, 1), outputs=True, cond=1,
                    rg_init=lambda d, ct: (self.RGH[:, d, ct:ct + 1], [("RGH", d)]), rg_final=lambda *a: None,
                    m0=lambda h: (self.MST[:, h:h + 1], [("MST", h)]), m_final=lambda *a: None,
                    cw_init=cw_init, cw_final=lambda *a: None, gng=self.gng, gng_key="gng")

    def sweep_pass(self, p):
        I = self.I
        self.block_tiles()
        df, ndf = self.pdf[:, p, 0:1], self.pdf[:, p, 1:2]
        for ax, key in enumerate(("prg_wa", "prg_wx")):
            for e2 in range(2):
                src = I[key][p, e2::2, :, :].rearrange("c k j -> k c j")
                self.dma("pool", self.WRGp[e2 * 64:(e2 + 1) * 64, ax, :, e2 * 64:(e2 + 1) * 64], src, w=["WRGp"])
        self.tt("pool", self.RGI[:, :], self.RGH[:, 0, :], self.RGH[:, 1, :], ALU.subtract, r=["RGH"], w=["RGI"])
        self.stt(self.RGI[:, :], self.RGI[:, :], df, self.RGH[:, 1, :], ALU.mult, ALU.add, r=["RGI", "RGH", "pdf"], w=["RGI"])
        self.tt("pool", self.MI[0:1, :], self.MS2[0:1, 0, :], self.MS2[0:1, 1, :], ALU.subtract, r=["MS2"], w=["MI"])
        self.stt(self.MI[0:1, :], self.MI[0:1, :], self.pdf[0:1, p, 0:1], self.MS2[0:1, 1, :], ALU.mult, ALU.add, r=["MI", "MS2", "pdf"], w=["MI"])

        def rg_final(d, ct, si, ap, keys):
            self.cp("pool", self.RGW[:, 0, ct:ct + 1], ap, r=keys, w=[("RGW", 0, ct)])

        def cw_init(h, d, slot):
            CW = self.f(slot)[:, 0:514].rearrange("p (t e) -> p t e", e=HD + 1)
            self.tt("pool", CW, self.CST[:, 0, h, :, :], self.CST[:, 1, h, :, :], ALU.subtract, r=[("CST", 0, h), ("CST", 1, h)], w=[("F", slot)])
            self.stt(CW, CW, df, self.CST[:, 1, h, :, :], ALU.mult, ALU.add, r=[("F", slot), ("CST", 1, h), "pdf"], w=[("F", slot)])

        def cw_final(h, d, si, slot):
            CW = self.f(slot)[:, 0:514].rearrange("p (t e) -> p t e", e=HD + 1)
            Tt = self.f(slot + 1)[:, 0:514].rearrange("p (t e) -> p t e", e=HD + 1)
            for dd, msk in ((0, df), (1, ndf)):
                self.tt("pool", Tt, CW, self.CST[:, dd, h, :, :], ALU.subtract, r=[("F", slot), ("CST", dd, h)], w=[("F", slot + 1)])
                self.stt(self.CST[:, dd, h, :, :], Tt, msk, self.CST[:, dd, h, :, :], ALU.mult, ALU.add,
                         r=[("F", slot + 1), ("CST", dd, h), "pdf"], w=[("CST", dd, h)])

        def m_final(h, mn, keys, col=NCH - 1):
            self.cp("pool", self.MW[0:1, h:h + 1], mn[0:1, col:col + 1], r=keys, w=[("MW", h)])

        cfg = dict(seqs=[(0, TB)], conv_seg=64, dirs=(0,), outputs=False, cond=1, params=self.pass_params(p), route=(df, ndf),
                   rg_init=lambda d, ct: (self.RGI[:, ct:ct + 1], ["RGI"]), rg_final=rg_final,
                   m0=lambda h: (self.MI[:, h:h + 1], ["MI"]), m_final=m_final,
                   cw_init=cw_init, cw_final=cw_final, gng=self.gng, gng_key="gng")
        self.wplan([it for h in range(NH) for it in self.unit_witems(h, False)])
        self.wload_next()
        self.wload_next()
        self.build_uT(self.I["xoth"], p * TB, 1, self.UT, 0)
        for h in range(NH):
            self.unit(h, cfg)
        for dd, msk, mrow in ((0, df, self.pdf[0:1, p, 0:1]), (1, ndf, self.pdf[0:1, p, 1:2])):
            self.tt("pool", self.RGT[:, :], self.RGW[:, 0, :], self.RGH[:, dd, :], ALU.subtract, r=[("RGW", 0), ("RGH", dd)], w=["RGT"])
            self.stt(self.RGH[:, dd, :], self.RGT[:, :], msk, self.RGH[:, dd, :], ALU.mult, ALU.add, r=["RGT", ("RGH", dd), "pdf"],
                     w=[("RGH", dd)])
            self.tt("pool", self.MT[0:1, :], self.MW[0:1, :], self.MS2[0:1, dd, :], ALU.subtract, r=["MW", "MS2"], w=["MT"])
            self.stt(self.MS2[0:1, dd, :], self.MT[0:1, :], mrow, self.MS2[0:1, dd, :], ALU.mult, ALU.add, r=["MT", "MS2", "pdf"], w=["MS2"])

    def sweeps_done(self):
        self.cp("pool", self.MST[0:1, :], self.MS2[0:1, 0, :], r=["MS2"], w=["MST"])
        self.dma("sp", self.MST[32:33, :], self.MS2[0:1, 1, :], r=["MS2"], w=["MST"])

    def post_mixer(self, cfg, xsrc):
        I = self.I
        cond = cfg["cond"]
        self.trk.fence(*([("F", s) for s in range(NFS)] + [("B", s) for s in range(12, NBS)]))
        uT = self.bview(self.UT, 4, 8, TB)
        hrgT = self.bview(self.HRG, 4, 8, TB)
        hmT = self.bview(self.HMT, 4, 8, TB)
        self.dma("sp", self.f(0), I["rows6"][0:1, :].broadcast_to([128, D]), w=[("F", 0)])
        self.dma("sp", self.f(1), I["rows6"][1:2, :].broadcast_to([128, D]), w=[("F", 1)])
        mergedT = self.bview(12, 4, 8, TB)
        gsig, tmp = self.f(4), self.f(5)

        def issue_slices(ft):
            ws = 16 + 2 * (ft % 3)
            Wsl = self.bview(ws, 2, 32, 128)
            for kind, (src, col0) in enumerate((("w_rg_proj", ft * 128), ("w_ml_proj", ft * 128),
                                                ("w_in", OFF_GM + ft * 128), ("w_in", OFF_GM + D + ft * 128))):
                self.load_w(Wsl[:, kind * 8:(kind + 1) * 8, :], col0, [("B", ws + kind // 2, kind)], src=src, ncol=128)

        issue_slices(0)
        issue_slices(1)
        acts = (hrgT, hmT, uT, uT)
        akey = (self.HRG, self.HMT, self.UT, self.UT)
        WoutA, WoutB = self.bview(20, 2, 4, D), self.bview(16, 2, 4, D)
        for ft in range(8):
            if ft + 2 < 8:
                issue_slices(ft + 2)
            if ft == 6:
                self.dma("pool", WoutA, I["w_out"][0:512, :].rearrange("(k p) c -> p k c", p=128), w=[("B", 20), ("B", 21)])
            if ft == 7:
                self.dma("pool", WoutB, I["w_out"][512:1024, :].rearrange("(k p) c -> p k c", p=128), w=[("B", 16), ("B", 17)])
            ws = 16 + 2 * (ft % 3)
            Wsl = self.bview(ws, 2, 32, 128)
            for tb2 in range(2):
                tsl = slice(tb2 * 512, (tb2 + 1) * 512)
                pbase = 4 * ((ft * 2 + tb2) % 2)
                for kind in range(4):
                    for k in range(8):
                        self.mm(self.ps[pbase + kind][:, :], Wsl[:, kind * 8 + k, :], acts[kind][:, k, tsl], k == 0, k == 7,
                                r=[("B", ws + kind // 2, kind), ("B", akey[kind] + k // 2, k)], w=[("ps", pbase + kind)])
                for j in range(2):
                    self.act(gsig[:, j * 512:(j + 1) * 512], self.ps[pbase + 2 + j][:, :], AF.Sigmoid, r=["bmerge"], w=[("F", 4, j)], x=[("ps", pbase + 2 + j)],
                             bias=self.bmerge[:, 8 * j + ft:8 * j + ft + 1])
                for j in range(2):
                    self.tt("dve", tmp[:, j * 512:(j + 1) * 512], self.ps[pbase + j][:, :], gsig[:, j * 512:(j + 1) * 512], ALU.mult,
                            r=[("F", 4, j)], w=[("F", 5, j)], x=[("ps", pbase + j)])
                self.tt("pool", mergedT[:, ft, tsl], tmp[:, 0:512], tmp[:, 512:1024], ALU.add, r=[("F", 5)], w=[("B", 12 + ft // 2, ft, tb2)])
        for g0 in range(0, NCH, 2):
            grp = (g0, g0 + 1)
            st = {}
            for c in grp:
                xs = 2 + self.nxt("xin2", 2)
                self.dma("sp", self.f(xs), xsrc[c * 128:(c + 1) * 128, :], w=[("F", xs)])
                st[c] = dict(xs=xs, x1c=self.f(6 + c), kx1=("F", 6 + c), pbs=[4 + 2 * (c % 2), 5 + 2 * (c % 2)])
            for c in grp:
                for nh in range(2):
                    pb = st[c]["pbs"][nh]
                    nsl = slice(nh * 512, (nh + 1) * 512)
                    for k in range(8):
                        Wk = WoutA[:, k, nsl] if k < 4 else WoutB[:, k - 4, nsl]
                        kw = ("B", 20 + k // 2) if k < 4 else ("B", 16 + (k - 4) // 2)
                        self.mm(self.ps[pb][:, :], mergedT[:, k, c * 128:(c + 1) * 128], Wk, k == 0, k == 7,
                                r=[("B", 12 + k // 2, k), kw], w=[("ps", pb)])
            for c in grp:
                for nh in range(2):
                    pb = st[c]["pbs"][nh]
                    nsl = slice(nh * 512, (nh + 1) * 512)
                    self.tt("dve", st[c]["x1c"][:, nsl], self.ps[pb][:, :], self.G[:, 0, nsl], ALU.mult, r=["G"], w=[st[c]["kx1"] + (nh,)],
                            x=[("ps", pb)])
            for c in grp:
                self.stt(st[c]["x1c"], self.f(st[c]["xs"]), ALPHA, st[c]["x1c"], ALU.mult, ALU.add, r=[("F", st[c]["xs"]), st[c]["kx1"]],
                         w=[st[c]["kx1"]])
            self.ln_affine_group([(st[c]["x1c"], st[c]["kx1"]) for c in grp], self.f(0), ("F", 0), self.f(1), ("F", 1))
        self.dump("x1", self.fview(6, 8, 8, D), [128, 8, D], r=[("F", 6 + c) for c in range(8)])
        self.trk.fence(*[("B", s) for s in range(0, 4)])
        self.build_uT(None, 0, cond, self.UT, 24, x_keep=lambda c: (self.f(6 + c), [("F", 6 + c)]))

    def mlp(self, cfg, ydst):
        I = self.I
        self.trk.fence(*([("F", s) for s in range(6)] + [("B", s) for s in range(4, NBS)]))
        u2T = self.bview(self.UT, 4, 8, TB)
        for j, row in enumerate((2, 3, 4)):
            self.dma("sp", self.f(j), I["rows6"][row:row + 1, :].broadcast_to([128, D]), w=[("F", j)])
        hidT = self.bview(4, 8, 16, TB)

        def issue_fc(i):
            ws = 12 + 2 * (i % 2)
            self.load_w(self.bview(ws, 2, 8, 512), i * 512, [("B", ws), ("B", ws + 1)], src="w_fc", ncol=512)

        def issue_pj(i):
            wp = 16 + i % 6
            ffh, nh, kg4 = i // 8, (i % 8) // 4, i % 4
            r0 = ffh * 2048 + kg4 * 512
            self.dma("pool", self.bview(wp, 1, 4, 512),
                     I["w_proj"][r0:r0 + 512, nh * 512:(nh + 1) * 512].rearrange("(k p) c -> p k c", p=128), w=[("B", wp)])

        issue_fc(0)
        for ffh in range(2):
            for g in range(4):
                i = ffh * 4 + g
                if i + 1 < 8:
                    issue_fc(i + 1)
                issue_pj(ffh * 8 + g)
                ws = 12 + 2 * (i % 2)
                Wfc = self.bview(ws, 2, 8, 512)
                for f4 in range(4):
                    fl = g * 4 + f4
                    fft = ffh * 16 + fl
                    for tb2 in range(2):
                        tsl = slice(tb2 * 512, (tb2 + 1) * 512)
                        pb = self.nxt("psH", 4)
                        for k in range(8):
                            self.mm(self.ps[pb][:, :], Wfc[:, k, f4 * 128:(f4 + 1) * 128], u2T[:, k, tsl], k == 0, k == 7,
                                    r=[("B", ws), ("B", ws + 1), ("B", self.UT + k // 2, k)], w=[("ps", pb)])
                        rt = self.nxt("relu", 4)
                        rl = self.f(3 + rt // 2)[:, (rt % 2) * 512:(rt % 2 + 1) * 512]
                        krl = ("F", 3 + rt // 2, rt % 2)
                        kh = ("B", 4 + fl // 2, fl, tb2)
                        if (fl + tb2) % 2 == 0:
                            self.act(rl, self.ps[pb][:, :], AF.Relu, r=["bfc"], w=[krl], x=[("ps", pb)], bias=self.bfc[:, fft:fft + 1])
                            self.tt("dve", hidT[:, fl, tsl], rl, rl, ALU.mult, r=[krl], w=[kh])
                        else:
                            self.ts("dve", rl, self.ps[pb][:, :], self.bfc[:, fft:fft + 1], 0.0, ALU.add, ALU.max, r=["bfc"], w=[krl],
                                    x=[("ps", pb)])
                            self.act(hidT[:, fl, tsl], rl, AF.Square, r=[krl], w=[kh])
            for nh in range(2):
                nsl = slice(nh * 512, (nh + 1) * 512)
                for kg4 in range(4):
                    i = ffh * 8 + nh * 4 + kg4
                    if i + 4 < (ffh + 1) * 8:
                        issue_pj(i + 4)
                    wp = 16 + i % 6
                    Wp = self.bview(wp, 1, 4, 512)
                    for c in range(NCH):
                        for k4 in range(4):
                            fl = kg4 * 4 + k4
                            self.mm(self.ps[c][:, :], hidT[:, fl, c * 128:(c + 1) * 128], Wp[:, k4, :], fl == 0, fl == 15,
                                    r=[("B", 4 + fl // 2, fl), ("B", wp)], w=[("ps", c)])
                for c in range(NCH):
                    x1c, kx1 = self.f(6 + c), ("F", 6 + c)
                    rt = self.nxt("relu", 4)
                    t = self.f(3 + rt // 2)[:, (rt % 2) * 512:(rt % 2 + 1) * 512]
                    kt = ("F", 3 + rt // 2, rt % 2)
                    if ffh == 0:
                        self.tt("dve", t, self.ps[c][:, :], self.f(2)[:, nsl], ALU.add, r=[("F", 2)], w=[kt], x=[("ps", c)])
                        self.tt("dve", t, t, self.G[:, 1, nsl], ALU.mult, r=[kt, "G"], w=[kt])
                        self.stt(x1c[:, nsl], x1c[:, nsl], ALPHA, t, ALU.mult, ALU.add, r=[kx1, kt], w=[kx1 + (nh,)])
                    else:
                        self.tt("dve", t, self.ps[c][:, :], self.G[:, 1, nsl], ALU.mult, r=["G"], w=[kt], x=[("ps", c)])
                        self.tt("dve", x1c[:, nsl], x1c[:, nsl], t, ALU.add, r=[kx1, kt], w=[kx1 + (nh,)])
        for c0 in range(0, NCH, 2):
            cs = [c0, c0 + 1]
            self.ln_affine_group([(self.f(6 + c), ("F", 6 + c)) for c in cs], self.f(0), ("F", 0), self.f(1), ("F", 1))
            for c in cs:
                self.dma("sp", ydst[c * 128:(c + 1) * 128, :], self.f(6 + c), r=[("F", 6 + c)], w=[("ydst", id(ydst), c)])
        self.trk.fence(*([("F", s) for s in range(NFS)] + [("B", s) for s in range(NBS)]))

    def full_block(self, cfg, xsrc, ydst):
        self.wplan([it for h in range(NH) for it in self.unit_witems(h, True)])
        self.wload_next()
        self.wload_next()
        self.build_uT(xsrc, 0, cfg["cond"], self.UT, 0)
        self.mark("uT")
        for h in range(NH):
            self.unit(h, cfg)
        self.mark("mixer")
        self.post_mixer(cfg, xsrc)
        self.mark("post")
        self.mlp(cfg, ydst)

    def build(self):
        try:
            self.setup()
            self.gates(0)
            self.mark("setup")
            cfgp = self.cfg_prompt()
            self.full_block(cfgp, self.I["xp"], self.O["yp"])
            self.dma("sp", self.O["o_h"][:, :, :, :], self.RGO[:], r=["RGO"], w=["o_h"])
            self.dma("sp", self.O["o_n"][:, :, :, :, :], self.NFIN[:], r=["NFIN"], w=["o_n"])
            self.dma("sp", self.O["o_m"][:, :, :], self.MFIN[:], r=["MFIN"], w=["o_m"])
            self.mark("prompt")
            self.gates(1)
            for p in range(3):
                self.sweep_pass(p)
            self.sweeps_done()
            self.mark("sweeps")
            cfgs = self.cfg_sample()
            self.full_block(cfgs, self.I["xseg"], self.O["ys"])
        except StopBuild as e:
            print("build stopped at", e)
        self.trk.finish()
        return self.nc


def assemble(results):
    yp = np.zeros((32, 256, D), np.float32)
    ys = np.zeros((2, 4 * TB, D), np.float32)
    nh_ = np.zeros((32, 1, 2, D), np.float32)
    nC = np.zeros((32, 1, 2, NH, HD, HD), np.float32)
    nn = np.zeros((32, 1, 2, NH, HD), np.float32)
    nm = np.zeros((32, 1, 2, NH), np.float32)
    for i, r in enumerate(results):
        b, seg = i // 4, i % 4
        yp[4 * i:4 * i + 4] = np.asarray(r["yp"]).reshape(4, 256, D)
        ys[b, seg * TB:(seg + 1) * TB] = np.asarray(r["ys"])
        nh_[4 * i:4 * i + 4, 0] = np.asarray(r["o_h"]).transpose(1, 2, 3, 0).reshape(4, 2, D)
        nC[4 * i:4 * i + 4, 0] = np.asarray(r["o_C"])
        nn[4 * i:4 * i + 4, 0] = np.asarray(r["o_n"]).transpose(1, 2, 3, 4, 0).reshape(4, 2, NH, HD)
        om = np.asarray(r["o_m"])
        nm[4 * i:4 * i + 4, 0, 0] = om[0].T
        nm[4 * i:4 * i + 4, 0, 1] = om[32].T
    return yp, ys, nh_, nC, nn, nm


def kernel(**inputs):
    inp = {k: np.asarray(v) for k, v in inputs.items()}
    prog = Prog()
    nc = prog.build()
    in_maps = [prep_core_inputs(i, inp) for i in range(NCORES)]
    res = run_bass_kernel_spmd(nc, in_maps, core_ids=list(range(NCORES)))
    return assemble(res.results)
```
